# Optimizing a Trainium2 kernel written in Bass

```python
import jax, jax.numpy as jnp
from jax import lax
import numpy as np

D_MODEL = 2048
BATCH = 2
SEQ = 4096
DEPTH = 2

GRID_W = 64
CTX_LEN = 256
MIX_WIDTH = D_MODEL
HEAD_DIM = 128
N_DIR = 2
GLA_WIDTH = MIX_WIDTH // 4
GLA_HEADS = GLA_WIDTH // HEAD_DIM
GLA_DV = HEAD_DIM
GLA_DK = HEAD_DIM // 2
GLA_RANK = 16
GLA_TAU = 16.0
HG_WIDTH = MIX_WIDTH // 4
HG_HEADS = HG_WIDTH // HEAD_DIM
HG_EXPAND = HEAD_DIM
HG_DV = HEAD_DIM
NA_WIDTH = MIX_WIDTH // 2
NA_HEADS = NA_WIDTH // HEAD_DIM
NA_DH = HEAD_DIM
WIN_ROWS = 8
WIN_COLS = 16
D_FF = 4 * D_MODEL
CHUNK = 64
ROPE_BASE = 10000.0
EPS = 1e-6
IN_SIZES = (GLA_HEADS * GLA_DK, GLA_HEADS * GLA_DK, GLA_WIDTH, GLA_WIDTH, N_DIR * GLA_RANK,
            HG_WIDTH, N_DIR * HG_WIDTH, HG_WIDTH, HG_WIDTH,
            NA_WIDTH, NA_WIDTH, NA_WIDTH)
IN_COLS = sum(IN_SIZES)

kernel_name = "hybrid_gla_hgrn2_natten_dit_block"


def rmsnorm(x, g):
    xf = x.astype(jnp.float32)
    y = xf * lax.rsqrt(jnp.mean(xf * xf, axis=-1, keepdims=True) + EPS)
    return (y * g.astype(jnp.float32)).astype(x.dtype)


def split_cols(p):
    return jnp.split(p, np.cumsum(IN_SIZES)[:-1].tolist(), axis=-1)


def to_heads(t, n_heads):
    b, t_len, _ = t.shape
    return t.reshape(b, t_len, n_heads, -1).transpose(0, 2, 1, 3)


def from_heads(t):
    b, h, t_len, d = t.shape
    return t.transpose(0, 2, 1, 3).reshape(b, t_len, h * d)


def axial_rope(x, row_pos, col_pos):
    half = x.shape[-1] // 2
    quarter = half // 2
    inv_freq = ROPE_BASE ** (-jnp.arange(quarter, dtype=jnp.float32) / quarter)
    xf = x.astype(jnp.float32)

    def rotate(xa, pos):
        ang = pos.astype(jnp.float32)[:, None] * inv_freq
        cos, sin = jnp.cos(ang), jnp.sin(ang)
        x1, x2 = xa[..., :quarter], xa[..., quarter:]
        return jnp.concatenate([x1 * cos - x2 * sin, x1 * sin + x2 * cos], axis=-1)

    return jnp.concatenate([rotate(xf[..., :half], row_pos), rotate(xf[..., half:], col_pos)], axis=-1).astype(x.dtype)


def chunk_gated_scan(q, k, v, log_a, s0):
    b, h, t_len, dk = q.shape
    dv = v.shape[-1]
    n = t_len // CHUNK

    def chunks(t):
        return jnp.moveaxis(t.astype(jnp.float32).reshape(b, h, n, CHUNK, t.shape[-1]), 2, 0)

    lower = jnp.tril(jnp.ones((CHUNK, CHUNK), dtype=bool))[:, :, None]

    def step(s, blk):
        qb, kb, vb, gb = blk
        cum = jnp.cumsum(gb, axis=2)
        rel = jnp.where(lower, cum[:, :, :, None, :] - cum[:, :, None, :, :], -jnp.inf)
        att = jnp.einsum('bhid,bhjd,bhijd->bhij', qb, kb, jnp.exp(rel))
        o = jnp.einsum('bhij,bhjv->bhiv', att, vb) + jnp.einsum('bhid,bhdv->bhiv', qb * jnp.exp(cum), s)
        cum_end = cum[:, :, -1:, :]
        s_new = jnp.exp(cum_end[:, :, 0, :, None]) * s + jnp.einsum('bhjd,bhjv->bhdv', kb * jnp.exp(cum_end - cum), vb)
        return s_new, o

    s_fin, o = lax.scan(step, s0, (chunks(q), chunks(k), chunks(v), chunks(log_a)))
    return jnp.moveaxis(o, 0, 2).reshape(b, h, t_len, dv), s_fin


def prefix_scan(ctx_in, lat_in, reverse):
    if reverse:
        ctx_in = tuple(t[:, :, ::-1] for t in ctx_in)
        lat_in = tuple(t[:, :, ::-1] for t in lat_in)
    b, h, _, dk = ctx_in[0].shape
    dv = ctx_in[2].shape[-1]
    s0 = jnp.zeros((b, h, dk, dv), jnp.float32)
    o_ctx, s_ctx = chunk_gated_scan(*ctx_in, s0)
    o_lat, _ = chunk_gated_scan(*lat_in, s_ctx)
    if reverse:
        o_ctx, o_lat = o_ctx[:, :, ::-1], o_lat[:, :, ::-1]
    return o_ctx, o_lat


def gated_readout(o, gate, norm_g):
    return (from_heads(rmsnorm(o, norm_g)) * jax.nn.silu(gate.astype(jnp.float32))).astype(gate.dtype)


def gla_mixer(p_ctx, p_lat, w_a2, b_a, norm_g, row_pos, col_pos, with_ctx_out):
    def prep(p, rotate):
        q, k, v, r, a_low = p
        q = to_heads(q, GLA_HEADS) * GLA_DK ** -0.5
        k = to_heads(k, GLA_HEADS)
        if rotate:
            q, k = axial_rope(q, row_pos, col_pos), axial_rope(k, row_pos, col_pos)
        v = to_heads(v, GLA_HEADS)
        log_a = tuple(
            to_heads(jax.nn.log_sigmoid((a_low[..., d * GLA_RANK:(d + 1) * GLA_RANK] @ w_a2[d] + b_a[d]).astype(jnp.float32)) / GLA_TAU, GLA_HEADS)
            for d in range(N_DIR))
        return q, k, v, r, log_a

    qc, kc, vc, rc, gc = prep(p_ctx, False)
    ql, kl, vl, rl, gl = prep(p_lat, True)
    oc_f, ol_f = prefix_scan((qc, kc, vc, gc[0]), (ql, kl, vl, gl[0]), reverse=False)
    oc_b, ol_b = prefix_scan((qc, kc, vc, gc[1]), (ql, kl, vl, gl[1]), reverse=True)
    y_lat = gated_readout(ol_f + ol_b, rl, norm_g)
    y_ctx = gated_readout(oc_f + oc_b, rc, norm_g) if with_ctx_out else None
    return y_lat, y_ctx


def hgrn2_mixer(p_ctx, p_lat, lower_bound, norm_g, with_ctx_out):
    def prep(p):
        q, f2, i, g = p
        q = jax.nn.silu(to_heads(q, HG_HEADS))
        v = to_heads(i, HG_HEADS)
        ks, log_f = [], []
        for d in range(N_DIR):
            logit = to_heads(f2[..., d * HG_WIDTH:(d + 1) * HG_WIDTH], HG_HEADS).astype(jnp.float32)
            lb = lower_bound[d].reshape(HG_HEADS, 1, HG_EXPAND)
            forget = lb + (1.0 - lb) * jax.nn.sigmoid(logit)
            log_f.append(jnp.log(forget))
            ks.append((1.0 - lb) * jax.nn.sigmoid(-logit))
        return q, v, g, ks, log_f

    qc, vc, gtc, kc, fc = prep(p_ctx)
    ql, vl, gtl, kl, fl = prep(p_lat)
    oc_f, ol_f = prefix_scan((qc, kc[0], vc, fc[0]), (ql, kl[0], vl, fl[0]), reverse=False)
    oc_b, ol_b = prefix_scan((qc, kc[1], vc, fc[1]), (ql, kl[1], vl, fl[1]), reverse=True)
    y_lat = gated_readout(ol_f + ol_b, gtl, norm_g)
    y_ctx = gated_readout(oc_f + oc_b, gtc, norm_g) if with_ctx_out else None
    return y_lat, y_ctx


def neighbourhood_attention(q, k, v, k_ctx, v_ctx, rpb):
    b, h, t_len, dh = q.shape
    rows = t_len // GRID_W
    kr = min(WIN_ROWS, rows)
    kc = WIN_COLS
    qg, kg, vg = (t.reshape(b, h, rows, GRID_W, dh) for t in (q, k, v))
    row_start = jnp.clip(jnp.arange(rows) - kr // 2, 0, rows - kr)
    col_start = jnp.clip(jnp.arange(GRID_W) - kc // 2, 0, GRID_W - kc)
    col_idx = col_start[:, None] + jnp.arange(kc)
    dc = col_idx - jnp.arange(GRID_W)[:, None] + WIN_COLS - 1
    scale = dh ** -0.5
    n_win = kr * kc

    def row_block(args):
        q_r, r0, r = args
        k_win = lax.dynamic_slice_in_dim(kg, r0, kr, axis=2)[:, :, :, col_idx]
        v_win = lax.dynamic_slice_in_dim(vg, r0, kr, axis=2)[:, :, :, col_idx]
        dr = r0 + jnp.arange(kr) - r + WIN_ROWS - 1
        bias = rpb[:, dr[None, :, None], dc[:, None, :]]
        s_win = jnp.einsum('bhqd,bhrqcd->bhqrc', q_r, k_win).astype(jnp.float32) * scale + bias.astype(jnp.float32)
        s_ctx = jnp.einsum('bhqd,bhkd->bhqk', q_r, k_ctx).astype(jnp.float32) * scale
        s = jnp.concatenate([s_win.reshape(b, h, GRID_W, n_win), s_ctx], axis=-1)
        p = jax.nn.softmax(s, axis=-1).astype(v.dtype)
        p_win = p[..., :n_win].reshape(b, h, GRID_W, kr, kc)
        return (jnp.einsum('bhqrc,bhrqcd->bhqd', p_win, v_win)
                + jnp.einsum('bhqk,bhkd->bhqd', p[..., n_win:], v_ctx))

    o = lax.map(row_block, (jnp.moveaxis(qg, 2, 0), row_start, jnp.arange(rows)))
    return jnp.moveaxis(o, 0, 2).reshape(b, h, t_len, dh)


def context_attention(q, k, v):
    s = jnp.einsum('bhqd,bhkd->bhqk', q, k).astype(jnp.float32) * q.shape[-1] ** -0.5
    return jnp.einsum('bhqk,bhkd->bhqd', jax.nn.softmax(s, axis=-1).astype(v.dtype), v)


def na_mixer(p_ctx, p_lat, q_norm, k_norm, rpb, with_ctx_out):
    def prep(p):
        q, k, v = p
        return (rmsnorm(to_heads(q, NA_HEADS), q_norm), rmsnorm(to_heads(k, NA_HEADS), k_norm), to_heads(v, NA_HEADS))

    qc, kc, vc = prep(p_ctx)
    ql, kl, vl = prep(p_lat)
    y_lat = from_heads(neighbourhood_attention(ql, kl, vl, kc, vc, rpb))
    y_ctx = from_heads(context_attention(qc, kc, vc)) if with_ctx_out else None
    return y_lat, y_ctx


def sqrelu_mlp(h, w1, w2):
    return jnp.square(jax.nn.relu(h @ w1)) @ w2


def setup_inputs(seed: int = 0) -> dict:
    key = jax.random.key(seed)
    ks = jax.random.split(key, 20)

    def nrm(k, shape, s):
        return jax.random.normal(k, shape, jnp.float32) * s

    return {
        "x": nrm(ks[0], (BATCH, SEQ, D_MODEL), 1.0),
        "c": nrm(ks[1], (BATCH, D_MODEL), 1.0),
        "ctx": nrm(ks[2], (BATCH, CTX_LEN, D_MODEL), 1.0),
        "c_ctx": nrm(ks[3], (D_MODEL,), 1.0),
        "w_mod": nrm(ks[4], (DEPTH, D_MODEL, 6 * D_MODEL), 0.5 * D_MODEL ** -0.5),
        "b_mod": nrm(ks[5], (DEPTH, 6 * D_MODEL), 0.02),
        "attn_norm": 1.0 + nrm(ks[6], (DEPTH, D_MODEL), 0.02),
        "w_in": nrm(ks[7], (DEPTH, D_MODEL, IN_COLS), D_MODEL ** -0.5),
        "gla_w_a2": nrm(ks[8], (DEPTH, N_DIR, GLA_RANK, GLA_HEADS * GLA_DK), GLA_RANK ** -0.5),
        "gla_b_a": nrm(ks[9], (DEPTH, N_DIR, GLA_HEADS * GLA_DK), 0.1),
        "gla_norm": 1.0 + nrm(ks[10], (DEPTH, GLA_DV), 0.02),
        "hg_lower_bounds": nrm(ks[11], (DEPTH, N_DIR, HG_WIDTH), 0.1),
        "hg_norm": 1.0 + nrm(ks[12], (DEPTH, HG_DV), 0.02),
        "na_q_norm": 1.0 + nrm(ks[13], (DEPTH, NA_DH), 0.02),
        "na_k_norm": 1.0 + nrm(ks[14], (DEPTH, NA_DH), 0.02),
        "na_rpb": nrm(ks[15], (DEPTH, NA_HEADS, 2 * WIN_ROWS - 1, 2 * WIN_COLS - 1), 0.1),
        "w_out": nrm(ks[16], (DEPTH, MIX_WIDTH, D_MODEL), MIX_WIDTH ** -0.5),
        "mlp_norm": 1.0 + nrm(ks[17], (DEPTH, D_MODEL), 0.02),
        "w_mlp1": nrm(ks[18], (DEPTH, D_MODEL, D_FF), D_MODEL ** -0.5),
        "w_mlp2": nrm(ks[19], (DEPTH, D_FF, D_MODEL), D_FF ** -0.5),
    }


def reference(x, c, ctx, c_ctx, w_mod, b_mod, attn_norm, w_in, gla_w_a2, gla_b_a, gla_norm,
              hg_lower_bounds, hg_norm, na_q_norm, na_k_norm, na_rpb, w_out, mlp_norm, w_mlp1, w_mlp2):
    t_len = x.shape[1]
    pos = jnp.arange(t_len)
    row_pos, col_pos = pos // GRID_W, pos % GRID_W
    lb_p = jax.nn.softmax(hg_lower_bounds.astype(jnp.float32), axis=0)
    lower_bounds = jnp.cumsum(lb_p, axis=0) - lb_p[0]

    for l in range(DEPTH):
        with_ctx_out = l < DEPTH - 1
        mod_lat = (jax.nn.silu(c) @ w_mod[l] + b_mod[l])[:, None, :]
        mod_ctx = jax.nn.silu(c_ctx) @ w_mod[l] + b_mod[l]
        sh_a, sc_a, g_a, sh_m, sc_m, g_m = jnp.split(mod_lat, 6, axis=-1)
        csh_a, csc_a, cg_a, csh_m, csc_m, cg_m = jnp.split(mod_ctx, 6, axis=-1)

        h_lat = rmsnorm(x, attn_norm[l]) * (1.0 + sc_a) + sh_a
        h_ctx = rmsnorm(ctx, attn_norm[l]) * (1.0 + csc_a) + csh_a
        p_lat = split_cols(h_lat @ w_in[l])
        p_ctx = split_cols(h_ctx @ w_in[l])

        ya_lat, ya_ctx = gla_mixer(p_ctx[0:5], p_lat[0:5], gla_w_a2[l], gla_b_a[l], gla_norm[l], row_pos, col_pos, with_ctx_out)
        yb_lat, yb_ctx = hgrn2_mixer(p_ctx[5:9], p_lat[5:9], lower_bounds[l], hg_norm[l], with_ctx_out)
        yc_lat, yc_ctx = na_mixer(p_ctx[9:12], p_lat[9:12], na_q_norm[l], na_k_norm[l], na_rpb[l], with_ctx_out)

        y_lat = jnp.concatenate([ya_lat, yb_lat, yc_lat], axis=-1)
        x = x + g_a * (y_lat @ w_out[l])
        x = x + g_m * sqrelu_mlp(rmsnorm(x, mlp_norm[l]) * (1.0 + sc_m) + sh_m, w_mlp1[l], w_mlp2[l])
        if with_ctx_out:
            y_ctx = jnp.concatenate([ya_ctx, yb_ctx, yc_ctx], axis=-1)
            ctx = ctx + cg_a * (y_ctx @ w_out[l])
            ctx = ctx + cg_m * sqrelu_mlp(rmsnorm(ctx, mlp_norm[l]) * (1.0 + csc_m) + csh_m, w_mlp1[l], w_mlp2[l])
    return x
```

```python
import numpy as np
from contextlib import ExitStack
import concourse.bass as bass
import concourse.mybir as mybir
from concourse.bass_utils import run_bass_kernel_spmd

F32 = mybir.dt.float32
BF16 = mybir.dt.bfloat16
AF = mybir.ActivationFunctionType
ALU = mybir.AluOpType

D = 2048
DFF = 8192
NCH = 16
SEQ = 4096
CTX = 256
EPS = 1e-6


class Buf:
    def __init__(self, name, t):
        self.name = name
        self.t = t
        self.w = None
        self.r = {}
        self.dsem = None
        self.dcount = 0

    def __getitem__(self, idx):
        return self.t[idx]


class Eng:
    def __init__(self, name, eng, sem, selfsync):
        self.name, self.eng, self.sem, self.selfsync = name, eng, sem, selfsync
        self.count = 0
        self.seen = {}


class Prog:
    def __init__(self):
        self.nc = bass.Bass("TRN2", target_bir_lowering=False)
        self.es = ExitStack()
        nc = self.nc

        def mk(name, e, selfsync):
            return Eng(name, e, self.es.enter_context(nc.semaphore("s_" + name)), selfsync)

        self.pe = mk("pe", nc.tensor, False)
        self.act = mk("act", nc.scalar, True)
        self.dve = mk("dve", nc.vector, True)
        self.pool = mk("pool", nc.gpsimd, True)
        self.sp = mk("sp", nc.sync, True)
        self.dma_bufs = []
        self.nbuf = 0
        self.stacks = [self.es]

    def push(self):
        st = ExitStack()
        self.stacks.append(st)

    def pop(self):
        self.barrier()
        self.stacks.pop().close()

    def barrier(self):
        engs = [self.pe, self.act, self.dve, self.pool, self.sp]
        for E in engs:
            for F in engs:
                if F is E or F.count == 0:
                    continue
                k = id(F.sem)
                if E.seen.get(k, 0) < F.count:
                    E.eng.wait_ge(F.sem, F.count)
                    E.seen[k] = F.count
            for b in self.dma_bufs:
                k = id(b.dsem)
                if b.dcount and E.seen.get(k, 0) < 16 * b.dcount:
                    E.eng.wait_ge(b.dsem, 16 * b.dcount)
                    E.seen[k] = 16 * b.dcount

    def sbuf(self, name, shape, dt):
        self.nbuf += 1
        return Buf(name, self.stacks[-1].enter_context(self.nc.sbuf_tensor(f"{name}_{self.nbuf}", list(shape), dt)))

    def psum(self, name, shape, dt):
        self.nbuf += 1
        return Buf(name, self.es.enter_context(self.nc.psum_tensor(f"{name}_{self.nbuf}", list(shape), dt)))

    def din(self, name, shape, dt=F32):
        return self.nc.dram_tensor(name, list(shape), dt, kind="ExternalInput").ap()

    def dout(self, name, shape, dt=F32):
        return self.nc.dram_tensor(name, list(shape), dt, kind="ExternalOutput").ap()

    def _wait(self, E, deps):
        best = {}
        for (sem, val, src) in deps:
            k = id(sem)
            if k not in best or val > best[k][1]:
                best[k] = (sem, val)
        for k, (sem, val) in best.items():
            if E.seen.get(k, 0) >= val:
                continue
            E.eng.wait_ge(sem, val)
            E.seen[k] = val

    def _deps(self, E, reads, writes, skip_dma_waw=False):
        deps = []
        for b in reads:
            if b.w is not None and not (b.w[2] is E and not E.selfsync):
                deps.append(b.w)
        for b in writes:
            if b.w is not None and not (b.w[2] is E and not E.selfsync):
                if not (skip_dma_waw and b.w[2] is None and b.w[0] is b.dsem):
                    deps.append(b.w)
            for tok in b.r.values():
                if tok[2] is E:
                    continue
                deps.append(tok)
        return deps

    def op(self, E, fn, reads=(), writes=(), inc=True):
        self._wait(E, self._deps(E, reads, writes))
        ins = fn(E.eng)
        tok = (E.sem, E.count + 1, E)
        if inc:
            ins.then_inc(E.sem, 1)
            E.count += 1
        for b in writes:
            b.w = tok
            b.r = {}
        for b in reads:
            b.r[id(E.sem)] = tok
        return ins

    def dma(self, E, out_ap, in_ap, rd=None, wr=None):
        b = wr if wr is not None else rd
        reads = [rd] if rd is not None else []
        writes = [wr] if wr is not None else []
        self._wait(E, self._deps(E, reads, writes, skip_dma_waw=True))
        if b.dsem is None:
            b.dsem = self.es.enter_context(self.nc.semaphore("d_" + b.name + str(len(self.dma_bufs))))
            self.dma_bufs.append(b)
        ins = E.eng.dma_start(out=out_ap, in_=in_ap)
        b.dcount += 1
        ins.then_inc(b.dsem, 16)
        tok = (b.dsem, 16 * b.dcount, None)
        if wr is not None:
            b.w = tok
            b.r = {}
        else:
            b.r[id(b.dsem)] = tok
        return ins

    def mm(self, ob, out_ap, lb, lhsT_ap, rb, rhs_ap, start, stop, inc=None):
        rd = [lb] if lb is rb else [lb, rb]
        return self.op(self.pe, lambda e: e.matmul(out_ap, lhsT=lhsT_ap, rhs=rhs_ap, start=start, stop=stop),
                       reads=rd, writes=[ob], inc=stop if inc is None else inc)

    def finish(self):
        E = self.sp
        for b in self.dma_bufs:
            if b.dcount:
                k = id(b.dsem)
                if E.seen.get(k, 0) < 16 * b.dcount:
                    E.eng.wait_ge(b.dsem, 16 * b.dcount)
                    E.seen[k] = 16 * b.dcount
        self.es.close()


def build_M():
    P = Prog()
    nc = P.nc
    NCC = 24
    cT = P.din("cT", [128, NCH * 3])
    wm = P.din("wm", [D, NCC * 128])
    bm = P.din("bm", [128, NCC])
    om = P.dout("om", [128, NCC * 3])
    c_sb = P.sbuf("c_sb", [128, NCH * 3], F32)
    e_sb = P.sbuf("e_sb", [128, NCH * 3], F32)
    s_sb = P.sbuf("s_sb", [128, NCH * 3], F32)
    b_sb = P.sbuf("b_sb", [128, NCC], F32)
    o_sb = P.sbuf("o_sb", [128, NCC * 3], F32)
    P.dma(P.sp, c_sb[:], cT[:, :], wr=c_sb)
    P.dma(P.sp, b_sb[:], bm[:, :], wr=b_sb)
    P.op(P.act, lambda e: e.activation(out=e_sb[:], in_=c_sb[:], func=AF.Exp, scale=-1.0), reads=[c_sb], writes=[e_sb])
    P.op(P.dve, lambda e: e.tensor_scalar(out=e_sb[:], in0=e_sb[:], scalar1=1.0, scalar2=None, op0=ALU.add), reads=[e_sb], writes=[e_sb])
    P.op(P.dve, lambda e: e.reciprocal(out=e_sb[:], in_=e_sb[:]), reads=[e_sb], writes=[e_sb])
    P.op(P.dve, lambda e: e.tensor_tensor(out=s_sb[:], in0=c_sb[:], in1=e_sb[:], op=ALU.mult), reads=[c_sb, e_sb], writes=[s_sb])
    GW = 4
    ng = NCC // GW
    wts = [P.sbuf(f"wt{i}", [128, NCH, GW * 128], F32) for i in range(2)]
    pss = [P.psum(f"ps{i}", [128, 512], F32) for i in range(4)]
    wmv = wm.rearrange("(k p) c -> p k c", p=128)

    def load(g):
        wt = wts[g % 2]
        for h in range(2):
            P.dma(P.sp, wt[:, h * 8:(h + 1) * 8, :], wmv[:, h * 8:(h + 1) * 8, g * GW * 128:(g + 1) * GW * 128], wr=wt)

    load(0)
    load(1)
    for g in range(ng):
        wt = wts[g % 2]
        for j in range(GW):
            cc = g * GW + j
            ps = pss[cc % 4]
            for k in range(NCH):
                P.mm(ps, ps[:, 0:3], wt, wt[:, k, j * 128:(j + 1) * 128], s_sb, s_sb[:, k * 3:(k + 1) * 3], k == 0, k == NCH - 1)
            P.op(P.dve, lambda e: e.tensor_scalar(out=o_sb[:, cc * 3:(cc + 1) * 3], in0=ps[:, 0:3], scalar1=b_sb[:, cc:cc + 1],
                                                  scalar2=None, op0=ALU.add), reads=[ps, b_sb], writes=[o_sb])
        if g + 2 < ng:
            load(g + 2)
    P.dma(P.sp, om[:, :], o_sb[:], rd=o_sb)
    P.finish()
    return nc


def build_B(blocks):
    N = sum(b[1] for b in blocks)
    P = Prog()
    nc = P.nc
    xT = P.din("xT", [D, N])
    yT = P.din("yT", [D, N])
    vecs = P.din("vecs", [128, 9 * NCH])
    w_out = P.din("w_out", [D, D])
    w1 = P.din("w1", [D, DFF])
    w2 = P.din("w2", [DFF, D])
    xo = P.dout("xo", [D, N])
    xTv = xT.rearrange("(c p) n -> p c n", p=128)
    yTv = yT.rearrange("(c p) n -> p c n", p=128)
    xov = xo.rearrange("(c p) n -> p c n", p=128)

    v_sb = P.sbuf("v_sb", [128, 9 * NCH], F32)
    gs_sb = P.sbuf("gs_sb", [128, 2 * NCH], F32)
    eps_sb = P.sbuf("eps_sb", [128, 1], F32)
    ones_sb = P.sbuf("ones_sb", [128, 128], BF16)
    xs = [P.sbuf(f"x{i}", [128, NCH, b[1]], F32) for i, b in enumerate(blocks)]
    ys = [P.sbuf(f"y{i}", [128, NCH, b[1]], BF16) for i, b in enumerate(blocks)]
    NW = 3
    wbufs = [P.sbuf(f"w{i}", [128, 8192], BF16) for i in range(NW)]
    a_sb = [[P.sbuf(f"a{j}_{i}", [128, 4, b[1]], BF16) for i, b in enumerate(blocks)] for j in range(2)]
    r_sb = [P.sbuf(f"r{i}", [128, 512], F32) for i in range(2)]
    sq_sb = [P.sbuf(f"sq{i}", [128, 512], BF16) for i in range(2)]
    t_sb = [P.sbuf(f"t{i}", [128, 512], F32) for i in range(2)]
    rstd_sb = P.sbuf("rstd", [128, 512], F32)
    pss = [P.psum(f"ps{i}", [128, 512], F32) for i in range(8)]
    psi = [0]

    def nextps():
        psi[0] += 1
        return pss[psi[0] % 8]

    P.dma(P.sp, v_sb[:], vecs[:, :], wr=v_sb)
    P.op(P.dve, lambda e: e.memset(eps_sb[:], EPS), writes=[eps_sb])
    P.op(P.dve, lambda e: e.memset(ones_sb[:], 1.0), writes=[ones_sb])
    for kind in range(2):
        P.op(P.dve, lambda e: e.scalar_tensor_tensor(out=gs_sb[:, kind * NCH:(kind + 1) * NCH],
                                                     in0=v_sb[:, (kind * 4 + 2) * NCH:(kind * 4 + 3) * NCH], scalar=1.0,
                                                     in1=v_sb[:, 8 * NCH:9 * NCH], op0=ALU.add, op1=ALU.mult),
             reads=[v_sb], writes=[gs_sb])
    for i, (s, n, kind) in enumerate(blocks):
        for h in range(4):
            P.dma(P.sp, xs[i][:, h * 4:(h + 1) * 4, :], xTv[:, h * 4:(h + 1) * 4, s:s + n], wr=xs[i])
        for h in range(4):
            P.dma(P.pool, ys[i][:, h * 4:(h + 1) * 4, :], yTv[:, h * 4:(h + 1) * 4, s:s + n], wr=ys[i])

    tiles = []
    for g in range(4):
        tiles.append(("o", g))
    for fg in range(16):
        tiles.append(("1", fg))
        tiles.append(("2", fg))
    w_outv = w_out.rearrange("(k p) c -> p k c", p=128)
    w1v = w1.rearrange("(k p) c -> p k c", p=128)
    w2v = w2.rearrange("(f p) c -> p f c", p=128)

    def load(ti):
        kind, g = tiles[ti]
        wb = wbufs[ti % NW]
        if kind == "o":
            dst = wb[:].rearrange("p (k c) -> p k c", k=16)
            for h in range(4):
                P.dma(P.pool, dst[:, h * 4:(h + 1) * 4, :], w_outv[:, h * 4:(h + 1) * 4, g * 512:(g + 1) * 512], wr=wb)
        elif kind == "1":
            dst = wb[:].rearrange("p (k c) -> p k c", k=16)
            for h in range(4):
                P.dma(P.pool, dst[:, h * 4:(h + 1) * 4, :], w1v[:, h * 4:(h + 1) * 4, g * 512:(g + 1) * 512], wr=wb)
        else:
            dst = wb[:].rearrange("p (f c) -> p f c", f=4)
            for h in range(4):
                P.dma(P.pool, dst[:, h:h + 1, :], w2v[:, g * 4 + h:g * 4 + h + 1, :], wr=wb)

    for ti in range(NW):
        load(ti)
    ti = 0
    for g in range(4):
        wb = wbufs[ti % NW]
        wv = wb[:].rearrange("p (k c) -> p k c", k=16)
        for i, (s, n, kind) in enumerate(blocks):
            for j in range(4):
                dc = g * 4 + j
                ps = nextps()
                for k in range(NCH):
                    P.mm(ps, ps[:, :n], wb, wv[:, k, j * 128:(j + 1) * 128], ys[i], ys[i][:, k, :], k == 0, k == NCH - 1)
                ga = v_sb[:, (kind * 4 + 0) * NCH + dc:(kind * 4 + 0) * NCH + dc + 1]
                P.op(P.dve, lambda e: e.scalar_tensor_tensor(out=xs[i][:, dc, :], in0=ps[:, :n], scalar=ga, in1=xs[i][:, dc, :],
                                                             op0=ALU.mult, op1=ALU.add), reads=[ps, v_sb, xs[i]], writes=[xs[i]])
        if ti + NW < len(tiles):
            load(ti + NW)
        ti += 1
    for i, (s, n, kind) in enumerate(blocks):
        ps = nextps()
        for c in range(NCH):
            sq = sq_sb[c % 2]
            P.op(P.act, lambda e: e.activation(out=sq[:, :n], in_=xs[i][:, c, :], func=AF.Square), reads=[xs[i]], writes=[sq])
            P.mm(ps, ps[:, :n], ones_sb, ones_sb[:], sq, sq[:, :n], c == 0, c == NCH - 1, inc=True)
        P.op(P.act, lambda e: e.activation(out=rstd_sb[:, :n], in_=ps[:, :n], func=AF.Ln, bias=eps_sb[:, 0:1], scale=1.0 / D),
             reads=[ps, eps_sb], writes=[rstd_sb])
        P.op(P.act, lambda e: e.activation(out=rstd_sb[:, :n], in_=rstd_sb[:, :n], func=AF.Exp, scale=-0.5),
             reads=[rstd_sb], writes=[rstd_sb])
        for c in range(NCH):
            t = t_sb[c % 2]
            gsc = gs_sb[:, kind * NCH + c:kind * NCH + c + 1]
            shc = v_sb[:, (kind * 4 + 1) * NCH + c:(kind * 4 + 1) * NCH + c + 1]
            P.op(P.dve, lambda e: e.scalar_tensor_tensor(out=t[:, :n], in0=xs[i][:, c, :], scalar=gsc, in1=rstd_sb[:, :n],
                                                         op0=ALU.mult, op1=ALU.mult), reads=[xs[i], gs_sb, rstd_sb], writes=[t])
            P.op(P.act, lambda e: e.activation(out=ys[i][:, c, :], in_=t[:, :n], func=AF.Identity, bias=shc, scale=1.0),
                 reads=[t, v_sb], writes=[ys[i]])
    ri = 0
    for fg in range(16):
        wb1 = wbufs[ti % NW]
        w1t = wb1[:].rearrange("p (k c) -> p k c", k=16)
        ab = a_sb[fg % 2]
        for i, (s, n, kind) in enumerate(blocks):
            for j in range(4):
                ps = nextps()
                for k in range(NCH):
                    P.mm(ps, ps[:, :n], wb1, w1t[:, k, j * 128:(j + 1) * 128], ys[i], ys[i][:, k, :], k == 0, k == NCH - 1)
                r = r_sb[ri % 2]
                ri += 1
                P.op(P.act, lambda e: e.activation(out=r[:, :n], in_=ps[:, :n], func=AF.Relu), reads=[ps], writes=[r])
                P.op(P.act, lambda e: e.activation(out=ab[i][:, j, :], in_=r[:, :n], func=AF.Square), reads=[r], writes=[ab[i]])
        if ti + NW < len(tiles):
            load(ti + NW)
        ti += 1
        wb2 = wbufs[ti % NW]
        w2t = wb2[:].rearrange("p (f c) -> p f c", f=4)
        for i, (s, n, kind) in enumerate(blocks):
            for dc in range(NCH):
                ps = nextps()
                for j in range(4):
                    P.mm(ps, ps[:, :n], wb2, w2t[:, j, dc * 128:(dc + 1) * 128], ab[i], ab[i][:, j, :], j == 0, j == 3)
                gm = v_sb[:, (kind * 4 + 3) * NCH + dc:(kind * 4 + 3) * NCH + dc + 1]
                P.op(P.dve, lambda e: e.scalar_tensor_tensor(out=xs[i][:, dc, :], in0=ps[:, :n], scalar=gm, in1=xs[i][:, dc, :],
                                                             op0=ALU.mult, op1=ALU.add), reads=[ps, v_sb, xs[i]], writes=[xs[i]])
        if ti + NW < len(tiles):
            load(ti + NW)
        ti += 1
    for i, (s, n, kind) in enumerate(blocks):
        for h in range(4):
            P.dma(P.sp, xov[:, h * 4:(h + 1) * 4, s:s + n], xs[i][:, h * 4:(h + 1) * 4, :], rd=xs[i])
    P.finish()
    return nc


NTOK = CTX + SEQ
NT = NTOK // 128
NBLK = NTOK // 256
NEG = -30000.0


def _stage1(P, xTv, vA, gsA, eps_sb, ones_bf, pss, nextps, w_tm, tm_groups, w_fm, fm_groups, tm_cb, fm_cb):
    P.push()
    xb = [P.sbuf(f"xb{i}", [128, NCH, 256], F32) for i in range(2)]
    hb = [P.sbuf(f"hb{i}", [128, NCH, 256], BF16) for i in range(2)]
    sq_sb = [P.sbuf(f"sq{i}", [128, 256], BF16) for i in range(2)]
    t_sb = [P.sbuf(f"t{i}", [128, 256], F32) for i in range(2)]
    rstd_sb = P.sbuf("rstd", [128, 256], F32)

    def loadx(blk):
        x = xb[blk % 2]
        for h in range(4):
            P.dma(P.sp, x[:, h * 4:(h + 1) * 4, :], xTv[:, h * 4:(h + 1) * 4, blk * 256:(blk + 1) * 256], wr=x)

    loadx(0)
    for blk in range(NBLK):
        if blk + 1 < NBLK:
            loadx(blk + 1)
        kind = 1 if blk == 0 else 0
        x = xb[blk % 2]
        h = hb[blk % 2]
        ps = nextps()
        for c in range(NCH):
            sq = sq_sb[c % 2]
            P.op(P.act, lambda e: e.activation(out=sq[:], in_=x[:, c, :], func=AF.Square), reads=[x], writes=[sq])
            P.mm(ps, ps[:, :256], ones_bf, ones_bf[:], sq, sq[:], c == 0, c == NCH - 1, inc=True)
        P.op(P.act, lambda e: e.activation(out=rstd_sb[:], in_=ps[:, :256], func=AF.Ln, bias=eps_sb[:, 0:1], scale=1.0 / D),
             reads=[ps, eps_sb], writes=[rstd_sb])
        P.op(P.act, lambda e: e.activation(out=rstd_sb[:], in_=rstd_sb[:], func=AF.Exp, scale=-0.5), reads=[rstd_sb], writes=[rstd_sb])
        for c in range(NCH):
            t = t_sb[c % 2]
            gsc = gsA[:, kind * NCH + c:kind * NCH + c + 1]
            shc = vA[:, (kind * 2) * NCH + c:(kind * 2) * NCH + c + 1]
            P.op(P.dve, lambda e: e.scalar_tensor_tensor(out=t[:], in0=x[:, c, :], scalar=gsc, in1=rstd_sb[:],
                                                         op0=ALU.mult, op1=ALU.mult), reads=[x, gsA, rstd_sb], writes=[t])
            P.op(P.act, lambda e: e.activation(out=h[:, c, :], in_=t[:], func=AF.Identity, bias=shc, scale=1.0),
                 reads=[t, vA], writes=[h])
        for gi, (c0, M) in enumerate(fm_groups):
            ps = nextps()
            for k in range(NCH):
                P.mm(ps, ps[:M, :256], w_fm, w_fm[:, k, c0:c0 + M], h, h[:, k, :], k == 0, k == NCH - 1)
            fm_cb(gi, blk, ps)
        for tt in range(2):
            tile = blk * 2 + tt
            for gi, (c0, W) in enumerate(tm_groups):
                ps = nextps()
                for k in range(NCH):
                    P.mm(ps, ps[:, :W], h, h[:, k, tt * 128:(tt + 1) * 128], w_tm, w_tm[:, k, c0:c0 + W], k == 0, k == NCH - 1)
                tm_cb(gi, tile, ps)
    P.pop()


def _silu_gate(P, ps, n, dst_buf, dst_ap, tmp):
    P.op(P.act, lambda e: e.activation(out=tmp[:, :n], in_=ps[:, :n], func=AF.Exp, scale=-1.0), reads=[ps], writes=[tmp])
    P.op(P.dve, lambda e: e.tensor_scalar(out=tmp[:, :n], in0=tmp[:, :n], scalar1=1.0, scalar2=None, op0=ALU.add), reads=[tmp], writes=[tmp])
    P.op(P.dve, lambda e: e.reciprocal(out=tmp[:, :n], in_=tmp[:, :n]), reads=[tmp], writes=[tmp])
    P.op(P.dve, lambda e: e.tensor_tensor(out=dst_ap, in0=ps[:, :n], in1=tmp[:, :n], op=ALU.mult), reads=[ps, tmp], writes=[dst_buf])


def _scan(P, pss, nextps, cm, ident_bf, dk, q_st, k_st, v_st, la_st, O_st):
    P.push()
    NB = 2
    e12 = [[P.sbuf(f"e12_{d}{i}", [128, 384], F32) for i in range(NB)] for d in range(2)]
    e2n = [[P.sbuf(f"e2n_{d}{i}", [128, 256], F32) for i in range(NB)] for d in range(2)]
    e3 = [[P.sbuf(f"e3_{d}{i}", [128, 128], F32) for i in range(NB)] for d in range(2)]
    qb = [[P.sbuf(f"qb_{d}{i}", [128, 128], BF16) for i in range(NB)] for d in range(2)]
    qa = [[P.sbuf(f"qa_{d}{i}", [128, 128], BF16) for i in range(NB)] for d in range(2)]
    ka = [[P.sbuf(f"ka_{d}{i}", [128, 128], BF16) for i in range(NB)] for d in range(2)]
    qc = [[P.sbuf(f"qc_{d}{i}", [128, 128], BF16) for i in range(NB)] for d in range(2)]
    kc = [[P.sbuf(f"kc_{d}{i}", [128, 128], BF16) for i in range(NB)] for d in range(2)]
    kb = [[P.sbuf(f"kb_{d}{i}", [128, 128], BF16) for i in range(NB)] for d in range(2)]
    am1 = [[P.sbuf(f"am1_{d}{i}", [128, 128], BF16) for i in range(NB)] for d in range(2)]
    am2 = [[P.sbuf(f"am2_{d}{i}", [128, 128], BF16) for i in range(NB)] for d in range(2)]
    S32 = [P.sbuf(f"S32_{d}", [128, 128], F32) for d in range(2)]
    Sbf = [[P.sbuf(f"Sbf_{d}{i}", [128, 128], BF16) for i in range(2)] for d in range(2)]
    sidx = [0, 0]
    P.op(P.dve, lambda e: e.memset(O_st[:], 0.0), writes=[O_st])
    for d in range(2):
        P.op(P.dve, lambda e: e.memset(S32[d][:], 0.0), writes=[S32[d]])
        P.op(P.dve, lambda e: e.memset(Sbf[d][0][:], 0.0), writes=[Sbf[d][0]])
    order = [list(range(NT)), [1, 0] + list(range(NT - 1, 1, -1))]
    for step in range(NT):
        for d in range(2):
            tile = order[d][step]
            it = step % NB
            la_buf, la_ap = la_st[d](tile)
            U, Ua, Uc, Ub, M1, M2 = (cm[:, (d * 6 + i) * 128:(d * 6 + i + 1) * 128] for i in range(6))
            p12 = nextps("b")
            P.mm(p12, p12[:dk, 0:128], la_buf, la_ap, cm, U, True, True)
            P.mm(p12, p12[:dk, 128:256], la_buf, la_ap, cm, Ua, True, True)
            P.mm(p12, p12[:dk, 256:384], la_buf, la_ap, cm, Uc, True, True)
            p3 = nextps("b")
            P.mm(p3, p3[:, 0:dk], cm, Ub, la_buf, la_ap, True, True)
            pt = nextps("b")
            P.mm(pt, pt[:dk, 0:128], q_st, q_st[:, tile, :], ident_bf, ident_bf[:], True, True)
            P.mm(pt, pt[:dk, 128:256], k_st[d], k_st[d][:, tile, :], ident_bf, ident_bf[:], True, True)
            E12, E2n, E3 = e12[d][it], e2n[d][it], e3[d][it]
            P.op(P.act, lambda e: e.activation(out=E12[:dk, :], in_=p12[:dk, 0:384], func=AF.Exp), reads=[p12], writes=[E12])
            P.op(P.act, lambda e: e.activation(out=E2n[:dk, :], in_=p12[:dk, 128:384], func=AF.Exp, scale=-1.0), reads=[p12], writes=[E2n])
            P.op(P.act, lambda e: e.activation(out=E3[:, :dk], in_=p3[:, :dk], func=AF.Exp), reads=[p3], writes=[E3])
            QB, QA, KA, QC, KC, KB = qb[d][it], qa[d][it], ka[d][it], qc[d][it], kc[d][it], kb[d][it]
            AM1, AM2 = am1[d][it], am2[d][it]
            P.op(P.dve, lambda e: e.tensor_tensor(out=QB[:dk, :], in0=pt[:dk, 0:128], in1=E12[:dk, 0:128], op=ALU.mult),
                 reads=[pt, E12], writes=[QB])
            P.op(P.dve, lambda e: e.tensor_tensor(out=QA[:dk, :], in0=pt[:dk, 0:128], in1=E12[:dk, 128:256], op=ALU.mult),
                 reads=[pt, E12], writes=[QA])
            P.op(P.dve, lambda e: e.tensor_tensor(out=KA[:dk, :], in0=pt[:dk, 128:256], in1=E2n[:dk, 0:128], op=ALU.mult),
                 reads=[pt, E2n], writes=[KA])
            P.op(P.dve, lambda e: e.scalar_tensor_tensor(out=QC[:dk, :], in0=E12[:dk, 256:384], scalar=1.0, in1=pt[:dk, 0:128],
                                                         op0=ALU.min, op1=ALU.mult), reads=[pt, E12], writes=[QC])
            P.op(P.dve, lambda e: e.scalar_tensor_tensor(out=KC[:dk, :], in0=E2n[:dk, 128:256], scalar=1.0, in1=pt[:dk, 128:256],
                                                         op0=ALU.min, op1=ALU.mult), reads=[pt, E2n], writes=[KC])
            P.op(P.dve, lambda e: e.tensor_tensor(out=KB[:, :dk], in0=k_st[d][:, tile, :], in1=E3[:, :dk], op=ALU.mult),
                 reads=[k_st[d], E3], writes=[KB])
            pa = nextps("b")
            P.mm(pa, pa[:, 0:128], KA, KA[:dk, :], QA, QA[:dk, :], True, True)
            P.mm(pa, pa[:, 128:256], KC, KC[:dk, :], QC, QC[:dk, :], True, True)
            P.op(P.dve, lambda e: e.tensor_tensor(out=AM1[:], in0=pa[:, 0:128], in1=M1, op=ALU.mult), reads=[pa, cm], writes=[AM1])
            P.op(P.dve, lambda e: e.tensor_tensor(out=AM2[:], in0=pa[:, 128:256], in1=M2, op=ALU.mult), reads=[pa, cm], writes=[AM2])
            po = nextps("a")
            corder = [0, 1] if d == 0 else [1, 0]
            for c in corder:
                cs = slice(c * 64, (c + 1) * 64)
                Sb = Sbf[d][sidx[d] % 2]
                P.mm(po, po[:, cs], v_st, v_st[:, tile, :], AM1, AM1[:, cs], True, False)
                P.mm(po, po[:, cs], v_st, v_st[:, tile, :], AM2, AM2[:, cs], False, False)
                P.mm(po, po[:, cs], Sb, Sb[:dk, :], QB, QB[:dk, cs], False, True)
                pk = nextps("a")
                P.mm(pk, pk[:dk, 0:128], KB, KB[cs, :dk], v_st, v_st[cs, tile, :], True, True)
                ecol = (c * 64 + 63) if d == 0 else (c * 64)
                P.op(P.dve, lambda e: e.scalar_tensor_tensor(out=S32[d][:dk, :], in0=S32[d][:dk, :], scalar=E12[:dk, ecol:ecol + 1],
                                                             in1=pk[:dk, 0:128], op0=ALU.mult, op1=ALU.add),
                     reads=[S32[d], E12, pk], writes=[S32[d]])
                sidx[d] += 1
                Sn = Sbf[d][sidx[d] % 2]
                P.op(P.act, lambda e: e.activation(out=Sn[:dk, :], in_=S32[d][:dk, :], func=AF.Identity), reads=[S32[d]], writes=[Sn])
            ts = slice(tile * 128, (tile + 1) * 128)
            P.op(P.dve, lambda e: e.tensor_tensor(out=O_st[:, ts], in0=po[:, 0:128], in1=O_st[:, ts], op=ALU.add),
                 reads=[po, O_st], writes=[O_st])
    P.pop()


def _readout(P, nextps, O_st, gate_st, gn_ap, gn_buf, eps_sb, ones_bf, yTv_rows):
    P.push()
    sq = [P.sbuf(f"rsq{i}", [128, 512], BF16) for i in range(2)]
    rs = [P.sbuf(f"rrs{i}", [128, 512], F32) for i in range(2)]
    yo = [P.sbuf(f"ryo{i}", [128, 512], F32) for i in range(2)]
    for bi, s in enumerate(range(0, NTOK, 512)):
        n = min(512, NTOK - s)
        SQ, RS, YO = sq[bi % 2], rs[bi % 2], yo[bi % 2]
        P.op(P.act, lambda e: e.activation(out=SQ[:, :n], in_=O_st[:, s:s + n], func=AF.Square), reads=[O_st], writes=[SQ])
        ps = nextps()
        P.mm(ps, ps[:, :n], ones_bf, ones_bf[:], SQ, SQ[:, :n], True, True)
        P.op(P.act, lambda e: e.activation(out=RS[:, :n], in_=ps[:, :n], func=AF.Ln, bias=eps_sb[:, 0:1], scale=1.0 / 128),
             reads=[ps, eps_sb], writes=[RS])
        P.op(P.act, lambda e: e.activation(out=RS[:, :n], in_=RS[:, :n], func=AF.Exp, scale=-0.5), reads=[RS], writes=[RS])
        P.op(P.dve, lambda e: e.scalar_tensor_tensor(out=YO[:, :n], in0=O_st[:, s:s + n], scalar=gn_ap, in1=RS[:, :n],
                                                     op0=ALU.mult, op1=ALU.mult), reads=[O_st, gn_buf, RS], writes=[YO])
        P.op(P.dve, lambda e: e.tensor_tensor(out=YO[:, :n], in0=YO[:, :n], in1=gate_st[:, s:s + n], op=ALU.mult),
             reads=[YO, gate_st], writes=[YO])
        P.dma(P.sp, yTv_rows[:, s:s + n], YO[:, :n], rd=YO)
    P.pop()


def na_tile_plan():
    plan = {}
    keys = {}
    for n in range(8):
        for m in range(32):
            rows_q = np.arange(8 * n, 8 * n + 8)
            r0 = np.clip(rows_q - 4, 0, 56)
            ok = False
            for kr in (2 * m, 2 * m + 1):
                if np.any((r0 <= kr) & (kr < r0 + 8)):
                    ok = True
            if not ok:
                continue
            key = ("top", m) if n == 0 else (("bot", m) if n == 7 else ("int", m - 4 * n))
            if key not in keys:
                keys[key] = (len(keys), n, m)
            plan[(n, m)] = keys[key][0]
    reps = sorted(keys.values())
    return plan, [(n, m) for (_, n, m) in reps]


def build_A():
    P = Prog()
    nc = P.nc
    plan, reps = na_tile_plan()
    NBT = len(reps)
    xT = P.din("xT", [D, NTOK])
    vecsA = P.din("vecsA", [128, 5 * NCH])
    pvec = P.din("pvec", [128, 4])
    w_na_tm = P.din("w_na_tm", [D, 256])
    w_na_fm = P.din("w_na_fm", [D, 512])
    w_gla_tm = P.din("w_gla_tm", [D, 384])
    w_gla_fm = P.din("w_gla_fm", [D, 160])
    w_hg_tm = P.din("w_hg_tm", [D, 512])
    w_hg_fm = P.din("w_hg_fm", [D, 128])
    wa2 = P.din("wa2", [128, 128])
    ba = P.din("ba", [128, 128])
    lbraw = P.din("lbraw", [128, 512])
    lflag = P.din("lflag", [128, 1])
    rope = P.din("rope", [128, 32 * 128])
    cmats = P.din("cmats", [128, 12 * 128])
    identd = P.din("ident", [128, 128])
    nabias = P.din("nabias", [128, 2 * NBT * 512])
    yT = P.dout("yT", [512, NTOK])
    xTv = xT.rearrange("(c p) n -> p c n", p=128)

    vA = P.sbuf("vA", [128, 5 * NCH], F32)
    gsA = P.sbuf("gsA", [128, 2 * NCH], F32)
    pv = P.sbuf("pv", [128, 4], F32)
    pvs = P.sbuf("pvs", [128, 1], F32)
    eps_sb = P.sbuf("eps_sb", [128, 1], F32)
    ones_bf = P.sbuf("ones_bf", [128, 128], BF16)
    ones_f = P.sbuf("ones_f", [128, 128], F32)
    ident_bf = P.sbuf("ident_bf", [128, 128], BF16)
    cm = P.sbuf("cm", [128, 12 * 128], F32)
    pss = [P.psum(f"ps{i}", [128, 512], F32) for i in range(8)]
    psi = [0]

    psa = [0]
    psb = [0]

    def nextps(pool=None):
        if pool == "a":
            psa[0] += 1
            return pss[psa[0] % 4]
        if pool == "b":
            psb[0] += 1
            return pss[4 + psb[0] % 4]
        psi[0] += 1
        return pss[psi[0] % 8]

    P.dma(P.sp, vA[:], vecsA[:, :], wr=vA)
    P.dma(P.sp, pv[:], pvec[:, :], wr=pv)
    P.dma(P.sp, cm[:], cmats[:, :], wr=cm)
    P.dma(P.pool, ident_bf[:], identd[:, :], wr=ident_bf)
    P.op(P.dve, lambda e: e.memset(eps_sb[:], EPS), writes=[eps_sb])
    P.op(P.dve, lambda e: e.memset(ones_bf[:], 1.0), writes=[ones_bf])
    P.op(P.dve, lambda e: e.memset(ones_f[:], 1.0), writes=[ones_f])
    for kind in range(2):
        P.op(P.dve, lambda e: e.scalar_tensor_tensor(out=gsA[:, kind * NCH:(kind + 1) * NCH],
                                                     in0=vA[:, (kind * 2 + 1) * NCH:(kind * 2 + 2) * NCH], scalar=1.0,
                                                     in1=vA[:, 4 * NCH:5 * NCH], op0=ALU.add, op1=ALU.mult), reads=[vA], writes=[gsA])
    P.op(P.dve, lambda e: e.tensor_scalar(out=pvs[:], in0=pv[:, 2:3], scalar1=float(128 ** -0.5), scalar2=None, op0=ALU.mult),
         reads=[pv], writes=[pvs])

    def load_w(buf, src, ncols):
        v = src.rearrange("(k p) c -> p k c", p=128)
        for h in range(4):
            P.dma(P.pool, buf[:, h * 4:(h + 1) * 4, :], v[:, h * 4:(h + 1) * 4, :], wr=buf)

    def na_phase():
        P.push()
        w_tm = P.sbuf("wna_tm", [128, NCH, 256], BF16)
        w_fm = P.sbuf("wna_fm", [128, NCH, 512], BF16)
        load_w(w_tm, w_na_tm, 256)
        load_w(w_fm, w_na_fm, 512)
        qT = P.sbuf("na_qT", [128, 2, NTOK], BF16)
        kT = P.sbuf("na_kT", [128, 2, NTOK], BF16)
        v_na = P.sbuf("na_v", [128, NT, 256], BF16)
        bias_sb = P.sbuf("na_bias", [128, 2 * NBT, 512], BF16)
        nbv = nabias.rearrange("p (t q) -> p t q", q=512)
        for t0 in range(0, 2 * NBT, 8):
            t1 = min(2 * NBT, t0 + 8)
            P.dma(P.pool, bias_sb[:, t0:t1, :], nbv[:, t0:t1, :], wr=bias_sb)
        tq = [P.sbuf(f"na_tq{i}", [128, 256], F32) for i in range(2)]
        sqn = [P.sbuf(f"na_sq{i}", [128, 256], BF16) for i in range(2)]
        rsn = [P.sbuf(f"na_rs{i}", [128, 256], F32) for i in range(2)]
        cnt = [0]

        def na_fm_cb(gi, blk, ps):
            i = cnt[0] % 2
            cnt[0] += 1
            TQ, SQ, RS = tq[i], sqn[i], rsn[i]
            P.op(P.act, lambda e: e.activation(out=TQ[:], in_=ps[:, :256], func=AF.Identity), reads=[ps], writes=[TQ])
            P.op(P.act, lambda e: e.activation(out=SQ[:], in_=TQ[:], func=AF.Square), reads=[TQ], writes=[SQ])
            p2 = nextps()
            P.mm(p2, p2[:, :256], ones_bf, ones_bf[:], SQ, SQ[:], True, True)
            P.op(P.act, lambda e: e.activation(out=RS[:], in_=p2[:, :256], func=AF.Ln, bias=eps_sb[:, 0:1], scale=1.0 / 128),
                 reads=[p2, eps_sb], writes=[RS])
            P.op(P.act, lambda e: e.activation(out=RS[:], in_=RS[:], func=AF.Exp, scale=-0.5), reads=[RS], writes=[RS])
            dst = qT if gi < 2 else kT
            gsc = pvs[:, 0:1] if gi < 2 else pv[:, 3:4]
            P.op(P.dve, lambda e: e.scalar_tensor_tensor(out=dst[:, gi % 2, blk * 256:(blk + 1) * 256], in0=TQ[:], scalar=gsc, in1=RS[:],
                                                         op0=ALU.mult, op1=ALU.mult), reads=[TQ, pvs, pv, RS], writes=[dst])

        def na_tm_cb(gi, tile, ps):
            P.op(P.act, lambda e: e.activation(out=v_na[:, tile, :], in_=ps[:, :256], func=AF.Identity), reads=[ps], writes=[v_na])

        _stage1(P, xTv, vA, gsA, eps_sb, ones_bf, pss, nextps, w_tm, [(0, 256)], w_fm, [(0, 128), (128, 128), (256, 128), (384, 128)],
                na_tm_cb, na_fm_cb)
        if DBG.get("stop") == 1:
            P.pop()
            return
        pT = [P.sbuf(f"na_pT{i}", [128, 512], BF16) for i in range(3)]
        rsum = [P.sbuf(f"na_rsum{i}", [128, 512], F32) for i in range(2)]
        yo = [P.sbuf(f"na_yo{i}", [128, 512], F32) for i in range(2)]
        pi = 0
        qi = 0
        for hh in range(2):
            qblocks = [(0, 256, None)] + [(CTX + 512 * n, 512, n) for n in range(8)]
            for (q0, nq, n) in qblocks:
                ktiles = [(0, None), (1, None)]
                if n is not None:
                    ktiles += [(2 + m, hh * NBT + plan[(n, m)]) for m in range(32) if (n, m) in plan]
                po = nextps("a")
                psm = nextps("a")
                for ki, (kt, bt) in enumerate(ktiles):
                    pscore = nextps("b")
                    P.mm(pscore, pscore[:, :nq], kT, kT[:, hh, kt * 128:(kt + 1) * 128], qT, qT[:, hh, q0:q0 + nq], True, bt is None)
                    if bt is not None:
                        P.mm(pscore, pscore[:, :nq], ident_bf, ident_bf[:], bias_sb, bias_sb[:, bt, :nq], False, True)
                    PT = pT[pi % 3]
                    pi += 1
                    P.op(P.act, lambda e: e.activation(out=PT[:, :nq], in_=pscore[:, :nq], func=AF.Exp), reads=[pscore], writes=[PT])
                    last = ki == len(ktiles) - 1
                    P.mm(po, po[:, :nq], v_na, v_na[:, kt, hh * 128:(hh + 1) * 128], PT, PT[:, :nq], ki == 0, last, inc=True)
                    P.mm(psm, psm[:, :nq], ones_bf, ones_bf[:], PT, PT[:, :nq], ki == 0, last, inc=True)
                RSM, YO = rsum[qi % 2], yo[qi % 2]
                qi += 1
                P.op(P.dve, lambda e: e.reciprocal(out=RSM[:, :nq], in_=psm[:, :nq]), reads=[psm], writes=[RSM])
                P.op(P.dve, lambda e: e.tensor_tensor(out=YO[:, :nq], in0=po[:, :nq], in1=RSM[:, :nq], op=ALU.mult), reads=[po, RSM], writes=[YO])
                P.dma(P.sp, yT[256 + hh * 128:256 + (hh + 1) * 128, q0:q0 + nq], YO[:, :nq], rd=YO)
        P.pop()


    if not DBG.get("skip_na"):
        na_phase()
    if DBG.get("stop") == 1:
        P.finish()
        return nc
    if DBG.get("stop") == 2:
        P.finish()
        return nc
    P.push()
    w_tm = P.sbuf("wgla_tm", [128, NCH, 384], BF16)
    w_fm = P.sbuf("wgla_fm", [128, NCH, 160], BF16)
    load_w(w_tm, w_gla_tm, 384)
    load_w(w_fm, w_gla_fm, 160)
    rope_sb = P.sbuf("rope_sb", [128, 32, 128], F32)
    ropev = rope.rearrange("p (t c) -> p t c", c=128)
    for h in range(8):
        P.dma(P.sp, rope_sb[:, h * 4:(h + 1) * 4, :], ropev[:, h * 4:(h + 1) * 4, :], wr=rope_sb)
    wa2_sb = P.sbuf("wa2_sb", [128, 128], F32)
    ba_sb = P.sbuf("ba_sb", [128, 128], F32)
    P.dma(P.sp, wa2_sb[:], wa2[:, :], wr=wa2_sb)
    P.dma(P.sp, ba_sb[:], ba[:, :], wr=ba_sb)
    q_g = P.sbuf("gla_q", [128, NT, 64], BF16)
    k_g = P.sbuf("gla_k", [128, NT, 64], BF16)
    v_g = P.sbuf("gla_v", [128, NT, 128], BF16)
    la_g = P.sbuf("gla_la", [128, NT, 128], F32)
    gate_g = P.sbuf("gla_gate", [128, NTOK], BF16)
    al_g = [P.sbuf(f"gla_al{i}", [128, 256], F32) for i in range(2)]
    tmpg = [P.sbuf(f"gla_tmp{i}", [128, 256], F32) for i in range(2)]
    rt = [P.sbuf(f"gla_rt{i}", [128, 256], F32) for i in range(2)]
    cg = [0, 0]

    def gla_fm_cb(gi, blk, ps):
        if gi == 0:
            if DBG.get("skip_gate"):
                return
            T = tmpg[cg[0] % 2]
            cg[0] += 1
            _silu_gate(P, ps, 256, gate_g, gate_g[:, blk * 256:(blk + 1) * 256], T)
        elif not DBG.get("skip_la"):
            AL = al_g[blk % 2]
            P.op(P.act, lambda e: e.activation(out=AL[:, :], in_=ps[:, :256], func=AF.Identity), reads=[ps], writes=[AL])
            for tt in range(2):
                tile = blk * 2 + tt
                p2 = nextps()
                P.mm(p2, p2[:, :128], AL, AL[:, tt * 128:(tt + 1) * 128], wa2_sb, wa2_sb[:, :], True, True)
                T = tmpg[cg[0] % 2]
                cg[0] += 1
                P.op(P.dve, lambda e: e.tensor_tensor(out=T[:, :128], in0=p2[:, :128], in1=ba_sb[:, :], op=ALU.add), reads=[p2, ba_sb], writes=[T])
                P.op(P.act, lambda e: e.activation(out=T[:, :128], in_=T[:, :128], func=AF.Exp, scale=-1.0), reads=[T], writes=[T])
                P.op(P.act, lambda e: e.activation(out=T[:, :128], in_=T[:, :128], func=AF.Ln, bias=ones_f[:, 0:1], scale=1.0),
                     reads=[T, ones_f], writes=[T])
                P.op(P.dve, lambda e: e.tensor_scalar(out=la_g[:, tile, :], in0=T[:, :128], scalar1=-1.0 / 16.0, scalar2=None, op0=ALU.mult),
                     reads=[T], writes=[la_g])

    def gla_tm_cb(gi, tile, ps):
        if DBG.get("skip_tm"):
            return
        if tile < 2 or DBG.get("skip_rope"):
            P.op(P.act, lambda e: e.activation(out=q_g[:, tile, :], in_=ps[:, 0:64], func=AF.Identity, scale=0.125), reads=[ps], writes=[q_g])
            P.op(P.act, lambda e: e.activation(out=k_g[:, tile, :], in_=ps[:, 128:192], func=AF.Identity), reads=[ps], writes=[k_g])
        else:
            lt = tile - 2
            R = rt[cg[1] % 2]
            cg[1] += 1
            P.op(P.act, lambda e: e.activation(out=R[:, 0:128], in_=ps[:, 0:128], func=AF.Identity, scale=0.125), reads=[ps], writes=[R])
            P.op(P.act, lambda e: e.activation(out=R[:, 128:256], in_=ps[:, 128:256], func=AF.Identity), reads=[ps], writes=[R])
            for hh2 in range(2):
                P.op(P.dve, lambda e: e.tensor_tensor(out=R[:, hh2 * 128:(hh2 + 1) * 128], in0=R[:, hh2 * 128:(hh2 + 1) * 128],
                                                      in1=rope_sb[:, lt, :], op=ALU.mult), reads=[R, rope_sb], writes=[R])
            P.op(P.dve, lambda e: e.tensor_tensor(out=q_g[:, tile, :], in0=R[:, 0:64], in1=R[:, 64:128], op=ALU.add), reads=[R], writes=[q_g])
            P.op(P.dve, lambda e: e.tensor_tensor(out=k_g[:, tile, :], in0=R[:, 128:192], in1=R[:, 192:256], op=ALU.add), reads=[R], writes=[k_g])
        P.op(P.act, lambda e: e.activation(out=v_g[:, tile, :], in_=ps[:, 256:384], func=AF.Identity), reads=[ps], writes=[v_g])

    _stage1(P, xTv, vA, gsA, eps_sb, ones_bf, pss, nextps, w_tm, [(0, 384)], w_fm,
            [(32, 128)] if DBG.get("skip_alow") else [(32, 128), (0, 128)], gla_tm_cb, gla_fm_cb)
    if DBG.get("stop") == 3:
        P.pop()
        P.finish()
        return nc
    O_g = P.sbuf("gla_O", [128, NTOK], F32)
    _scan(P, pss, nextps, cm, ident_bf, 64, q_g, [k_g, k_g], v_g,
          [lambda tile: (la_g, la_g[:, tile, 0:64]), lambda tile: (la_g, la_g[:, tile, 64:128])], O_g)
    if DBG.get("stop") == 4:
        P.pop()
        P.finish()
        return nc
    _readout(P, nextps, O_g, gate_g, pv[:, 0:1], pv, eps_sb, ones_bf, yT[0:128, :])
    P.pop()
    if DBG.get("stop") == 5:
        P.finish()
        return nc

    P.push()
    w_tm = P.sbuf("whg_tm", [128, NCH, 512], BF16)
    w_fm = P.sbuf("whg_fm", [128, NCH, 128], BF16)
    load_w(w_tm, w_hg_tm, 512)
    load_w(w_fm, w_hg_fm, 128)
    lb_r = P.sbuf("lb_r", [128, 512], F32)
    lf_sb = P.sbuf("lf_sb", [128, 1], F32)
    lb_t = P.sbuf("lb_t", [128, 256], F32)
    oml_t = P.sbuf("oml_t", [128, 256], F32)
    P.dma(P.sp, lb_r[:], lbraw[:, :], wr=lb_r)
    P.dma(P.sp, lf_sb[:], lflag[:, :], wr=lf_sb)
    P.op(P.act, lambda e: e.activation(out=lb_r[:], in_=lb_r[:], func=AF.Exp), reads=[lb_r], writes=[lb_r])
    P.op(P.dve, lambda e: e.tensor_tensor(out=lb_t[:], in0=lb_r[:, 0:256], in1=lb_r[:, 256:512], op=ALU.add), reads=[lb_r], writes=[lb_t])
    P.op(P.dve, lambda e: e.reciprocal(out=lb_t[:], in_=lb_t[:]), reads=[lb_t], writes=[lb_t])
    P.op(P.dve, lambda e: e.tensor_tensor(out=lb_t[:], in0=lb_t[:], in1=lb_r[:, 256:512], op=ALU.mult), reads=[lb_t, lb_r], writes=[lb_t])
    P.op(P.dve, lambda e: e.tensor_scalar(out=lb_t[:], in0=lb_t[:], scalar1=lf_sb[:, 0:1], scalar2=None, op0=ALU.mult),
         reads=[lb_t, lf_sb], writes=[lb_t])
    P.op(P.dve, lambda e: e.tensor_scalar(out=oml_t[:], in0=lb_t[:], scalar1=-1.0, scalar2=1.0, op0=ALU.mult, op1=ALU.add),
         reads=[lb_t], writes=[oml_t])
    q_h = P.sbuf("hg_q", [128, NT, 128], BF16)
    k_h = [P.sbuf(f"hg_k{d}", [128, NT, 128], BF16) for d in range(2)]
    v_h = P.sbuf("hg_v", [128, NT, 128], BF16)
    la_h = P.sbuf("hg_la", [128, NT, 256], F32)
    gate_h = P.sbuf("hg_gate", [128, NTOK], BF16)
    tmph = [P.sbuf(f"hg_tmp{i}", [128, 256], F32) for i in range(2)]
    eh = [P.sbuf(f"hg_e{i}", [128, 384], F32) for i in range(2)]
    ch = [0, 0]

    def hg_fm_cb(gi, blk, ps):
        T = tmph[ch[0] % 2]
        ch[0] += 1
        _silu_gate(P, ps, 256, gate_h, gate_h[:, blk * 256:(blk + 1) * 256], T)

    def hg_tm_cb(gi, tile, ps):
        E = eh[ch[1] % 2]
        ch[1] += 1
        P.op(P.act, lambda e: e.activation(out=v_h[:, tile, :], in_=ps[:, 384:512], func=AF.Identity), reads=[ps], writes=[v_h])
        P.op(P.act, lambda e: e.activation(out=E[:, 0:384], in_=ps[:, 0:384], func=AF.Exp, scale=-1.0), reads=[ps], writes=[E])
        P.op(P.dve, lambda e: e.tensor_scalar(out=E[:, 0:384], in0=E[:, 0:384], scalar1=1.0, scalar2=None, op0=ALU.add), reads=[E], writes=[E])
        P.op(P.dve, lambda e: e.reciprocal(out=E[:, 0:384], in_=E[:, 0:384]), reads=[E], writes=[E])
        P.op(P.dve, lambda e: e.tensor_tensor(out=q_h[:, tile, :], in0=ps[:, 0:128], in1=E[:, 0:128], op=ALU.mult), reads=[ps, E], writes=[q_h])
        P.op(P.dve, lambda e: e.tensor_tensor(out=E[:, 128:384], in0=E[:, 128:384], in1=oml_t[:], op=ALU.mult), reads=[E, oml_t], writes=[E])
        P.op(P.dve, lambda e: e.tensor_tensor(out=E[:, 128:384], in0=E[:, 128:384], in1=lb_t[:], op=ALU.add), reads=[E, lb_t], writes=[E])
        P.op(P.act, lambda e: e.activation(out=la_h[:, tile, :], in_=E[:, 128:384], func=AF.Ln), reads=[E], writes=[la_h])
        for d in range(2):
            P.op(P.dve, lambda e: e.tensor_scalar(out=k_h[d][:, tile, :], in0=E[:, 128 + d * 128:256 + d * 128], scalar1=-1.0, scalar2=1.0,
                                                  op0=ALU.mult, op1=ALU.add), reads=[E], writes=[k_h[d]])

    _stage1(P, xTv, vA, gsA, eps_sb, ones_bf, pss, nextps, w_tm, [(0, 512)], w_fm, [(0, 128)], hg_tm_cb, hg_fm_cb)
    O_h = P.sbuf("hg_O", [128, NTOK], F32)
    _scan(P, pss, nextps, cm, ident_bf, 128, q_h, k_h, v_h,
          [lambda tile: (la_h, la_h[:, tile, 0:128]), lambda tile: (la_h, la_h[:, tile, 128:256])], O_h)
    _readout(P, nextps, O_h, gate_h, pv[:, 1:2], pv, eps_sb, ones_bf, yT[128:256, :])
    P.pop()
    P.finish()
    return nc


GLA_Q0, GLA_K0, GLA_V0, GLA_R0, GLA_A0 = 0, 256, 512, 1024, 1536
HG_Q0, HG_F0, HG_I0, HG_G0 = 1568, 2080, 3104, 3616
NA_Q0, NA_K0, NA_V0 = 4128, 5152, 6176


def _pp(v):
    return np.ascontiguousarray(np.asarray(v, np.float32).reshape(NCH, 128).T)


def _consts():
    j = np.arange(128)[:, None]
    i = np.arange(128)[None, :]
    same = (j // 64) == (i // 64)
    base = (i // 64) * 64
    half = (i % 64) // 32
    jhalf = (j % 64) // 32
    mats = []
    f = lambda m: m.astype(np.float32)
    U = f(same & (j <= i))
    Ua = U - f(same & (j <= base + 32 * half + 15))
    Uc = U - f(same & (j <= base + 31))
    Ub = f(same & (j > i))
    M1 = f(same & (jhalf == half) & (j <= i))
    M2 = f(same & (jhalf == 0) & (half == 1))
    mats += [U, Ua, Uc, Ub, M1, M2]
    Ur = f(same & (j >= i))
    Uar = Ur - f(same & (j >= base + 32 * half + 16))
    Ucr = Ur - f(same & (j >= base + 32))
    Ubr = f(same & (j < i))
    M1r = f(same & (jhalf == half) & (j >= i))
    M2r = f(same & (jhalf == 1) & (half == 0))
    mats += [Ur, Uar, Ucr, Ubr, M1r, M2r]
    cm = np.concatenate(mats, axis=1).astype(np.float32)
    ident = np.eye(128, dtype=np.float32)
    pos = np.arange(SEQ)
    rowp, colp = pos // 64, pos % 64
    inv = (10000.0 ** (-np.arange(16, dtype=np.float32) / 16)).astype(np.float32)
    cos = np.zeros((SEQ, 64), np.float32)
    sin = np.zeros((SEQ, 64), np.float32)
    for half, pp in enumerate((rowp, colp)):
        ang = pp.astype(np.float32)[:, None] * inv[None, :]
        c_, s_ = np.cos(ang).astype(np.float32), np.sin(ang).astype(np.float32)
        b0 = half * 32
        cos[:, b0:b0 + 16] = c_
        cos[:, b0 + 16:b0 + 32] = c_
        sin[:, b0:b0 + 16] = -s_
        sin[:, b0 + 16:b0 + 32] = s_
    tab = np.concatenate([cos, sin], axis=1).reshape(32, 128, 128).transpose(1, 0, 2).reshape(128, 32 * 128)
    return cm, ident, np.ascontiguousarray(tab)


def _swap_idx():
    idx = np.arange(64)
    out = idx.copy()
    for b0 in (0, 32):
        out[b0:b0 + 16] = idx[b0 + 16:b0 + 32]
        out[b0 + 16:b0 + 32] = idx[b0:b0 + 16]
    return out


def _na_bias_tiles(rpb_h, reps):
    out = np.full((128, len(reps), 512), NEG, np.float32)
    kl = np.arange(128)
    ql = np.arange(512)
    for ti, (n, m) in enumerate(reps):
        kr = (2 * m + kl // 64)[:, None]
        kc = (kl % 64)[:, None]
        r = (8 * n + ql // 64)[None, :]
        c = (ql % 64)[None, :]
        r0 = np.clip(r - 4, 0, 56)
        c0 = np.clip(c - 8, 0, 48)
        valid = (kr >= r0) & (kr < r0 + 8) & (kc >= c0) & (kc < c0 + 16)
        dr = np.clip(kr - r + 7, 0, 14)
        dc = np.clip(kc - c + 15, 0, 30)
        vals = rpb_h[dr, dc]
        out[:, ti, :] = np.where(valid, vals, np.float32(NEG))
    return out


_CACHE = {}
DBG = {}


def _get(name, fn):
    if name not in _CACHE:
        _CACHE[name] = fn()
    return _CACHE[name]


def kernel(x, c, ctx, c_ctx, w_mod, b_mod, attn_norm, w_in, gla_w_a2, gla_b_a, gla_norm, hg_lower_bounds, hg_norm,
           na_q_norm, na_k_norm, na_rpb, w_out, mlp_norm, w_mlp1, w_mlp2):
    f = lambda a: np.asarray(a, dtype=np.float32)
    x, c, ctx, c_ctx, w_mod, b_mod, attn_norm, w_in = map(f, (x, c, ctx, c_ctx, w_mod, b_mod, attn_norm, w_in))
    gla_w_a2, gla_b_a, gla_norm, hg_lower_bounds, hg_norm = map(f, (gla_w_a2, gla_b_a, gla_norm, hg_lower_bounds, hg_norm))
    na_q_norm, na_k_norm, na_rpb, w_out, mlp_norm, w_mlp1, w_mlp2 = map(f, (na_q_norm, na_k_norm, na_rpb, w_out, mlp_norm, w_mlp1, w_mlp2))
    cores = list(range(8))
    ncM = _get("M", build_M)
    cvec = np.stack([c[0], c[1], c_ctx], axis=1)
    cT = np.ascontiguousarray(cvec.reshape(NCH, 128, 3).transpose(1, 0, 2).reshape(128, NCH * 3))
    wcat = np.concatenate([w_mod[0], w_mod[1]], axis=1)
    bcat = np.concatenate([b_mod[0], b_mod[1]], axis=0)
    ims = []
    for cid in cores:
        sl = slice(cid * 3072, (cid + 1) * 3072)
        ims.append({"cT": cT, "wm": np.ascontiguousarray(wcat[:, sl]), "bm": np.ascontiguousarray(bcat[sl].reshape(24, 128).T)})
    res = run_bass_kernel_spmd(ncM, ims, core_ids=cores)
    mod = np.concatenate([r["om"].reshape(128, 24, 3).transpose(1, 0, 2).reshape(3072, 3) for r in res.results], axis=0)
    mod = mod.reshape(2, 6, D, 3)
    DBG["mod"] = mod
    if DBG.get("onlyM"):
        return None

    cm, ident, ropetab = _consts()
    plan, reps = na_tile_plan()
    sw = _swap_idx()
    ncA = _get("A", build_A)
    blocksB = [(0, 512, 0), (512, 512, 0), (1024, 64, 1)]
    ncB = _get("B", lambda: build_B(blocksB))
    xs = [x[0], x[1]]
    cs = [ctx[0], ctx[1]]
    for l in range(2):
        ims = []
        for cid in cores:
            b, j = cid // 4, cid % 4
            W = w_in[l]
            g64 = lambda o: np.arange(o + j * 64, o + (j + 1) * 64)
            g128 = lambda o, h=None: np.arange(o + (j if h is None else h) * 128, o + ((j if h is None else h) + 1) * 128)
            gq, gk = g64(GLA_Q0), g64(GLA_K0)
            cols_gla_tm = np.concatenate([gq, gq[sw], gk, gk[sw], g128(GLA_V0)])
            cols_gla_fm = np.concatenate([np.arange(GLA_A0, GLA_A0 + 32), g128(GLA_R0)])
            cols_hg_tm = np.concatenate([g128(HG_Q0), g128(HG_F0), g128(HG_F0 + 512), g128(HG_I0)])
            cols_hg_fm = g128(HG_G0)
            h0, h1 = 2 * j, 2 * j + 1
            cols_na_tm = np.concatenate([g128(NA_V0, h0), g128(NA_V0, h1)])
            cols_na_fm = np.concatenate([g128(NA_Q0, h0), g128(NA_Q0, h1), g128(NA_K0, h0), g128(NA_K0, h1)])
            wa2 = np.zeros((128, 128), np.float32)
            wa2[0:16, 0:64] = gla_w_a2[l, 0][:, j * 64:(j + 1) * 64]
            wa2[16:32, 64:128] = gla_w_a2[l, 1][:, j * 64:(j + 1) * 64]
            ba = np.concatenate([gla_b_a[l, 0][j * 64:(j + 1) * 64], gla_b_a[l, 1][j * 64:(j + 1) * 64]])[None, :]
            hc = slice(j * 128, (j + 1) * 128)
            lbraw = np.concatenate([hg_lower_bounds[0, 0][hc], hg_lower_bounds[0, 1][hc],
                                    hg_lower_bounds[1, 0][hc], hg_lower_bounds[1, 1][hc]])[None, :]
            nab = np.concatenate([_na_bias_tiles(na_rpb[l, h0], reps), _na_bias_tiles(na_rpb[l, h1], reps)], axis=1)
            vecsA = np.concatenate([_pp(mod[l, 0, :, b]), _pp(mod[l, 1, :, b]), _pp(mod[l, 0, :, 2]), _pp(mod[l, 1, :, 2]),
                                    _pp(attn_norm[l])], axis=1)
            pvec = np.stack([gla_norm[l], hg_norm[l], na_q_norm[l], na_k_norm[l]], axis=1)
            ims.append({
                "xT": np.ascontiguousarray(np.concatenate([cs[b].T, xs[b].T], axis=1)),
                "vecsA": np.ascontiguousarray(vecsA), "pvec": np.ascontiguousarray(pvec),
                "w_na_tm": np.ascontiguousarray(W[:, cols_na_tm]), "w_na_fm": np.ascontiguousarray(W[:, cols_na_fm]),
                "w_gla_tm": np.ascontiguousarray(W[:, cols_gla_tm]), "w_gla_fm": np.ascontiguousarray(W[:, cols_gla_fm]),
                "w_hg_tm": np.ascontiguousarray(W[:, cols_hg_tm]), "w_hg_fm": np.ascontiguousarray(W[:, cols_hg_fm]),
                "wa2": wa2, "ba": np.ascontiguousarray(np.repeat(ba, 128, axis=0)), "lbraw": np.ascontiguousarray(np.repeat(lbraw, 128, axis=0)),
                "lflag": np.full((128, 1), float(l), np.float32),
                "rope": ropetab, "cmats": cm, "ident": ident,
                "nabias": np.ascontiguousarray(nab.reshape(128, -1)),
            })
        if DBG.get("onlyA"):
            r0 = run_bass_kernel_spmd(ncA, ims[:1], core_ids=[0], trace=bool(DBG.get("trace")))
            DBG["yT"] = r0.results[0]["yT"]
            DBG["res"] = r0
            return None
        res = run_bass_kernel_spmd(ncA, ims, core_ids=cores)
        yfull = [np.zeros((D, NTOK), np.float32) for _ in range(2)]
        for cid in cores:
            b, j = cid // 4, cid % 4
            yt = res.results[cid]["yT"]
            yfull[b][j * 128:(j + 1) * 128] = yt[0:128]
            yfull[b][512 + j * 128:512 + (j + 1) * 128] = yt[128:256]
            yfull[b][1024 + 2 * j * 128:1024 + (2 * j + 2) * 128] = yt[256:512]
        _CACHE["dbg_y%d" % l] = yfull
        ims = []
        for cid in cores:
            b, q = cid // 4, cid % 4
            lat = slice(q * 1024, (q + 1) * 1024)
            cr = slice(q * 64, (q + 1) * 64)
            xT = np.concatenate([xs[b][lat].T, cs[b][cr].T], axis=1)
            yT = np.concatenate([yfull[b][:, CTX + q * 1024:CTX + (q + 1) * 1024], yfull[b][:, q * 64:(q + 1) * 64]], axis=1)
            vecs = np.concatenate([_pp(mod[l, 2, :, b]), _pp(mod[l, 3, :, b]), _pp(mod[l, 4, :, b]), _pp(mod[l, 5, :, b]),
                                   _pp(mod[l, 2, :, 2]), _pp(mod[l, 3, :, 2]), _pp(mod[l, 4, :, 2]), _pp(mod[l, 5, :, 2]),
                                   _pp(mlp_norm[l])], axis=1)
            ims.append({"xT": np.ascontiguousarray(xT), "yT": np.ascontiguousarray(yT), "vecs": np.ascontiguousarray(vecs),
                        "w_out": w_out[l], "w1": w_mlp1[l], "w2": w_mlp2[l]})
        res = run_bass_kernel_spmd(ncB, ims, core_ids=cores)
        nx = [np.zeros_like(xs[0]), np.zeros_like(xs[1])]
        ncx = [np.zeros_like(cs[0]), np.zeros_like(cs[1])]
        for cid in cores:
            b, q = cid // 4, cid % 4
            xo = res.results[cid]["xo"]
            nx[b][q * 1024:(q + 1) * 1024] = xo[:, :1024].T
            ncx[b][q * 64:(q + 1) * 64] = xo[:, 1024:].T
        xs, cs = nx, ncx
    return np.stack(xs, axis=0).astype(np.float32)
```

```python
import numpy as np
from contextlib import ExitStack
import concourse.bass as bass
import concourse.mybir as mybir
from concourse.bass_utils import run_bass_kernel_spmd

F32 = mybir.dt.float32
BF16 = mybir.dt.bfloat16
AF = mybir.ActivationFunctionType
ALU = mybir.AluOpType

D = 2048
DFF = 8192
NCH = 16
SEQ = 4096
CTX = 256
EPS = 1e-6


class Buf:
    def __init__(self, name, t):
        self.name = name
        self.t = t
        self.w = None
        self.r = {}
        self.dsem = None
        self.dcount = 0

    def __getitem__(self, idx):
        return self.t[idx]


class Eng:
    def __init__(self, name, eng, sem, selfsync):
        self.name, self.eng, self.sem, self.selfsync = name, eng, sem, selfsync
        self.count = 0
        self.seen = {}


class Prog:
    def __init__(self):
        self.nc = bass.Bass("TRN2", target_bir_lowering=False)
        self.es = ExitStack()
        nc = self.nc

        def mk(name, e, selfsync):
            return Eng(name, e, self.es.enter_context(nc.semaphore("s_" + name)), selfsync)

        self.pe = mk("pe", nc.tensor, False)
        self.act = mk("act", nc.scalar, True)
        self.dve = mk("dve", nc.vector, True)
        self.pool = mk("pool", nc.gpsimd, True)
        self.sp = mk("sp", nc.sync, True)
        self.dma_bufs = []
        self.nbuf = 0
        self.stacks = [self.es]

    def push(self):
        st = ExitStack()
        self.stacks.append(st)

    def pop(self):
        self.barrier()
        self.stacks.pop().close()

    def barrier(self):
        engs = [self.pe, self.act, self.dve, self.pool, self.sp]
        for E in engs:
            for F in engs:
                if F is E or F.count == 0:
                    continue
                k = id(F.sem)
                if E.seen.get(k, 0) < F.count:
                    E.eng.wait_ge(F.sem, F.count)
                    E.seen[k] = F.count
            for b in self.dma_bufs:
                k = id(b.dsem)
                if b.dcount and E.seen.get(k, 0) < 16 * b.dcount:
                    E.eng.wait_ge(b.dsem, 16 * b.dcount)
                    E.seen[k] = 16 * b.dcount

    def sbuf(self, name, shape, dt):
        self.nbuf += 1
        return Buf(name, self.stacks[-1].enter_context(self.nc.sbuf_tensor(f"{name}_{self.nbuf}", list(shape), dt)))

    def psum(self, name, shape, dt):
        self.nbuf += 1
        return Buf(name, self.es.enter_context(self.nc.psum_tensor(f"{name}_{self.nbuf}", list(shape), dt)))

    def din(self, name, shape, dt=F32):
        return self.nc.dram_tensor(name, list(shape), dt, kind="ExternalInput").ap()

    def dout(self, name, shape, dt=F32):
        return self.nc.dram_tensor(name, list(shape), dt, kind="ExternalOutput").ap()

    def _wait(self, E, deps):
        best = {}
        for (sem, val, src) in deps:
            k = id(sem)
            if k not in best or val > best[k][1]:
                best[k] = (sem, val)
        for k, (sem, val) in best.items():
            if E.seen.get(k, 0) >= val:
                continue
            E.eng.wait_ge(sem, val)
            E.seen[k] = val

    def _deps(self, E, reads, writes, skip_dma_waw=False):
        deps = []
        for b in reads:
            if b.w is not None and not (b.w[2] is E and not E.selfsync):
                deps.append(b.w)
        for b in writes:
            if b.w is not None and not (b.w[2] is E and not E.selfsync):
                if not (skip_dma_waw and b.w[2] is None and b.w[0] is b.dsem):
                    deps.append(b.w)
            for tok in b.r.values():
                if tok[2] is E:
                    continue
                deps.append(tok)
        return deps

    def op(self, E, fn, reads=(), writes=(), inc=True):
        self._wait(E, self._deps(E, reads, writes))
        ins = fn(E.eng)
        tok = (E.sem, E.count + 1, E)
        if inc:
            ins.then_inc(E.sem, 1)
            E.count += 1
        for b in writes:
            b.w = tok
            b.r = {}
        for b in reads:
            b.r[id(E.sem)] = tok
        return ins

    def dma(self, E, out_ap, in_ap, rd=None, wr=None):
        b = wr if wr is not None else rd
        reads = [rd] if rd is not None else []
        writes = [wr] if wr is not None else []
        self._wait(E, self._deps(E, reads, writes, skip_dma_waw=True))
        if b.dsem is None:
            b.dsem = self.es.enter_context(self.nc.semaphore("d_" + b.name + str(len(self.dma_bufs))))
            self.dma_bufs.append(b)
        ins = E.eng.dma_start(out=out_ap, in_=in_ap)
        b.dcount += 1
        ins.then_inc(b.dsem, 16)
        tok = (b.dsem, 16 * b.dcount, None)
        if wr is not None:
            b.w = tok
            b.r = {}
        else:
            b.r[id(b.dsem)] = tok
        return ins

    def mm(self, ob, out_ap, lb, lhsT_ap, rb, rhs_ap, start, stop, inc=None):
        rd = [lb] if lb is rb else [lb, rb]
        return self.op(self.pe, lambda e: e.matmul(out_ap, lhsT=lhsT_ap, rhs=rhs_ap, start=start, stop=stop),
                       reads=rd, writes=[ob], inc=stop if inc is None else inc)

    def finish(self):
        E = self.sp
        for b in self.dma_bufs:
            if b.dcount:
                k = id(b.dsem)
                if E.seen.get(k, 0) < 16 * b.dcount:
                    E.eng.wait_ge(b.dsem, 16 * b.dcount)
                    E.seen[k] = 16 * b.dcount
        self.es.close()


def build_M():
    P = Prog()
    nc = P.nc
    NCC = 24
    cT = P.din("cT", [128, NCH * 3])
    wm = P.din("wm", [D, NCC * 128])
    bm = P.din("bm", [128, NCC])
    om = P.dout("om", [128, NCC * 3])
    c_sb = P.sbuf("c_sb", [128, NCH * 3], F32)
    e_sb = P.sbuf("e_sb", [128, NCH * 3], F32)
    s_sb = P.sbuf("s_sb", [128, NCH * 3], F32)
    b_sb = P.sbuf("b_sb", [128, NCC], F32)
    o_sb = P.sbuf("o_sb", [128, NCC * 3], F32)
    P.dma(P.sp, c_sb[:], cT[:, :], wr=c_sb)
    P.dma(P.sp, b_sb[:], bm[:, :], wr=b_sb)
    P.op(P.act, lambda e: e.activation(out=e_sb[:], in_=c_sb[:], func=AF.Exp, scale=-1.0), reads=[c_sb], writes=[e_sb])
    P.op(P.dve, lambda e: e.tensor_scalar(out=e_sb[:], in0=e_sb[:], scalar1=1.0, scalar2=None, op0=ALU.add), reads=[e_sb], writes=[e_sb])
    P.op(P.dve, lambda e: e.reciprocal(out=e_sb[:], in_=e_sb[:]), reads=[e_sb], writes=[e_sb])
    P.op(P.dve, lambda e: e.tensor_tensor(out=s_sb[:], in0=c_sb[:], in1=e_sb[:], op=ALU.mult), reads=[c_sb, e_sb], writes=[s_sb])
    GW = 4
    ng = NCC // GW
    wts = [P.sbuf(f"wt{i}", [128, NCH, GW * 128], F32) for i in range(2)]
    pss = [P.psum(f"ps{i}", [128, 512], F32) for i in range(4)]
    wmv = wm.rearrange("(k p) c -> p k c", p=128)

    def load(g):
        wt = wts[g % 2]
        for h in range(2):
            P.dma(P.sp, wt[:, h * 8:(h + 1) * 8, :], wmv[:, h * 8:(h + 1) * 8, g * GW * 128:(g + 1) * GW * 128], wr=wt)

    load(0)
    load(1)
    for g in range(ng):
        wt = wts[g % 2]
        for j in range(GW):
            cc = g * GW + j
            ps = pss[cc % 4]
            for k in range(NCH):
                P.mm(ps, ps[:, 0:3], wt, wt[:, k, j * 128:(j + 1) * 128], s_sb, s_sb[:, k * 3:(k + 1) * 3], k == 0, k == NCH - 1)
            P.op(P.dve, lambda e: e.tensor_scalar(out=o_sb[:, cc * 3:(cc + 1) * 3], in0=ps[:, 0:3], scalar1=b_sb[:, cc:cc + 1],
                                                  scalar2=None, op0=ALU.add), reads=[ps, b_sb], writes=[o_sb])
        if g + 2 < ng:
            load(g + 2)
    P.dma(P.sp, om[:, :], o_sb[:], rd=o_sb)
    P.finish()
    return nc


def build_B(blocks):
    N = sum(b[1] for b in blocks)
    P = Prog()
    nc = P.nc
    xT = P.din("xT", [D, N])
    yT = P.din("yT", [D, N])
    vecs = P.din("vecs", [128, 9 * NCH])
    w_out = P.din("w_out", [D, D])
    w1 = P.din("w1", [D, DFF])
    w2 = P.din("w2", [DFF, D])
    xo = P.dout("xo", [D, N])
    xTv = xT.rearrange("(c p) n -> p c n", p=128)
    yTv = yT.rearrange("(c p) n -> p c n", p=128)
    xov = xo.rearrange("(c p) n -> p c n", p=128)

    v_sb = P.sbuf("v_sb", [128, 9 * NCH], F32)
    gs_sb = P.sbuf("gs_sb", [128, 2 * NCH], F32)
    eps_sb = P.sbuf("eps_sb", [128, 1], F32)
    ones_sb = P.sbuf("ones_sb", [128, 128], BF16)
    xs = [P.sbuf(f"x{i}", [128, NCH, b[1]], F32) for i, b in enumerate(blocks)]
    ys = [P.sbuf(f"y{i}", [128, NCH, b[1]], BF16) for i, b in enumerate(blocks)]
    NW = 3
    wbufs = [P.sbuf(f"w{i}", [128, 8192], BF16) for i in range(NW)]
    a_sb = [[P.sbuf(f"a{j}_{i}", [128, 4, b[1]], BF16) for i, b in enumerate(blocks)] for j in range(2)]
    r_sb = [P.sbuf(f"r{i}", [128, 512], F32) for i in range(2)]
    sq_sb = [P.sbuf(f"sq{i}", [128, 512], BF16) for i in range(2)]
    t_sb = [P.sbuf(f"t{i}", [128, 512], F32) for i in range(2)]
    rstd_sb = P.sbuf("rstd", [128, 512], F32)
    pss = [P.psum(f"ps{i}", [128, 512], F32) for i in range(8)]
    psi = [0]

    def nextps():
        psi[0] += 1
        return pss[psi[0] % 8]

    P.dma(P.sp, v_sb[:], vecs[:, :], wr=v_sb)
    P.op(P.dve, lambda e: e.memset(eps_sb[:], EPS), writes=[eps_sb])
    P.op(P.dve, lambda e: e.memset(ones_sb[:], 1.0), writes=[ones_sb])
    for kind in range(2):
        P.op(P.dve, lambda e: e.scalar_tensor_tensor(out=gs_sb[:, kind * NCH:(kind + 1) * NCH],
                                                     in0=v_sb[:, (kind * 4 + 2) * NCH:(kind * 4 + 3) * NCH], scalar=1.0,
                                                     in1=v_sb[:, 8 * NCH:9 * NCH], op0=ALU.add, op1=ALU.mult),
             reads=[v_sb], writes=[gs_sb])
    for i, (s, n, kind) in enumerate(blocks):
        for h in range(4):
            P.dma(P.sp, xs[i][:, h * 4:(h + 1) * 4, :], xTv[:, h * 4:(h + 1) * 4, s:s + n], wr=xs[i])
        for h in range(4):
            P.dma(P.pool, ys[i][:, h * 4:(h + 1) * 4, :], yTv[:, h * 4:(h + 1) * 4, s:s + n], wr=ys[i])

    tiles = []
    for g in range(4):
        tiles.append(("o", g))
    for fg in range(16):
        tiles.append(("1", fg))
        tiles.append(("2", fg))
    w_outv = w_out.rearrange("(k p) c -> p k c", p=128)
    w1v = w1.rearrange("(k p) c -> p k c", p=128)
    w2v = w2.rearrange("(f p) c -> p f c", p=128)

    def load(ti):
        kind, g = tiles[ti]
        wb = wbufs[ti % NW]
        if kind == "o":
            dst = wb[:].rearrange("p (k c) -> p k c", k=16)
            for h in range(4):
                P.dma(P.pool, dst[:, h * 4:(h + 1) * 4, :], w_outv[:, h * 4:(h + 1) * 4, g * 512:(g + 1) * 512], wr=wb)
        elif kind == "1":
            dst = wb[:].rearrange("p (k c) -> p k c", k=16)
            for h in range(4):
                P.dma(P.pool, dst[:, h * 4:(h + 1) * 4, :], w1v[:, h * 4:(h + 1) * 4, g * 512:(g + 1) * 512], wr=wb)
        else:
            dst = wb[:].rearrange("p (f c) -> p f c", f=4)
            for h in range(4):
                P.dma(P.pool, dst[:, h:h + 1, :], w2v[:, g * 4 + h:g * 4 + h + 1, :], wr=wb)

    for ti in range(NW):
        load(ti)
    ti = 0
    for g in range(4):
        wb = wbufs[ti % NW]
        wv = wb[:].rearrange("p (k c) -> p k c", k=16)
        for i, (s, n, kind) in enumerate(blocks):
            for j in range(4):
                dc = g * 4 + j
                ps = nextps()
                for k in range(NCH):
                    P.mm(ps, ps[:, :n], wb, wv[:, k, j * 128:(j + 1) * 128], ys[i], ys[i][:, k, :], k == 0, k == NCH - 1)
                ga = v_sb[:, (kind * 4 + 0) * NCH + dc:(kind * 4 + 0) * NCH + dc + 1]
                P.op(P.dve, lambda e: e.scalar_tensor_tensor(out=xs[i][:, dc, :], in0=ps[:, :n], scalar=ga, in1=xs[i][:, dc, :],
                                                             op0=ALU.mult, op1=ALU.add), reads=[ps, v_sb, xs[i]], writes=[xs[i]])
        if ti + NW < len(tiles):
            load(ti + NW)
        ti += 1
    for i, (s, n, kind) in enumerate(blocks):
        ps = nextps()
        for c in range(NCH):
            sq = sq_sb[c % 2]
            P.op(P.act, lambda e: e.activation(out=sq[:, :n], in_=xs[i][:, c, :], func=AF.Square), reads=[xs[i]], writes=[sq])
            P.mm(ps, ps[:, :n], ones_sb, ones_sb[:], sq, sq[:, :n], c == 0, c == NCH - 1, inc=True)
        P.op(P.act, lambda e: e.activation(out=rstd_sb[:, :n], in_=ps[:, :n], func=AF.Ln, bias=eps_sb[:, 0:1], scale=1.0 / D),
             reads=[ps, eps_sb], writes=[rstd_sb])
        P.op(P.act, lambda e: e.activation(out=rstd_sb[:, :n], in_=rstd_sb[:, :n], func=AF.Exp, scale=-0.5),
             reads=[rstd_sb], writes=[rstd_sb])
        for c in range(NCH):
            t = t_sb[c % 2]
            gsc = gs_sb[:, kind * NCH + c:kind * NCH + c + 1]
            shc = v_sb[:, (kind * 4 + 1) * NCH + c:(kind * 4 + 1) * NCH + c + 1]
            P.op(P.dve, lambda e: e.scalar_tensor_tensor(out=t[:, :n], in0=xs[i][:, c, :], scalar=gsc, in1=rstd_sb[:, :n],
                                                         op0=ALU.mult, op1=ALU.mult), reads=[xs[i], gs_sb, rstd_sb], writes=[t])
            P.op(P.act, lambda e: e.activation(out=ys[i][:, c, :], in_=t[:, :n], func=AF.Identity, bias=shc, scale=1.0),
                 reads=[t, v_sb], writes=[ys[i]])
    ri = 0
    for fg in range(16):
        wb1 = wbufs[ti % NW]
        w1t = wb1[:].rearrange("p (k c) -> p k c", k=16)
        ab = a_sb[fg % 2]
        for i, (s, n, kind) in enumerate(blocks):
            for j in range(4):
                ps = nextps()
                for k in range(NCH):
                    P.mm(ps, ps[:, :n], wb1, w1t[:, k, j * 128:(j + 1) * 128], ys[i], ys[i][:, k, :], k == 0, k == NCH - 1)
                r = r_sb[ri % 2]
                ri += 1
                P.op(P.act, lambda e: e.activation(out=r[:, :n], in_=ps[:, :n], func=AF.Relu), reads=[ps], writes=[r])
                P.op(P.act, lambda e: e.activation(out=ab[i][:, j, :], in_=r[:, :n], func=AF.Square), reads=[r], writes=[ab[i]])
        if ti + NW < len(tiles):
            load(ti + NW)
        ti += 1
        wb2 = wbufs[ti % NW]
        w2t = wb2[:].rearrange("p (f c) -> p f c", f=4)
        for i, (s, n, kind) in enumerate(blocks):
            for dc in range(NCH):
                ps = nextps()
                for j in range(4):
                    P.mm(ps, ps[:, :n], wb2, w2t[:, j, dc * 128:(dc + 1) * 128], ab[i], ab[i][:, j, :], j == 0, j == 3)
                gm = v_sb[:, (kind * 4 + 3) * NCH + dc:(kind * 4 + 3) * NCH + dc + 1]
                P.op(P.dve, lambda e: e.scalar_tensor_tensor(out=xs[i][:, dc, :], in0=ps[:, :n], scalar=gm, in1=xs[i][:, dc, :],
                                                             op0=ALU.mult, op1=ALU.add), reads=[ps, v_sb, xs[i]], writes=[xs[i]])
        if ti + NW < len(tiles):
            load(ti + NW)
        ti += 1
    for i, (s, n, kind) in enumerate(blocks):
        for h in range(4):
            P.dma(P.sp, xov[:, h * 4:(h + 1) * 4, s:s + n], xs[i][:, h * 4:(h + 1) * 4, :], rd=xs[i])
    P.finish()
    return nc


NTOK = CTX + SEQ
NT = NTOK // 128
NBLK = NTOK // 256
NEG = -30000.0


def _stage1(P, xTv, vA, gsA, eps_sb, ones_bf, pss, nextps, w_tm, tm_groups, w_fm, fm_groups, tm_cb, fm_cb, h_store=None, h_load=None):
    P.push()
    NHB = 2 if h_load is None else 3
    hb = [P.sbuf(f"hb{i}", [128, NCH, 256], BF16) for i in range(NHB)]
    if h_load is None:
        xb = [P.sbuf(f"xb{i}", [128, NCH, 256], F32) for i in range(2)]
        sq_sb = [P.sbuf(f"sq{i}", [128, 256], BF16) for i in range(2)]
        t_sb = [P.sbuf(f"t{i}", [128, 256], F32) for i in range(2)]
        rstd_sb = P.sbuf("rstd", [128, 256], F32)

    def loadx(blk):
        x = xb[blk % 2]
        for h in range(4):
            P.dma(P.sp, x[:, h * 4:(h + 1) * 4, :], xTv[blk, :, h * 1024:(h + 1) * 1024].rearrange("p (c t) -> p c t", c=4), wr=x)

    def loadh(blk):
        hh = hb[blk % NHB]
        for h in range(2):
            P.dma(P.sp, hh[:, h * 8:(h + 1) * 8, :], h_load[blk, :, h * 2048:(h + 1) * 2048].rearrange("p (c t) -> p c t", c=8), wr=hh)

    if h_load is None:
        loadx(0)
    else:
        loadh(0)
        loadh(1)
    for blk in range(NBLK):
        kind = 1 if blk == 0 else 0
        h = hb[blk % NHB]
        if h_load is not None:
            if blk + 2 < NBLK:
                loadh(blk + 2)
        else:
            if blk + 1 < NBLK:
                loadx(blk + 1)
            x = xb[blk % 2]
            ps = nextps()
            for c in range(NCH):
                sq = sq_sb[c % 2]
                P.op(P.act, lambda e: e.activation(out=sq[:], in_=x[:, c, :], func=AF.Square), reads=[x], writes=[sq])
                P.mm(ps, ps[:, :256], ones_bf, ones_bf[:], sq, sq[:], c == 0, c == NCH - 1, inc=True)
            P.op(P.act, lambda e: e.activation(out=rstd_sb[:], in_=ps[:, :256], func=AF.Ln, bias=eps_sb[:, 0:1], scale=1.0 / D),
                 reads=[ps, eps_sb], writes=[rstd_sb])
            P.op(P.act, lambda e: e.activation(out=rstd_sb[:], in_=rstd_sb[:], func=AF.Exp, scale=-0.5), reads=[rstd_sb], writes=[rstd_sb])
            for c in range(NCH):
                t = t_sb[c % 2]
                gsc = gsA[:, kind * NCH + c:kind * NCH + c + 1]
                shc = vA[:, (kind * 2) * NCH + c:(kind * 2) * NCH + c + 1]
                P.op(P.dve, lambda e: e.scalar_tensor_tensor(out=t[:], in0=x[:, c, :], scalar=gsc, in1=rstd_sb[:],
                                                             op0=ALU.mult, op1=ALU.mult), reads=[x, gsA, rstd_sb], writes=[t])
                P.op(P.act, lambda e: e.activation(out=h[:, c, :], in_=t[:], func=AF.Identity, bias=shc, scale=1.0),
                     reads=[t, vA], writes=[h])
            if h_store is not None:
                for hh2 in range(2):
                    P.dma(P.sp, h_store[blk, :, hh2 * 2048:(hh2 + 1) * 2048].rearrange("p (c t) -> p c t", c=8),
                          h[:, hh2 * 8:(hh2 + 1) * 8, :], rd=h)
        for gi, (c0, M) in enumerate(fm_groups):
            ps = nextps()
            for k in range(NCH):
                P.mm(ps, ps[:M, :256], w_fm, w_fm[:, k, c0:c0 + M], h, h[:, k, :], k == 0, k == NCH - 1)
            fm_cb(gi, blk, ps)
        for tt in range(2):
            tile = blk * 2 + tt
            for gi, (c0, W) in enumerate(tm_groups):
                ps = nextps()
                for k in range(NCH):
                    P.mm(ps, ps[:, :W], h, h[:, k, tt * 128:(tt + 1) * 128], w_tm, w_tm[:, k, c0:c0 + W], k == 0, k == NCH - 1)
                tm_cb(gi, tile, ps)
    P.pop()


def _silu_gate(P, ps, n, dst_buf, dst_ap, tmp):
    P.op(P.act, lambda e: e.activation(out=tmp[:, :n], in_=ps[:, :n], func=AF.Exp, scale=-1.0), reads=[ps], writes=[tmp])
    P.op(P.dve, lambda e: e.tensor_scalar(out=tmp[:, :n], in0=tmp[:, :n], scalar1=1.0, scalar2=None, op0=ALU.add), reads=[tmp], writes=[tmp])
    P.op(P.dve, lambda e: e.reciprocal(out=tmp[:, :n], in_=tmp[:, :n]), reads=[tmp], writes=[tmp])
    P.op(P.dve, lambda e: e.tensor_tensor(out=dst_ap, in0=ps[:, :n], in1=tmp[:, :n], op=ALU.mult), reads=[ps, tmp], writes=[dst_buf])


def _scan(P, pss, nextps, cm, ident_bf, dk, q_st, k_st, v_st, la_st, O_st):
    P.push()
    NB = 2
    e12 = [[P.sbuf(f"e12_{d}{i}", [128, 384], F32) for i in range(NB)] for d in range(2)]
    e2n = [[P.sbuf(f"e2n_{d}{i}", [128, 256], F32) for i in range(NB)] for d in range(2)]
    e3 = [[P.sbuf(f"e3_{d}{i}", [128, 128], F32) for i in range(NB)] for d in range(2)]
    qb = [[P.sbuf(f"qb_{d}{i}", [128, 128], BF16) for i in range(NB)] for d in range(2)]
    qa = [[P.sbuf(f"qa_{d}{i}", [128, 128], BF16) for i in range(NB)] for d in range(2)]
    ka = [[P.sbuf(f"ka_{d}{i}", [128, 128], BF16) for i in range(NB)] for d in range(2)]
    qc = [[P.sbuf(f"qc_{d}{i}", [128, 128], BF16) for i in range(NB)] for d in range(2)]
    kc = [[P.sbuf(f"kc_{d}{i}", [128, 128], BF16) for i in range(NB)] for d in range(2)]
    kb = [[P.sbuf(f"kb_{d}{i}", [128, 128], BF16) for i in range(NB)] for d in range(2)]
    am1 = [[P.sbuf(f"am1_{d}{i}", [128, 128], BF16) for i in range(NB)] for d in range(2)]
    am2 = [[P.sbuf(f"am2_{d}{i}", [128, 128], BF16) for i in range(NB)] for d in range(2)]
    S32 = [P.sbuf(f"S32_{d}", [128, 128], F32) for d in range(2)]
    Sbf = [[P.sbuf(f"Sbf_{d}{i}", [128, 128], BF16) for i in range(2)] for d in range(2)]
    sidx = [0, 0]
    P.op(P.dve, lambda e: e.memset(O_st[:], 0.0), writes=[O_st])
    for d in range(2):
        P.op(P.dve, lambda e: e.memset(S32[d][:], 0.0), writes=[S32[d]])
        P.op(P.dve, lambda e: e.memset(Sbf[d][0][:], 0.0), writes=[Sbf[d][0]])
    order = [list(range(NT)), [1, 0] + list(range(NT - 1, 1, -1))]
    for step in range(NT):
        for d in range(2):
            tile = order[d][step]
            it = step % NB
            la_buf, la_ap = la_st[d](tile)
            U, Ua, Uc, Ub, M1, M2 = (cm[:, (d * 6 + i) * 128:(d * 6 + i + 1) * 128] for i in range(6))
            p12 = nextps("b")
            P.mm(p12, p12[:dk, 0:128], la_buf, la_ap, cm, U, True, True)
            P.mm(p12, p12[:dk, 128:256], la_buf, la_ap, cm, Ua, True, True)
            P.mm(p12, p12[:dk, 256:384], la_buf, la_ap, cm, Uc, True, True)
            p3 = nextps("b")
            P.mm(p3, p3[:, 0:dk], cm, Ub, la_buf, la_ap, True, True)
            pt = nextps("b")
            P.mm(pt, pt[:dk, 0:128], q_st, q_st[:, tile, :], ident_bf, ident_bf[:], True, True)
            P.mm(pt, pt[:dk, 128:256], k_st[d], k_st[d][:, tile, :], ident_bf, ident_bf[:], True, True)
            E12, E2n, E3 = e12[d][it], e2n[d][it], e3[d][it]
            P.op(P.act, lambda e: e.activation(out=E12[:dk, :], in_=p12[:dk, 0:384], func=AF.Exp), reads=[p12], writes=[E12])
            P.op(P.act, lambda e: e.activation(out=E2n[:dk, :], in_=p12[:dk, 128:384], func=AF.Exp, scale=-1.0), reads=[p12], writes=[E2n])
            P.op(P.act, lambda e: e.activation(out=E3[:, :dk], in_=p3[:, :dk], func=AF.Exp), reads=[p3], writes=[E3])
            QB, QA, KA, QC, KC, KB = qb[d][it], qa[d][it], ka[d][it], qc[d][it], kc[d][it], kb[d][it]
            AM1, AM2 = am1[d][it], am2[d][it]
            P.op(P.dve, lambda e: e.tensor_tensor(out=QB[:dk, :], in0=pt[:dk, 0:128], in1=E12[:dk, 0:128], op=ALU.mult),
                 reads=[pt, E12], writes=[QB])
            P.op(P.dve, lambda e: e.tensor_tensor(out=QA[:dk, :], in0=pt[:dk, 0:128], in1=E12[:dk, 128:256], op=ALU.mult),
                 reads=[pt, E12], writes=[QA])
            P.op(P.dve, lambda e: e.tensor_tensor(out=KA[:dk, :], in0=pt[:dk, 128:256], in1=E2n[:dk, 0:128], op=ALU.mult),
                 reads=[pt, E2n], writes=[KA])
            P.op(P.dve, lambda e: e.scalar_tensor_tensor(out=QC[:dk, :], in0=E12[:dk, 256:384], scalar=1.0, in1=pt[:dk, 0:128],
                                                         op0=ALU.min, op1=ALU.mult), reads=[pt, E12], writes=[QC])
            P.op(P.dve, lambda e: e.scalar_tensor_tensor(out=KC[:dk, :], in0=E2n[:dk, 128:256], scalar=1.0, in1=pt[:dk, 128:256],
                                                         op0=ALU.min, op1=ALU.mult), reads=[pt, E2n], writes=[KC])
            P.op(P.dve, lambda e: e.tensor_tensor(out=KB[:, :dk], in0=k_st[d][:, tile, :], in1=E3[:, :dk], op=ALU.mult),
                 reads=[k_st[d], E3], writes=[KB])
            pa = nextps("b")
            P.mm(pa, pa[:, 0:128], KA, KA[:dk, :], QA, QA[:dk, :], True, True)
            P.mm(pa, pa[:, 128:256], KC, KC[:dk, :], QC, QC[:dk, :], True, True)
            P.op(P.dve, lambda e: e.tensor_tensor(out=AM1[:], in0=pa[:, 0:128], in1=M1, op=ALU.mult), reads=[pa, cm], writes=[AM1])
            P.op(P.dve, lambda e: e.tensor_tensor(out=AM2[:], in0=pa[:, 128:256], in1=M2, op=ALU.mult), reads=[pa, cm], writes=[AM2])
            po = nextps("a")
            corder = [0, 1] if d == 0 else [1, 0]
            for c in corder:
                cs = slice(c * 64, (c + 1) * 64)
                Sb = Sbf[d][sidx[d] % 2]
                P.mm(po, po[:, cs], v_st, v_st[:, tile, :], AM1, AM1[:, cs], True, False)
                P.mm(po, po[:, cs], v_st, v_st[:, tile, :], AM2, AM2[:, cs], False, False)
                P.mm(po, po[:, cs], Sb, Sb[:dk, :], QB, QB[:dk, cs], False, True)
                pk = nextps("a")
                P.mm(pk, pk[:dk, 0:128], KB, KB[cs, :dk], v_st, v_st[cs, tile, :], True, True)
                ecol = (c * 64 + 63) if d == 0 else (c * 64)
                P.op(P.dve, lambda e: e.scalar_tensor_tensor(out=S32[d][:dk, :], in0=S32[d][:dk, :], scalar=E12[:dk, ecol:ecol + 1],
                                                             in1=pk[:dk, 0:128], op0=ALU.mult, op1=ALU.add),
                     reads=[S32[d], E12, pk], writes=[S32[d]])
                sidx[d] += 1
                Sn = Sbf[d][sidx[d] % 2]
                P.op(P.act, lambda e: e.activation(out=Sn[:dk, :], in_=S32[d][:dk, :], func=AF.Identity), reads=[S32[d]], writes=[Sn])
            ts = slice(tile * 128, (tile + 1) * 128)
            P.op(P.dve, lambda e: e.tensor_tensor(out=O_st[:, ts], in0=po[:, 0:128], in1=O_st[:, ts], op=ALU.add),
                 reads=[po, O_st], writes=[O_st])
    P.pop()


def _readout(P, nextps, O_st, gate_st, gn_ap, gn_buf, eps_sb, ones_bf, yTv_rows):
    P.push()
    sq = [P.sbuf(f"rsq{i}", [128, 512], BF16) for i in range(2)]
    rs = [P.sbuf(f"rrs{i}", [128, 512], F32) for i in range(2)]
    yo = [P.sbuf(f"ryo{i}", [128, 512], F32) for i in range(2)]
    for bi, s in enumerate(range(0, NTOK, 512)):
        n = min(512, NTOK - s)
        SQ, RS, YO = sq[bi % 2], rs[bi % 2], yo[bi % 2]
        P.op(P.act, lambda e: e.activation(out=SQ[:, :n], in_=O_st[:, s:s + n], func=AF.Square), reads=[O_st], writes=[SQ])
        ps = nextps()
        P.mm(ps, ps[:, :n], ones_bf, ones_bf[:], SQ, SQ[:, :n], True, True)
        P.op(P.act, lambda e: e.activation(out=RS[:, :n], in_=ps[:, :n], func=AF.Ln, bias=eps_sb[:, 0:1], scale=1.0 / 128),
             reads=[ps, eps_sb], writes=[RS])
        P.op(P.act, lambda e: e.activation(out=RS[:, :n], in_=RS[:, :n], func=AF.Exp, scale=-0.5), reads=[RS], writes=[RS])
        P.op(P.dve, lambda e: e.scalar_tensor_tensor(out=YO[:, :n], in0=O_st[:, s:s + n], scalar=gn_ap, in1=RS[:, :n],
                                                     op0=ALU.mult, op1=ALU.mult), reads=[O_st, gn_buf, RS], writes=[YO])
        P.op(P.dve, lambda e: e.tensor_tensor(out=YO[:, :n], in0=YO[:, :n], in1=gate_st[:, s:s + n], op=ALU.mult),
             reads=[YO, gate_st], writes=[YO])
        P.dma(P.sp, yTv_rows[:, s:s + n], YO[:, :n], rd=YO)
    P.pop()


def na_tile_plan():
    plan = {}
    keys = {}
    for n in range(8):
        for m in range(32):
            rows_q = np.arange(8 * n, 8 * n + 8)
            r0 = np.clip(rows_q - 4, 0, 56)
            ok = False
            for kr in (2 * m, 2 * m + 1):
                if np.any((r0 <= kr) & (kr < r0 + 8)):
                    ok = True
            if not ok:
                continue
            key = ("top", m) if n == 0 else (("bot", m) if n == 7 else ("int", m - 4 * n))
            if key not in keys:
                keys[key] = (len(keys), n, m)
            plan[(n, m)] = keys[key][0]
    reps = sorted(keys.values())
    return plan, [(n, m) for (_, n, m) in reps]


def build_A():
    P = Prog()
    nc = P.nc
    plan, reps = na_tile_plan()
    NBT = len(reps)
    xT = P.din("xT", [NBLK, 128, NCH * 256])
    hscr = nc.dram_tensor("hscr", [NBLK, 128, NCH * 256], BF16).ap()
    vecsA = P.din("vecsA", [128, 5 * NCH])
    pvec = P.din("pvec", [128, 4])
    w_na_tm = P.din("w_na_tm", [D, 256])
    w_na_fm = P.din("w_na_fm", [D, 512])
    w_gla_tm = P.din("w_gla_tm", [D, 384])
    w_gla_fm = P.din("w_gla_fm", [D, 160])
    w_hg_tm = P.din("w_hg_tm", [D, 512])
    w_hg_fm = P.din("w_hg_fm", [D, 128])
    wa2 = P.din("wa2", [128, 128])
    ba = P.din("ba", [128, 128])
    lbraw = P.din("lbraw", [128, 512])
    lflag = P.din("lflag", [128, 1])
    rope = P.din("rope", [128, 32 * 128])
    cmats = P.din("cmats", [128, 12 * 128])
    identd = P.din("ident", [128, 128])
    nabias = P.din("nabias", [128, 2 * NBT * 512])
    yT = P.dout("yT", [512, NTOK])
    xTv = xT

    vA = P.sbuf("vA", [128, 5 * NCH], F32)
    gsA = P.sbuf("gsA", [128, 2 * NCH], F32)
    pv = P.sbuf("pv", [128, 4], F32)
    pvs = P.sbuf("pvs", [128, 1], F32)
    eps_sb = P.sbuf("eps_sb", [128, 1], F32)
    ones_bf = P.sbuf("ones_bf", [128, 128], BF16)
    ones_f = P.sbuf("ones_f", [128, 128], F32)
    ident_bf = P.sbuf("ident_bf", [128, 128], BF16)
    cm = P.sbuf("cm", [128, 12 * 128], F32)
    pss = [P.psum(f"ps{i}", [128, 512], F32) for i in range(8)]
    psi = [0]

    psa = [0]
    psb = [0]

    def nextps(pool=None):
        if pool == "a":
            psa[0] += 1
            return pss[psa[0] % 4]
        if pool == "b":
            psb[0] += 1
            return pss[4 + psb[0] % 4]
        psi[0] += 1
        return pss[psi[0] % 8]

    P.dma(P.sp, vA[:], vecsA[:, :], wr=vA)
    P.dma(P.sp, pv[:], pvec[:, :], wr=pv)
    P.dma(P.sp, cm[:], cmats[:, :], wr=cm)
    P.dma(P.pool, ident_bf[:], identd[:, :], wr=ident_bf)
    P.op(P.dve, lambda e: e.memset(eps_sb[:], EPS), writes=[eps_sb])
    P.op(P.dve, lambda e: e.memset(ones_bf[:], 1.0), writes=[ones_bf])
    P.op(P.dve, lambda e: e.memset(ones_f[:], 1.0), writes=[ones_f])
    for kind in range(2):
        P.op(P.dve, lambda e: e.scalar_tensor_tensor(out=gsA[:, kind * NCH:(kind + 1) * NCH],
                                                     in0=vA[:, (kind * 2 + 1) * NCH:(kind * 2 + 2) * NCH], scalar=1.0,
                                                     in1=vA[:, 4 * NCH:5 * NCH], op0=ALU.add, op1=ALU.mult), reads=[vA], writes=[gsA])
    P.op(P.dve, lambda e: e.tensor_scalar(out=pvs[:], in0=pv[:, 2:3], scalar1=float(128 ** -0.5), scalar2=None, op0=ALU.mult),
         reads=[pv], writes=[pvs])

    def load_w(buf, src, ncols):
        v = src.rearrange("(k p) c -> p k c", p=128)
        for h in range(4):
            P.dma(P.pool, buf[:, h * 4:(h + 1) * 4, :], v[:, h * 4:(h + 1) * 4, :], wr=buf)

    def na_phase():
        P.push()
        w_tm = P.sbuf("wna_tm", [128, NCH, 256], BF16)
        w_fm = P.sbuf("wna_fm", [128, NCH, 512], BF16)
        load_w(w_tm, w_na_tm, 256)
        load_w(w_fm, w_na_fm, 512)
        qT = P.sbuf("na_qT", [128, 2, NTOK], BF16)
        kT = P.sbuf("na_kT", [128, 2, NTOK], BF16)
        v_na = P.sbuf("na_v", [128, NT, 256], BF16)
        bias_sb = P.sbuf("na_bias", [128, 2 * NBT, 512], BF16)
        nbv = nabias.rearrange("p (t q) -> p t q", q=512)
        for t0 in range(0, 2 * NBT, 8):
            t1 = min(2 * NBT, t0 + 8)
            P.dma(P.pool, bias_sb[:, t0:t1, :], nbv[:, t0:t1, :], wr=bias_sb)
        tq = [P.sbuf(f"na_tq{i}", [128, 256], F32) for i in range(2)]
        sqn = [P.sbuf(f"na_sq{i}", [128, 256], BF16) for i in range(2)]
        rsn = [P.sbuf(f"na_rs{i}", [128, 256], F32) for i in range(2)]
        cnt = [0]

        def na_fm_cb(gi, blk, ps):
            i = cnt[0] % 2
            cnt[0] += 1
            TQ, SQ, RS = tq[i], sqn[i], rsn[i]
            P.op(P.act, lambda e: e.activation(out=TQ[:], in_=ps[:, :256], func=AF.Identity), reads=[ps], writes=[TQ])
            P.op(P.act, lambda e: e.activation(out=SQ[:], in_=TQ[:], func=AF.Square), reads=[TQ], writes=[SQ])
            p2 = nextps()
            P.mm(p2, p2[:, :256], ones_bf, ones_bf[:], SQ, SQ[:], True, True)
            P.op(P.act, lambda e: e.activation(out=RS[:], in_=p2[:, :256], func=AF.Ln, bias=eps_sb[:, 0:1], scale=1.0 / 128),
                 reads=[p2, eps_sb], writes=[RS])
            P.op(P.act, lambda e: e.activation(out=RS[:], in_=RS[:], func=AF.Exp, scale=-0.5), reads=[RS], writes=[RS])
            dst = qT if gi < 2 else kT
            gsc = pvs[:, 0:1] if gi < 2 else pv[:, 3:4]
            P.op(P.dve, lambda e: e.scalar_tensor_tensor(out=dst[:, gi % 2, blk * 256:(blk + 1) * 256], in0=TQ[:], scalar=gsc, in1=RS[:],
                                                         op0=ALU.mult, op1=ALU.mult), reads=[TQ, pvs, pv, RS], writes=[dst])

        def na_tm_cb(gi, tile, ps):
            P.op(P.act, lambda e: e.activation(out=v_na[:, tile, :], in_=ps[:, :256], func=AF.Identity), reads=[ps], writes=[v_na])

        _stage1(P, xTv, vA, gsA, eps_sb, ones_bf, pss, nextps, w_tm, [(0, 256)], w_fm, [(0, 128), (128, 128), (256, 128), (384, 128)],
                na_tm_cb, na_fm_cb, h_store=hscr)
        if DBG.get("stop") == 1:
            P.pop()
            return
        pT = [P.sbuf(f"na_pT{i}", [128, 512], BF16) for i in range(3)]
        rsum = [P.sbuf(f"na_rsum{i}", [128, 512], F32) for i in range(2)]
        yo = [P.sbuf(f"na_yo{i}", [128, 512], F32) for i in range(2)]
        pi = 0
        qi = 0
        for hh in range(2):
            qblocks = [(0, 256, None)] + [(CTX + 512 * n, 512, n) for n in range(8)]
            for (q0, nq, n) in qblocks:
                ktiles = [(0, None), (1, None)]
                if n is not None:
                    ktiles += [(2 + m, hh * NBT + plan[(n, m)]) for m in range(32) if (n, m) in plan]
                po = nextps("a")
                psm = nextps("a")
                for ki, (kt, bt) in enumerate(ktiles):
                    pscore = nextps("b")
                    P.mm(pscore, pscore[:, :nq], kT, kT[:, hh, kt * 128:(kt + 1) * 128], qT, qT[:, hh, q0:q0 + nq], True, bt is None)
                    if bt is not None:
                        P.mm(pscore, pscore[:, :nq], ident_bf, ident_bf[:], bias_sb, bias_sb[:, bt, :nq], False, True)
                    PT = pT[pi % 3]
                    pi += 1
                    P.op(P.act, lambda e: e.activation(out=PT[:, :nq], in_=pscore[:, :nq], func=AF.Exp), reads=[pscore], writes=[PT])
                    last = ki == len(ktiles) - 1
                    P.mm(po, po[:, :nq], v_na, v_na[:, kt, hh * 128:(hh + 1) * 128], PT, PT[:, :nq], ki == 0, last, inc=True)
                    P.mm(psm, psm[:, :nq], ones_bf, ones_bf[:], PT, PT[:, :nq], ki == 0, last, inc=True)
                RSM, YO = rsum[qi % 2], yo[qi % 2]
                qi += 1
                P.op(P.dve, lambda e: e.reciprocal(out=RSM[:, :nq], in_=psm[:, :nq]), reads=[psm], writes=[RSM])
                P.op(P.dve, lambda e: e.tensor_tensor(out=YO[:, :nq], in0=po[:, :nq], in1=RSM[:, :nq], op=ALU.mult), reads=[po, RSM], writes=[YO])
                P.dma(P.sp, yT[256 + hh * 128:256 + (hh + 1) * 128, q0:q0 + nq], YO[:, :nq], rd=YO)
        P.pop()


    if not DBG.get("skip_na"):
        na_phase()
    if DBG.get("stop") == 1:
        P.finish()
        return nc
    if DBG.get("stop") == 2:
        P.finish()
        return nc
    P.push()
    w_tm = P.sbuf("wgla_tm", [128, NCH, 384], BF16)
    w_fm = P.sbuf("wgla_fm", [128, NCH, 160], BF16)
    load_w(w_tm, w_gla_tm, 384)
    load_w(w_fm, w_gla_fm, 160)
    rope_sb = P.sbuf("rope_sb", [128, 32, 128], F32)
    ropev = rope.rearrange("p (t c) -> p t c", c=128)
    for h in range(8):
        P.dma(P.sp, rope_sb[:, h * 4:(h + 1) * 4, :], ropev[:, h * 4:(h + 1) * 4, :], wr=rope_sb)
    wa2_sb = P.sbuf("wa2_sb", [128, 128], F32)
    ba_sb = P.sbuf("ba_sb", [128, 128], F32)
    P.dma(P.sp, wa2_sb[:], wa2[:, :], wr=wa2_sb)
    P.dma(P.sp, ba_sb[:], ba[:, :], wr=ba_sb)
    q_g = P.sbuf("gla_q", [128, NT, 64], BF16)
    k_g = P.sbuf("gla_k", [128, NT, 64], BF16)
    v_g = P.sbuf("gla_v", [128, NT, 128], BF16)
    la_g = P.sbuf("gla_la", [128, NT, 128], F32)
    gate_g = P.sbuf("gla_gate", [128, NTOK], BF16)
    al_g = [P.sbuf(f"gla_al{i}", [128, 256], F32) for i in range(2)]
    tmpg = [P.sbuf(f"gla_tmp{i}", [128, 256], F32) for i in range(2)]
    rt = [P.sbuf(f"gla_rt{i}", [128, 256], F32) for i in range(2)]
    cg = [0, 0]

    def gla_fm_cb(gi, blk, ps):
        if gi == 0:
            if DBG.get("skip_gate"):
                return
            T = tmpg[cg[0] % 2]
            cg[0] += 1
            _silu_gate(P, ps, 256, gate_g, gate_g[:, blk * 256:(blk + 1) * 256], T)
        elif not DBG.get("skip_la"):
            AL = al_g[blk % 2]
            P.op(P.act, lambda e: e.activation(out=AL[:, :], in_=ps[:, :256], func=AF.Identity), reads=[ps], writes=[AL])
            for tt in range(2):
                tile = blk * 2 + tt
                p2 = nextps()
                P.mm(p2, p2[:, :128], AL, AL[:, tt * 128:(tt + 1) * 128], wa2_sb, wa2_sb[:, :], True, True)
                T = tmpg[cg[0] % 2]
                cg[0] += 1
                P.op(P.dve, lambda e: e.tensor_tensor(out=T[:, :128], in0=p2[:, :128], in1=ba_sb[:, :], op=ALU.add), reads=[p2, ba_sb], writes=[T])
                P.op(P.act, lambda e: e.activation(out=T[:, :128], in_=T[:, :128], func=AF.Exp, scale=-1.0), reads=[T], writes=[T])
                P.op(P.act, lambda e: e.activation(out=T[:, :128], in_=T[:, :128], func=AF.Ln, bias=ones_f[:, 0:1], scale=1.0),
                     reads=[T, ones_f], writes=[T])
                P.op(P.dve, lambda e: e.tensor_scalar(out=la_g[:, tile, :], in0=T[:, :128], scalar1=-1.0 / 16.0, scalar2=None, op0=ALU.mult),
                     reads=[T], writes=[la_g])

    def gla_tm_cb(gi, tile, ps):
        if DBG.get("skip_tm"):
            return
        if tile < 2 or DBG.get("skip_rope"):
            P.op(P.act, lambda e: e.activation(out=q_g[:, tile, :], in_=ps[:, 0:64], func=AF.Identity, scale=0.125), reads=[ps], writes=[q_g])
            P.op(P.act, lambda e: e.activation(out=k_g[:, tile, :], in_=ps[:, 128:192], func=AF.Identity), reads=[ps], writes=[k_g])
        else:
            lt = tile - 2
            R = rt[cg[1] % 2]
            cg[1] += 1
            P.op(P.act, lambda e: e.activation(out=R[:, 0:128], in_=ps[:, 0:128], func=AF.Identity, scale=0.125), reads=[ps], writes=[R])
            P.op(P.act, lambda e: e.activation(out=R[:, 128:256], in_=ps[:, 128:256], func=AF.Identity), reads=[ps], writes=[R])
            for hh2 in range(2):
                P.op(P.dve, lambda e: e.tensor_tensor(out=R[:, hh2 * 128:(hh2 + 1) * 128], in0=R[:, hh2 * 128:(hh2 + 1) * 128],
                                                      in1=rope_sb[:, lt, :], op=ALU.mult), reads=[R, rope_sb], writes=[R])
            P.op(P.dve, lambda e: e.tensor_tensor(out=q_g[:, tile, :], in0=R[:, 0:64], in1=R[:, 64:128], op=ALU.add), reads=[R], writes=[q_g])
            P.op(P.dve, lambda e: e.tensor_tensor(out=k_g[:, tile, :], in0=R[:, 128:192], in1=R[:, 192:256], op=ALU.add), reads=[R], writes=[k_g])
        P.op(P.act, lambda e: e.activation(out=v_g[:, tile, :], in_=ps[:, 256:384], func=AF.Identity), reads=[ps], writes=[v_g])

    _stage1(P, xTv, vA, gsA, eps_sb, ones_bf, pss, nextps, w_tm, [(0, 384)], w_fm,
            [(32, 128)] if DBG.get("skip_alow") else [(32, 128), (0, 128)], gla_tm_cb, gla_fm_cb, h_load=hscr)
    if DBG.get("stop") == 3:
        P.pop()
        P.finish()
        return nc
    O_g = P.sbuf("gla_O", [128, NTOK], F32)
    _scan(P, pss, nextps, cm, ident_bf, 64, q_g, [k_g, k_g], v_g,
          [lambda tile: (la_g, la_g[:, tile, 0:64]), lambda tile: (la_g, la_g[:, tile, 64:128])], O_g)
    if DBG.get("stop") == 4:
        P.pop()
        P.finish()
        return nc
    _readout(P, nextps, O_g, gate_g, pv[:, 0:1], pv, eps_sb, ones_bf, yT[0:128, :])
    P.pop()
    if DBG.get("stop") == 5:
        P.finish()
        return nc

    P.push()
    w_tm = P.sbuf("whg_tm", [128, NCH, 512], BF16)
    w_fm = P.sbuf("whg_fm", [128, NCH, 128], BF16)
    load_w(w_tm, w_hg_tm, 512)
    load_w(w_fm, w_hg_fm, 128)
    lb_r = P.sbuf("lb_r", [128, 512], F32)
    lf_sb = P.sbuf("lf_sb", [128, 1], F32)
    lb_t = P.sbuf("lb_t", [128, 256], F32)
    oml_t = P.sbuf("oml_t", [128, 256], F32)
    P.dma(P.sp, lb_r[:], lbraw[:, :], wr=lb_r)
    P.dma(P.sp, lf_sb[:], lflag[:, :], wr=lf_sb)
    P.op(P.act, lambda e: e.activation(out=lb_r[:], in_=lb_r[:], func=AF.Exp), reads=[lb_r], writes=[lb_r])
    P.op(P.dve, lambda e: e.tensor_tensor(out=lb_t[:], in0=lb_r[:, 0:256], in1=lb_r[:, 256:512], op=ALU.add), reads=[lb_r], writes=[lb_t])
    P.op(P.dve, lambda e: e.reciprocal(out=lb_t[:], in_=lb_t[:]), reads=[lb_t], writes=[lb_t])
    P.op(P.dve, lambda e: e.tensor_tensor(out=lb_t[:], in0=lb_t[:], in1=lb_r[:, 256:512], op=ALU.mult), reads=[lb_t, lb_r], writes=[lb_t])
    P.op(P.dve, lambda e: e.tensor_scalar(out=lb_t[:], in0=lb_t[:], scalar1=lf_sb[:, 0:1], scalar2=None, op0=ALU.mult),
         reads=[lb_t, lf_sb], writes=[lb_t])
    P.op(P.dve, lambda e: e.tensor_scalar(out=oml_t[:], in0=lb_t[:], scalar1=-1.0, scalar2=1.0, op0=ALU.mult, op1=ALU.add),
         reads=[lb_t], writes=[oml_t])
    q_h = P.sbuf("hg_q", [128, NT, 128], BF16)
    k_h = [P.sbuf(f"hg_k{d}", [128, NT, 128], BF16) for d in range(2)]
    v_h = P.sbuf("hg_v", [128, NT, 128], BF16)
    la_h = P.sbuf("hg_la", [128, NT, 256], F32)
    gate_h = P.sbuf("hg_gate", [128, NTOK], BF16)
    tmph = [P.sbuf(f"hg_tmp{i}", [128, 256], F32) for i in range(2)]
    eh = [P.sbuf(f"hg_e{i}", [128, 384], F32) for i in range(2)]
    ch = [0, 0]

    def hg_fm_cb(gi, blk, ps):
        T = tmph[ch[0] % 2]
        ch[0] += 1
        _silu_gate(P, ps, 256, gate_h, gate_h[:, blk * 256:(blk + 1) * 256], T)

    def hg_tm_cb(gi, tile, ps):
        E = eh[ch[1] % 2]
        ch[1] += 1
        P.op(P.act, lambda e: e.activation(out=v_h[:, tile, :], in_=ps[:, 384:512], func=AF.Identity), reads=[ps], writes=[v_h])
        P.op(P.act, lambda e: e.activation(out=E[:, 0:384], in_=ps[:, 0:384], func=AF.Exp, scale=-1.0), reads=[ps], writes=[E])
        P.op(P.dve, lambda e: e.tensor_scalar(out=E[:, 0:384], in0=E[:, 0:384], scalar1=1.0, scalar2=None, op0=ALU.add), reads=[E], writes=[E])
        P.op(P.dve, lambda e: e.reciprocal(out=E[:, 0:384], in_=E[:, 0:384]), reads=[E], writes=[E])
        P.op(P.dve, lambda e: e.tensor_tensor(out=q_h[:, tile, :], in0=ps[:, 0:128], in1=E[:, 0:128], op=ALU.mult), reads=[ps, E], writes=[q_h])
        P.op(P.dve, lambda e: e.tensor_tensor(out=E[:, 128:384], in0=E[:, 128:384], in1=oml_t[:], op=ALU.mult), reads=[E, oml_t], writes=[E])
        P.op(P.dve, lambda e: e.tensor_tensor(out=E[:, 128:384], in0=E[:, 128:384], in1=lb_t[:], op=ALU.add), reads=[E, lb_t], writes=[E])
        P.op(P.act, lambda e: e.activation(out=la_h[:, tile, :], in_=E[:, 128:384], func=AF.Ln), reads=[E], writes=[la_h])
        for d in range(2):
            P.op(P.dve, lambda e: e.tensor_scalar(out=k_h[d][:, tile, :], in0=E[:, 128 + d * 128:256 + d * 128], scalar1=-1.0, scalar2=1.0,
                                                  op0=ALU.mult, op1=ALU.add), reads=[E], writes=[k_h[d]])

    _stage1(P, xTv, vA, gsA, eps_sb, ones_bf, pss, nextps, w_tm, [(0, 512)], w_fm, [(0, 128)], hg_tm_cb, hg_fm_cb, h_load=hscr)
    O_h = P.sbuf("hg_O", [128, NTOK], F32)
    _scan(P, pss, nextps, cm, ident_bf, 128, q_h, k_h, v_h,
          [lambda tile: (la_h, la_h[:, tile, 0:128]), lambda tile: (la_h, la_h[:, tile, 128:256])], O_h)
    _readout(P, nextps, O_h, gate_h, pv[:, 1:2], pv, eps_sb, ones_bf, yT[128:256, :])
    P.pop()
    P.finish()
    return nc


GLA_Q0, GLA_K0, GLA_V0, GLA_R0, GLA_A0 = 0, 256, 512, 1024, 1536
HG_Q0, HG_F0, HG_I0, HG_G0 = 1568, 2080, 3104, 3616
NA_Q0, NA_K0, NA_V0 = 4128, 5152, 6176


def _pp(v):
    return np.ascontiguousarray(np.asarray(v, np.float32).reshape(NCH, 128).T)


def _consts():
    j = np.arange(128)[:, None]
    i = np.arange(128)[None, :]
    same = (j // 64) == (i // 64)
    base = (i // 64) * 64
    half = (i % 64) // 32
    jhalf = (j % 64) // 32
    mats = []
    f = lambda m: m.astype(np.float32)
    U = f(same & (j <= i))
    Ua = U - f(same & (j <= base + 32 * half + 15))
    Uc = U - f(same & (j <= base + 31))
    Ub = f(same & (j > i))
    M1 = f(same & (jhalf == half) & (j <= i))
    M2 = f(same & (jhalf == 0) & (half == 1))
    mats += [U, Ua, Uc, Ub, M1, M2]
    Ur = f(same & (j >= i))
    Uar = Ur - f(same & (j >= base + 32 * half + 16))
    Ucr = Ur - f(same & (j >= base + 32))
    Ubr = f(same & (j < i))
    M1r = f(same & (jhalf == half) & (j >= i))
    M2r = f(same & (jhalf == 1) & (half == 0))
    mats += [Ur, Uar, Ucr, Ubr, M1r, M2r]
    cm = np.concatenate(mats, axis=1).astype(np.float32)
    ident = np.eye(128, dtype=np.float32)
    pos = np.arange(SEQ)
    rowp, colp = pos // 64, pos % 64
    inv = (10000.0 ** (-np.arange(16, dtype=np.float32) / 16)).astype(np.float32)
    cos = np.zeros((SEQ, 64), np.float32)
    sin = np.zeros((SEQ, 64), np.float32)
    for half, pp in enumerate((rowp, colp)):
        ang = pp.astype(np.float32)[:, None] * inv[None, :]
        c_, s_ = np.cos(ang).astype(np.float32), np.sin(ang).astype(np.float32)
        b0 = half * 32
        cos[:, b0:b0 + 16] = c_
        cos[:, b0 + 16:b0 + 32] = c_
        sin[:, b0:b0 + 16] = -s_
        sin[:, b0 + 16:b0 + 32] = s_
    tab = np.concatenate([cos, sin], axis=1).reshape(32, 128, 128).transpose(1, 0, 2).reshape(128, 32 * 128)
    return cm, ident, np.ascontiguousarray(tab)


def _swap_idx():
    idx = np.arange(64)
    out = idx.copy()
    for b0 in (0, 32):
        out[b0:b0 + 16] = idx[b0 + 16:b0 + 32]
        out[b0 + 16:b0 + 32] = idx[b0:b0 + 16]
    return out


def _na_bias_tiles(rpb_h, reps):
    out = np.full((128, len(reps), 512), NEG, np.float32)
    kl = np.arange(128)
    ql = np.arange(512)
    for ti, (n, m) in enumerate(reps):
        kr = (2 * m + kl // 64)[:, None]
        kc = (kl % 64)[:, None]
        r = (8 * n + ql // 64)[None, :]
        c = (ql % 64)[None, :]
        r0 = np.clip(r - 4, 0, 56)
        c0 = np.clip(c - 8, 0, 48)
        valid = (kr >= r0) & (kr < r0 + 8) & (kc >= c0) & (kc < c0 + 16)
        dr = np.clip(kr - r + 7, 0, 14)
        dc = np.clip(kc - c + 15, 0, 30)
        vals = rpb_h[dr, dc]
        out[:, ti, :] = np.where(valid, vals, np.float32(NEG))
    return out


_CACHE = {}
DBG = {}


def _get(name, fn):
    if name not in _CACHE:
        _CACHE[name] = fn()
    return _CACHE[name]


def kernel(x, c, ctx, c_ctx, w_mod, b_mod, attn_norm, w_in, gla_w_a2, gla_b_a, gla_norm, hg_lower_bounds, hg_norm,
           na_q_norm, na_k_norm, na_rpb, w_out, mlp_norm, w_mlp1, w_mlp2):
    f = lambda a: np.asarray(a, dtype=np.float32)
    x, c, ctx, c_ctx, w_mod, b_mod, attn_norm, w_in = map(f, (x, c, ctx, c_ctx, w_mod, b_mod, attn_norm, w_in))
    gla_w_a2, gla_b_a, gla_norm, hg_lower_bounds, hg_norm = map(f, (gla_w_a2, gla_b_a, gla_norm, hg_lower_bounds, hg_norm))
    na_q_norm, na_k_norm, na_rpb, w_out, mlp_norm, w_mlp1, w_mlp2 = map(f, (na_q_norm, na_k_norm, na_rpb, w_out, mlp_norm, w_mlp1, w_mlp2))
    cores = list(range(8))
    ncM = _get("M", build_M)
    cvec = np.stack([c[0], c[1], c_ctx], axis=1)
    cT = np.ascontiguousarray(cvec.reshape(NCH, 128, 3).transpose(1, 0, 2).reshape(128, NCH * 3))
    wcat = np.concatenate([w_mod[0], w_mod[1]], axis=1)
    bcat = np.concatenate([b_mod[0], b_mod[1]], axis=0)
    ims = []
    for cid in cores:
        sl = slice(cid * 3072, (cid + 1) * 3072)
        ims.append({"cT": cT, "wm": np.ascontiguousarray(wcat[:, sl]), "bm": np.ascontiguousarray(bcat[sl].reshape(24, 128).T)})
    res = run_bass_kernel_spmd(ncM, ims, core_ids=cores)
    mod = np.concatenate([r["om"].reshape(128, 24, 3).transpose(1, 0, 2).reshape(3072, 3) for r in res.results], axis=0)
    mod = mod.reshape(2, 6, D, 3)
    DBG["mod"] = mod
    if DBG.get("onlyM"):
        return None

    cm, ident, ropetab = _consts()
    plan, reps = na_tile_plan()
    sw = _swap_idx()
    ncA = _get("A", build_A)
    blocksB = [(0, 512, 0), (512, 512, 0), (1024, 64, 1)]
    ncB = _get("B", lambda: build_B(blocksB))
    xs = [x[0], x[1]]
    cs = [ctx[0], ctx[1]]
    for l in range(2):
        ims = []
        for cid in cores:
            b, j = cid // 4, cid % 4
            W = w_in[l]
            g64 = lambda o: np.arange(o + j * 64, o + (j + 1) * 64)
            g128 = lambda o, h=None: np.arange(o + (j if h is None else h) * 128, o + ((j if h is None else h) + 1) * 128)
            gq, gk = g64(GLA_Q0), g64(GLA_K0)
            cols_gla_tm = np.concatenate([gq, gq[sw], gk, gk[sw], g128(GLA_V0)])
            cols_gla_fm = np.concatenate([np.arange(GLA_A0, GLA_A0 + 32), g128(GLA_R0)])
            cols_hg_tm = np.concatenate([g128(HG_Q0), g128(HG_F0), g128(HG_F0 + 512), g128(HG_I0)])
            cols_hg_fm = g128(HG_G0)
            h0, h1 = 2 * j, 2 * j + 1
            cols_na_tm = np.concatenate([g128(NA_V0, h0), g128(NA_V0, h1)])
            cols_na_fm = np.concatenate([g128(NA_Q0, h0), g128(NA_Q0, h1), g128(NA_K0, h0), g128(NA_K0, h1)])
            wa2 = np.zeros((128, 128), np.float32)
            wa2[0:16, 0:64] = gla_w_a2[l, 0][:, j * 64:(j + 1) * 64]
            wa2[16:32, 64:128] = gla_w_a2[l, 1][:, j * 64:(j + 1) * 64]
            ba = np.concatenate([gla_b_a[l, 0][j * 64:(j + 1) * 64], gla_b_a[l, 1][j * 64:(j + 1) * 64]])[None, :]
            hc = slice(j * 128, (j + 1) * 128)
            lbraw = np.concatenate([hg_lower_bounds[0, 0][hc], hg_lower_bounds[0, 1][hc],
                                    hg_lower_bounds[1, 0][hc], hg_lower_bounds[1, 1][hc]])[None, :]
            nab = np.concatenate([_na_bias_tiles(na_rpb[l, h0], reps), _na_bias_tiles(na_rpb[l, h1], reps)], axis=1)
            vecsA = np.concatenate([_pp(mod[l, 0, :, b]), _pp(mod[l, 1, :, b]), _pp(mod[l, 0, :, 2]), _pp(mod[l, 1, :, 2]),
                                    _pp(attn_norm[l])], axis=1)
            pvec = np.stack([gla_norm[l], hg_norm[l], na_q_norm[l], na_k_norm[l]], axis=1)
            ims.append({
                "xT": np.ascontiguousarray(np.concatenate([cs[b].T, xs[b].T], axis=1).reshape(NCH, 128, NBLK, 256).transpose(2, 1, 0, 3)).reshape(NBLK, 128, NCH * 256),
                "vecsA": np.ascontiguousarray(vecsA), "pvec": np.ascontiguousarray(pvec),
                "w_na_tm": np.ascontiguousarray(W[:, cols_na_tm]), "w_na_fm": np.ascontiguousarray(W[:, cols_na_fm]),
                "w_gla_tm": np.ascontiguousarray(W[:, cols_gla_tm]), "w_gla_fm": np.ascontiguousarray(W[:, cols_gla_fm]),
                "w_hg_tm": np.ascontiguousarray(W[:, cols_hg_tm]), "w_hg_fm": np.ascontiguousarray(W[:, cols_hg_fm]),
                "wa2": wa2, "ba": np.ascontiguousarray(np.repeat(ba, 128, axis=0)), "lbraw": np.ascontiguousarray(np.repeat(lbraw, 128, axis=0)),
                "lflag": np.full((128, 1), float(l), np.float32),
                "rope": ropetab, "cmats": cm, "ident": ident,
                "nabias": np.ascontiguousarray(nab.reshape(128, -1)),
            })
        if DBG.get("onlyA"):
            r0 = run_bass_kernel_spmd(ncA, ims[:1], core_ids=[0], trace=bool(DBG.get("trace")))
            DBG["yT"] = r0.results[0]["yT"]
            DBG["res"] = r0
            return None
        res = run_bass_kernel_spmd(ncA, ims, core_ids=cores)
        yfull = [np.zeros((D, NTOK), np.float32) for _ in range(2)]
        for cid in cores:
            b, j = cid // 4, cid % 4
            yt = res.results[cid]["yT"]
            yfull[b][j * 128:(j + 1) * 128] = yt[0:128]
            yfull[b][512 + j * 128:512 + (j + 1) * 128] = yt[128:256]
            yfull[b][1024 + 2 * j * 128:1024 + (2 * j + 2) * 128] = yt[256:512]
        _CACHE["dbg_y%d" % l] = yfull
        ims = []
        for cid in cores:
            b, q = cid // 4, cid % 4
            lat = slice(q * 1024, (q + 1) * 1024)
            cr = slice(q * 64, (q + 1) * 64)
            xT = np.concatenate([xs[b][lat].T, cs[b][cr].T], axis=1)
            yT = np.concatenate([yfull[b][:, CTX + q * 1024:CTX + (q + 1) * 1024], yfull[b][:, q * 64:(q + 1) * 64]], axis=1)
            vecs = np.concatenate([_pp(mod[l, 2, :, b]), _pp(mod[l, 3, :, b]), _pp(mod[l, 4, :, b]), _pp(mod[l, 5, :, b]),
                                   _pp(mod[l, 2, :, 2]), _pp(mod[l, 3, :, 2]), _pp(mod[l, 4, :, 2]), _pp(mod[l, 5, :, 2]),
                                   _pp(mlp_norm[l])], axis=1)
            ims.append({"xT": np.ascontiguousarray(xT), "yT": np.ascontiguousarray(yT), "vecs": np.ascontiguousarray(vecs),
                        "w_out": w_out[l], "w1": w_mlp1[l], "w2": w_mlp2[l]})
        res = run_bass_kernel_spmd(ncB, ims, core_ids=cores)
        nx = [np.zeros_like(xs[0]), np.zeros_like(xs[1])]
        ncx = [np.zeros_like(cs[0]), np.zeros_like(cs[1])]
        for cid in cores:
            b, q = cid // 4, cid % 4
            xo = res.results[cid]["xo"]
            nx[b][q * 1024:(q + 1) * 1024] = xo[:, :1024].T
            ncx[b][q * 64:(q + 1) * 64] = xo[:, 1024:].T
        xs, cs = nx, ncx
    return np.stack(xs, axis=0).astype(np.float32)
```

```python
import numpy as np
from contextlib import ExitStack
import concourse.bass as bass
import concourse.mybir as mybir
from concourse.bass_utils import run_bass_kernel_spmd

F32 = mybir.dt.float32
BF16 = mybir.dt.bfloat16
AF = mybir.ActivationFunctionType
ALU = mybir.AluOpType

D = 2048
DFF = 8192
NCH = 16
SEQ = 4096
CTX = 256
EPS = 1e-6


class Buf:
    def __init__(self, name, t):
        self.name = name
        self.t = t
        self.w = None
        self.r = {}
        self.dsem = None
        self.dcount = 0

    def __getitem__(self, idx):
        return self.t[idx]


class Eng:
    def __init__(self, name, eng, sem, selfsync):
        self.name, self.eng, self.sem, self.selfsync = name, eng, sem, selfsync
        self.count = 0
        self.seen = {}


class Prog:
    def __init__(self):
        self.nc = bass.Bass("TRN2", target_bir_lowering=False)
        self.es = ExitStack()
        nc = self.nc

        def mk(name, e, selfsync):
            return Eng(name, e, self.es.enter_context(nc.semaphore("s_" + name)), selfsync)

        self.pe = mk("pe", nc.tensor, False)
        self.act = mk("act", nc.scalar, True)
        self.dve = mk("dve", nc.vector, True)
        self.pool = mk("pool", nc.gpsimd, True)
        self.sp = mk("sp", nc.sync, True)
        self.dma_bufs = []
        self.nbuf = 0
        self.stacks = [self.es]

    def push(self):
        st = ExitStack()
        self.stacks.append(st)

    def pop(self):
        self.barrier()
        self.stacks.pop().close()

    def barrier(self):
        engs = [self.pe, self.act, self.dve, self.pool, self.sp]
        for E in engs:
            for F in engs:
                if F is E or F.count == 0:
                    continue
                k = id(F.sem)
                if E.seen.get(k, 0) < F.count:
                    E.eng.wait_ge(F.sem, F.count)
                    E.seen[k] = F.count
            for b in self.dma_bufs:
                k = id(b.dsem)
                if b.dcount and E.seen.get(k, 0) < 16 * b.dcount:
                    E.eng.wait_ge(b.dsem, 16 * b.dcount)
                    E.seen[k] = 16 * b.dcount

    def sbuf(self, name, shape, dt):
        self.nbuf += 1
        return Buf(name, self.stacks[-1].enter_context(self.nc.sbuf_tensor(f"{name}_{self.nbuf}", list(shape), dt)))

    def psum(self, name, shape, dt):
        self.nbuf += 1
        return Buf(name, self.es.enter_context(self.nc.psum_tensor(f"{name}_{self.nbuf}", list(shape), dt)))

    def din(self, name, shape, dt=F32):
        return self.nc.dram_tensor(name, list(shape), dt, kind="ExternalInput").ap()

    def dout(self, name, shape, dt=F32):
        return self.nc.dram_tensor(name, list(shape), dt, kind="ExternalOutput").ap()

    def _wait(self, E, deps):
        best = {}
        for (sem, val, src) in deps:
            k = id(sem)
            if k not in best or val > best[k][1]:
                best[k] = (sem, val)
        for k, (sem, val) in best.items():
            if E.seen.get(k, 0) >= val:
                continue
            E.eng.wait_ge(sem, val)
            E.seen[k] = val

    def _deps(self, E, reads, writes, skip_dma_waw=False):
        deps = []
        for b in reads:
            if b.w is not None and not (b.w[2] is E and not E.selfsync):
                deps.append(b.w)
        for b in writes:
            if b.w is not None and not (b.w[2] is E and not E.selfsync):
                if not (skip_dma_waw and b.w[2] is None and b.w[0] is b.dsem):
                    deps.append(b.w)
            for tok in b.r.values():
                if tok[2] is E:
                    continue
                deps.append(tok)
        return deps

    def op(self, E, fn, reads=(), writes=(), inc=True):
        self._wait(E, self._deps(E, reads, writes))
        ins = fn(E.eng)
        tok = (E.sem, E.count + 1, E)
        if inc:
            ins.then_inc(E.sem, 1)
            E.count += 1
        for b in writes:
            b.w = tok
            b.r = {}
        for b in reads:
            b.r[id(E.sem)] = tok
        return ins

    def dma(self, E, out_ap, in_ap, rd=None, wr=None):
        b = wr if wr is not None else rd
        reads = [rd] if rd is not None else []
        writes = [wr] if wr is not None else []
        self._wait(E, self._deps(E, reads, writes, skip_dma_waw=True))
        if b.dsem is None:
            b.dsem = self.es.enter_context(self.nc.semaphore("d_" + b.name + str(len(self.dma_bufs))))
            self.dma_bufs.append(b)
        ins = E.eng.dma_start(out=out_ap, in_=in_ap)
        b.dcount += 1
        ins.then_inc(b.dsem, 16)
        tok = (b.dsem, 16 * b.dcount, None)
        if wr is not None:
            b.w = tok
            b.r = {}
        else:
            b.r[id(b.dsem)] = tok
        return ins

    def mm(self, ob, out_ap, lb, lhsT_ap, rb, rhs_ap, start, stop, inc=None):
        rd = [lb] if lb is rb else [lb, rb]
        return self.op(self.pe, lambda e: e.matmul(out_ap, lhsT=lhsT_ap, rhs=rhs_ap, start=start, stop=stop),
                       reads=rd, writes=[ob], inc=stop if inc is None else inc)

    def finish(self):
        E = self.sp
        for b in self.dma_bufs:
            if b.dcount:
                k = id(b.dsem)
                if E.seen.get(k, 0) < 16 * b.dcount:
                    E.eng.wait_ge(b.dsem, 16 * b.dcount)
                    E.seen[k] = 16 * b.dcount
        self.es.close()


def build_M():
    P = Prog()
    nc = P.nc
    NCC = 24
    cT = P.din("cT", [128, NCH * 3])
    wm = P.din("wm", [D, NCC * 128])
    bm = P.din("bm", [128, NCC])
    om = P.dout("om", [128, NCC * 3])
    c_sb = P.sbuf("c_sb", [128, NCH * 3], F32)
    e_sb = P.sbuf("e_sb", [128, NCH * 3], F32)
    s_sb = P.sbuf("s_sb", [128, NCH * 3], F32)
    b_sb = P.sbuf("b_sb", [128, NCC], F32)
    o_sb = P.sbuf("o_sb", [128, NCC * 3], F32)
    P.dma(P.sp, c_sb[:], cT[:, :], wr=c_sb)
    P.dma(P.sp, b_sb[:], bm[:, :], wr=b_sb)
    P.op(P.act, lambda e: e.activation(out=e_sb[:], in_=c_sb[:], func=AF.Exp, scale=-1.0), reads=[c_sb], writes=[e_sb])
    P.op(P.dve, lambda e: e.tensor_scalar(out=e_sb[:], in0=e_sb[:], scalar1=1.0, scalar2=None, op0=ALU.add), reads=[e_sb], writes=[e_sb])
    P.op(P.dve, lambda e: e.reciprocal(out=e_sb[:], in_=e_sb[:]), reads=[e_sb], writes=[e_sb])
    P.op(P.dve, lambda e: e.tensor_tensor(out=s_sb[:], in0=c_sb[:], in1=e_sb[:], op=ALU.mult), reads=[c_sb, e_sb], writes=[s_sb])
    GW = 4
    ng = NCC // GW
    wts = [P.sbuf(f"wt{i}", [128, NCH, GW * 128], F32) for i in range(2)]
    pss = [P.psum(f"ps{i}", [128, 512], F32) for i in range(4)]
    wmv = wm.rearrange("(k p) c -> p k c", p=128)

    def load(g):
        wt = wts[g % 2]
        for h in range(2):
            P.dma(P.sp, wt[:, h * 8:(h + 1) * 8, :], wmv[:, h * 8:(h + 1) * 8, g * GW * 128:(g + 1) * GW * 128], wr=wt)

    load(0)
    load(1)
    for g in range(ng):
        wt = wts[g % 2]
        for j in range(GW):
            cc = g * GW + j
            ps = pss[cc % 4]
            for k in range(NCH):
                P.mm(ps, ps[:, 0:3], wt, wt[:, k, j * 128:(j + 1) * 128], s_sb, s_sb[:, k * 3:(k + 1) * 3], k == 0, k == NCH - 1)
            P.op(P.dve, lambda e: e.tensor_scalar(out=o_sb[:, cc * 3:(cc + 1) * 3], in0=ps[:, 0:3], scalar1=b_sb[:, cc:cc + 1],
                                                  scalar2=None, op0=ALU.add), reads=[ps, b_sb], writes=[o_sb])
        if g + 2 < ng:
            load(g + 2)
    P.dma(P.sp, om[:, :], o_sb[:], rd=o_sb)
    P.finish()
    return nc


def build_B(blocks):
    N = sum(b[1] for b in blocks)
    P = Prog()
    nc = P.nc
    xT = P.din("xT", [D, N])
    yT = P.din("yT", [D, N])
    vecs = P.din("vecs", [128, 9 * NCH])
    w_out = P.din("w_out", [D, D])
    w1 = P.din("w1", [D, DFF])
    w2 = P.din("w2", [DFF, D])
    xo = P.dout("xo", [D, N])
    xTv = xT.rearrange("(c p) n -> p c n", p=128)
    yTv = yT.rearrange("(c p) n -> p c n", p=128)
    xov = xo.rearrange("(c p) n -> p c n", p=128)

    v_sb = P.sbuf("v_sb", [128, 9 * NCH], F32)
    gs_sb = P.sbuf("gs_sb", [128, 2 * NCH], F32)
    eps_sb = P.sbuf("eps_sb", [128, 1], F32)
    ones_sb = P.sbuf("ones_sb", [128, 128], BF16)
    xs = [P.sbuf(f"x{i}", [128, NCH, b[1]], F32) for i, b in enumerate(blocks)]
    ys = [P.sbuf(f"y{i}", [128, NCH, b[1]], BF16) for i, b in enumerate(blocks)]
    NW = 3
    wbufs = [P.sbuf(f"w{i}", [128, 8192], BF16) for i in range(NW)]
    a_sb = [[P.sbuf(f"a{j}_{i}", [128, 4, b[1]], BF16) for i, b in enumerate(blocks)] for j in range(2)]
    r_sb = [P.sbuf(f"r{i}", [128, 512], F32) for i in range(2)]
    sq_sb = [P.sbuf(f"sq{i}", [128, 512], BF16) for i in range(2)]
    t_sb = [P.sbuf(f"t{i}", [128, 512], F32) for i in range(2)]
    rstd_sb = P.sbuf("rstd", [128, 512], F32)
    pss = [P.psum(f"ps{i}", [128, 512], F32) for i in range(8)]
    psi = [0]

    def nextps():
        psi[0] += 1
        return pss[psi[0] % 8]

    P.dma(P.sp, v_sb[:], vecs[:, :], wr=v_sb)
    P.op(P.dve, lambda e: e.memset(eps_sb[:], EPS), writes=[eps_sb])
    P.op(P.dve, lambda e: e.memset(ones_sb[:], 1.0), writes=[ones_sb])
    for kind in range(2):
        P.op(P.dve, lambda e: e.scalar_tensor_tensor(out=gs_sb[:, kind * NCH:(kind + 1) * NCH],
                                                     in0=v_sb[:, (kind * 4 + 2) * NCH:(kind * 4 + 3) * NCH], scalar=1.0,
                                                     in1=v_sb[:, 8 * NCH:9 * NCH], op0=ALU.add, op1=ALU.mult),
             reads=[v_sb], writes=[gs_sb])
    for i, (s, n, kind) in enumerate(blocks):
        for h in range(4):
            P.dma(P.sp, xs[i][:, h * 4:(h + 1) * 4, :], xTv[:, h * 4:(h + 1) * 4, s:s + n], wr=xs[i])
        for h in range(4):
            P.dma(P.pool, ys[i][:, h * 4:(h + 1) * 4, :], yTv[:, h * 4:(h + 1) * 4, s:s + n], wr=ys[i])

    tiles = []
    for g in range(4):
        tiles.append(("o", g))
    for fg in range(16):
        tiles.append(("1", fg))
        tiles.append(("2", fg))
    w_outv = w_out.rearrange("(k p) c -> p k c", p=128)
    w1v = w1.rearrange("(k p) c -> p k c", p=128)
    w2v = w2.rearrange("(f p) c -> p f c", p=128)

    def load(ti):
        kind, g = tiles[ti]
        wb = wbufs[ti % NW]
        if kind == "o":
            dst = wb[:].rearrange("p (k c) -> p k c", k=16)
            for h in range(4):
                P.dma(P.pool, dst[:, h * 4:(h + 1) * 4, :], w_outv[:, h * 4:(h + 1) * 4, g * 512:(g + 1) * 512], wr=wb)
        elif kind == "1":
            dst = wb[:].rearrange("p (k c) -> p k c", k=16)
            for h in range(4):
                P.dma(P.pool, dst[:, h * 4:(h + 1) * 4, :], w1v[:, h * 4:(h + 1) * 4, g * 512:(g + 1) * 512], wr=wb)
        else:
            dst = wb[:].rearrange("p (f c) -> p f c", f=4)
            for h in range(4):
                P.dma(P.pool, dst[:, h:h + 1, :], w2v[:, g * 4 + h:g * 4 + h + 1, :], wr=wb)

    for ti in range(NW):
        load(ti)
    ti = 0
    for g in range(4):
        wb = wbufs[ti % NW]
        wv = wb[:].rearrange("p (k c) -> p k c", k=16)
        for i, (s, n, kind) in enumerate(blocks):
            for j in range(4):
                dc = g * 4 + j
                ps = nextps()
                for k in range(NCH):
                    P.mm(ps, ps[:, :n], wb, wv[:, k, j * 128:(j + 1) * 128], ys[i], ys[i][:, k, :], k == 0, k == NCH - 1)
                ga = v_sb[:, (kind * 4 + 0) * NCH + dc:(kind * 4 + 0) * NCH + dc + 1]
                P.op(P.dve, lambda e: e.scalar_tensor_tensor(out=xs[i][:, dc, :], in0=ps[:, :n], scalar=ga, in1=xs[i][:, dc, :],
                                                             op0=ALU.mult, op1=ALU.add), reads=[ps, v_sb, xs[i]], writes=[xs[i]])
        if ti + NW < len(tiles):
            load(ti + NW)
        ti += 1
    for i, (s, n, kind) in enumerate(blocks):
        ps = nextps()
        for c in range(NCH):
            sq = sq_sb[c % 2]
            P.op(P.act, lambda e: e.activation(out=sq[:, :n], in_=xs[i][:, c, :], func=AF.Square), reads=[xs[i]], writes=[sq])
            P.mm(ps, ps[:, :n], ones_sb, ones_sb[:], sq, sq[:, :n], c == 0, c == NCH - 1, inc=True)
        P.op(P.act, lambda e: e.activation(out=rstd_sb[:, :n], in_=ps[:, :n], func=AF.Ln, bias=eps_sb[:, 0:1], scale=1.0 / D),
             reads=[ps, eps_sb], writes=[rstd_sb])
        P.op(P.act, lambda e: e.activation(out=rstd_sb[:, :n], in_=rstd_sb[:, :n], func=AF.Exp, scale=-0.5),
             reads=[rstd_sb], writes=[rstd_sb])
        for c in range(NCH):
            t = t_sb[c % 2]
            gsc = gs_sb[:, kind * NCH + c:kind * NCH + c + 1]
            shc = v_sb[:, (kind * 4 + 1) * NCH + c:(kind * 4 + 1) * NCH + c + 1]
            P.op(P.dve, lambda e: e.scalar_tensor_tensor(out=t[:, :n], in0=xs[i][:, c, :], scalar=gsc, in1=rstd_sb[:, :n],
                                                         op0=ALU.mult, op1=ALU.mult), reads=[xs[i], gs_sb, rstd_sb], writes=[t])
            P.op(P.act, lambda e: e.activation(out=ys[i][:, c, :], in_=t[:, :n], func=AF.Identity, bias=shc, scale=1.0),
                 reads=[t, v_sb], writes=[ys[i]])
    ri = 0
    for fg in range(16):
        wb1 = wbufs[ti % NW]
        w1t = wb1[:].rearrange("p (k c) -> p k c", k=16)
        ab = a_sb[fg % 2]
        for i, (s, n, kind) in enumerate(blocks):
            for j in range(4):
                ps = nextps()
                for k in range(NCH):
                    P.mm(ps, ps[:, :n], wb1, w1t[:, k, j * 128:(j + 1) * 128], ys[i], ys[i][:, k, :], k == 0, k == NCH - 1)
                r = r_sb[ri % 2]
                ri += 1
                P.op(P.act, lambda e: e.activation(out=r[:, :n], in_=ps[:, :n], func=AF.Relu), reads=[ps], writes=[r])
                P.op(P.act, lambda e: e.activation(out=ab[i][:, j, :], in_=r[:, :n], func=AF.Square), reads=[r], writes=[ab[i]])
        if ti + NW < len(tiles):
            load(ti + NW)
        ti += 1
        wb2 = wbufs[ti % NW]
        w2t = wb2[:].rearrange("p (f c) -> p f c", f=4)
        for i, (s, n, kind) in enumerate(blocks):
            for dc in range(NCH):
                ps = nextps()
                for j in range(4):
                    P.mm(ps, ps[:, :n], wb2, w2t[:, j, dc * 128:(dc + 1) * 128], ab[i], ab[i][:, j, :], j == 0, j == 3)
                gm = v_sb[:, (kind * 4 + 3) * NCH + dc:(kind * 4 + 3) * NCH + dc + 1]
                P.op(P.dve, lambda e: e.scalar_tensor_tensor(out=xs[i][:, dc, :], in0=ps[:, :n], scalar=gm, in1=xs[i][:, dc, :],
                                                             op0=ALU.mult, op1=ALU.add), reads=[ps, v_sb, xs[i]], writes=[xs[i]])
        if ti + NW < len(tiles):
            load(ti + NW)
        ti += 1
    for i, (s, n, kind) in enumerate(blocks):
        for h in range(4):
            P.dma(P.sp, xov[:, h * 4:(h + 1) * 4, s:s + n], xs[i][:, h * 4:(h + 1) * 4, :], rd=xs[i])
    P.finish()
    return nc


NTOK = CTX + SEQ
NT = NTOK // 128
NBLK = NTOK // 256
NEG = -30000.0


def _stage1(P, xTv, vA, gsA, eps_sb, ones_bf, pss, nextps, w_tm, tm_groups, w_fm, fm_groups, tm_cb, fm_cb, h_store=None, h_load=None):
    P.push()
    NHB = 2 if h_load is None else 3
    hb = [P.sbuf(f"hb{i}", [128, NCH, 256], BF16) for i in range(NHB)]
    if h_load is None:
        xb = [P.sbuf(f"xb{i}", [128, NCH, 256], F32) for i in range(2)]
        sq_sb = [P.sbuf(f"sq{i}", [128, 256], BF16) for i in range(2)]
        t_sb = [P.sbuf(f"t{i}", [128, 256], F32) for i in range(2)]
        rstd_sb = P.sbuf("rstd", [128, 256], F32)

    def loadx(blk):
        x = xb[blk % 2]
        for h in range(4):
            P.dma(P.sp, x[:, h * 4:(h + 1) * 4, :], xTv[blk, :, h * 1024:(h + 1) * 1024].rearrange("p (c t) -> p c t", c=4), wr=x)

    def loadh(blk):
        hh = hb[blk % NHB]
        for h in range(2):
            P.dma(P.sp, hh[:, h * 8:(h + 1) * 8, :], h_load[blk, :, h * 2048:(h + 1) * 2048].rearrange("p (c t) -> p c t", c=8), wr=hh)

    if h_load is None:
        loadx(0)
    else:
        loadh(0)
        loadh(1)
    for blk in range(NBLK):
        kind = 1 if blk == 0 else 0
        h = hb[blk % NHB]
        if h_load is not None:
            if blk + 2 < NBLK:
                loadh(blk + 2)
        else:
            if blk + 1 < NBLK:
                loadx(blk + 1)
            x = xb[blk % 2]
            ps = nextps()
            for c in range(NCH):
                sq = sq_sb[c % 2]
                P.op(P.act, lambda e: e.activation(out=sq[:], in_=x[:, c, :], func=AF.Square), reads=[x], writes=[sq])
                P.mm(ps, ps[:, :256], ones_bf, ones_bf[:], sq, sq[:], c == 0, c == NCH - 1, inc=True)
            P.op(P.act, lambda e: e.activation(out=rstd_sb[:], in_=ps[:, :256], func=AF.Ln, bias=eps_sb[:, 0:1], scale=1.0 / D),
                 reads=[ps, eps_sb], writes=[rstd_sb])
            P.op(P.act, lambda e: e.activation(out=rstd_sb[:], in_=rstd_sb[:], func=AF.Exp, scale=-0.5), reads=[rstd_sb], writes=[rstd_sb])
            for c in range(NCH):
                t = t_sb[c % 2]
                gsc = gsA[:, kind * NCH + c:kind * NCH + c + 1]
                shc = vA[:, (kind * 2) * NCH + c:(kind * 2) * NCH + c + 1]
                P.op(P.dve, lambda e: e.scalar_tensor_tensor(out=t[:], in0=x[:, c, :], scalar=gsc, in1=rstd_sb[:],
                                                             op0=ALU.mult, op1=ALU.mult), reads=[x, gsA, rstd_sb], writes=[t])
                P.op(P.act, lambda e: e.activation(out=h[:, c, :], in_=t[:], func=AF.Identity, bias=shc, scale=1.0),
                     reads=[t, vA], writes=[h])
            if h_store is not None:
                for hh2 in range(2):
                    P.dma(P.sp, h_store[blk, :, hh2 * 2048:(hh2 + 1) * 2048].rearrange("p (c t) -> p c t", c=8),
                          h[:, hh2 * 8:(hh2 + 1) * 8, :], rd=h)
        for gi, (c0, M) in enumerate(fm_groups):
            ps = nextps()
            for k in range(NCH):
                P.mm(ps, ps[:M, :256], w_fm, w_fm[:, k, c0:c0 + M], h, h[:, k, :], k == 0, k == NCH - 1)
            fm_cb(gi, blk, ps)
        for tt in range(2):
            tile = blk * 2 + tt
            for gi, (c0, W) in enumerate(tm_groups):
                ps = nextps()
                for k in range(NCH):
                    P.mm(ps, ps[:, :W], h, h[:, k, tt * 128:(tt + 1) * 128], w_tm, w_tm[:, k, c0:c0 + W], k == 0, k == NCH - 1)
                tm_cb(gi, tile, ps)
    P.pop()


def _silu_gate(P, ps, n, dst_buf, dst_ap, tmp):
    P.op(P.act, lambda e: e.activation(out=tmp[:, :n], in_=ps[:, :n], func=AF.Exp, scale=-1.0), reads=[ps], writes=[tmp])
    P.op(P.dve, lambda e: e.tensor_scalar(out=tmp[:, :n], in0=tmp[:, :n], scalar1=1.0, scalar2=None, op0=ALU.add), reads=[tmp], writes=[tmp])
    P.op(P.dve, lambda e: e.reciprocal(out=tmp[:, :n], in_=tmp[:, :n]), reads=[tmp], writes=[tmp])
    P.op(P.dve, lambda e: e.tensor_tensor(out=dst_ap, in0=ps[:, :n], in1=tmp[:, :n], op=ALU.mult), reads=[ps, tmp], writes=[dst_buf])


def _scan(P, pss, nextps, cm, ident_bf, dk, q_st, k_st, v_st, la_st, O_st):
    P.push()
    NB = 2
    e12 = [[P.sbuf(f"e12_{d}{i}", [128, 384], F32) for i in range(NB)] for d in range(2)]
    e2n = [[P.sbuf(f"e2n_{d}{i}", [128, 256], F32) for i in range(NB)] for d in range(2)]
    e3 = [[P.sbuf(f"e3_{d}{i}", [128, 128], F32) for i in range(NB)] for d in range(2)]
    qb = [[P.sbuf(f"qb_{d}{i}", [128, 128], BF16) for i in range(NB)] for d in range(2)]
    qa = [[P.sbuf(f"qa_{d}{i}", [128, 128], BF16) for i in range(NB)] for d in range(2)]
    ka = [[P.sbuf(f"ka_{d}{i}", [128, 128], BF16) for i in range(NB)] for d in range(2)]
    qc = [[P.sbuf(f"qc_{d}{i}", [128, 128], BF16) for i in range(NB)] for d in range(2)]
    kc = [[P.sbuf(f"kc_{d}{i}", [128, 128], BF16) for i in range(NB)] for d in range(2)]
    kb = [[P.sbuf(f"kb_{d}{i}", [128, 128], BF16) for i in range(NB)] for d in range(2)]
    am1 = [[P.sbuf(f"am1_{d}{i}", [128, 128], BF16) for i in range(NB)] for d in range(2)]
    am2 = [[P.sbuf(f"am2_{d}{i}", [128, 128], BF16) for i in range(NB)] for d in range(2)]
    S32 = [P.sbuf(f"S32_{d}", [128, 128], F32) for d in range(2)]
    Sbf = [[P.sbuf(f"Sbf_{d}{i}", [128, 128], BF16) for i in range(2)] for d in range(2)]
    sidx = [0, 0]
    P.op(P.dve, lambda e: e.memset(O_st[:], 0.0), writes=[O_st])
    for d in range(2):
        P.op(P.dve, lambda e: e.memset(S32[d][:], 0.0), writes=[S32[d]])
        P.op(P.dve, lambda e: e.memset(Sbf[d][0][:], 0.0), writes=[Sbf[d][0]])
    order = [list(range(NT)), [1, 0] + list(range(NT - 1, 1, -1))]
    bks = [(pss[0], pss[1], pss[2]), (pss[3], pss[4], pss[5])]
    bka = [pss[6], pss[7]]
    for step in range(NT):
        it = step % NB
        ctxs = []
        for d in range(2):
            tile = order[d][step]
            la_buf, la_ap = la_st[d](tile)
            U, Ua, Uc, Ub, M1, M2 = (cm[:, (d * 6 + i) * 128:(d * 6 + i + 1) * 128] for i in range(6))
            bk1, bk2, bk3 = bks[d]
            P.mm(bk1, bk1[:dk, 0:128], la_buf, la_ap, cm, U, True, True)
            P.mm(bk1, bk1[:dk, 128:256], la_buf, la_ap, cm, Ua, True, True)
            P.mm(bk1, bk1[:dk, 256:384], la_buf, la_ap, cm, Uc, True, True)
            P.mm(bk1, bk1[:, 384:384 + dk], cm, Ub, la_buf, la_ap, True, True)
            P.mm(bk2, bk2[:dk, 0:128], q_st, q_st[:, tile, :], ident_bf, ident_bf[:], True, True)
            P.mm(bk2, bk2[:dk, 128:256], k_st[d], k_st[d][:, tile, :], ident_bf, ident_bf[:], True, True)
            ctxs.append((tile, M1, M2, bk1, bk2, bk3))
        for d in range(2):
            tile, M1, M2, bk1, bk2, bk3 = ctxs[d]
            E12, E2n, E3 = e12[d][it], e2n[d][it], e3[d][it]
            P.op(P.act, lambda e: e.activation(out=E12[:dk, :], in_=bk1[:dk, 0:384], func=AF.Exp), reads=[bk1], writes=[E12])
            P.op(P.act, lambda e: e.activation(out=E2n[:dk, :], in_=bk1[:dk, 128:384], func=AF.Exp, scale=-1.0), reads=[bk1], writes=[E2n])
            P.op(P.act, lambda e: e.activation(out=E3[:, :dk], in_=bk1[:, 384:384 + dk], func=AF.Exp), reads=[bk1], writes=[E3])
        for d in range(2):
            tile, M1, M2, bk1, bk2, bk3 = ctxs[d]
            E12, E2n, E3 = e12[d][it], e2n[d][it], e3[d][it]
            QB, QA, KA, QC, KC, KB = qb[d][it], qa[d][it], ka[d][it], qc[d][it], kc[d][it], kb[d][it]
            pt = bk2
            P.op(P.dve, lambda e: e.tensor_tensor(out=QB[:dk, :], in0=pt[:dk, 0:128], in1=E12[:dk, 0:128], op=ALU.mult),
                 reads=[pt, E12], writes=[QB])
            P.op(P.dve, lambda e: e.tensor_tensor(out=QA[:dk, :], in0=pt[:dk, 0:128], in1=E12[:dk, 128:256], op=ALU.mult),
                 reads=[pt, E12], writes=[QA])
            P.op(P.dve, lambda e: e.tensor_tensor(out=KA[:dk, :], in0=pt[:dk, 128:256], in1=E2n[:dk, 0:128], op=ALU.mult),
                 reads=[pt, E2n], writes=[KA])
            P.op(P.dve, lambda e: e.scalar_tensor_tensor(out=QC[:dk, :], in0=E12[:dk, 256:384], scalar=1.0, in1=pt[:dk, 0:128],
                                                         op0=ALU.min, op1=ALU.mult), reads=[pt, E12], writes=[QC])
            P.op(P.dve, lambda e: e.scalar_tensor_tensor(out=KC[:dk, :], in0=E2n[:dk, 128:256], scalar=1.0, in1=pt[:dk, 128:256],
                                                         op0=ALU.min, op1=ALU.mult), reads=[pt, E2n], writes=[KC])
            P.op(P.dve, lambda e: e.tensor_tensor(out=KB[:, :dk], in0=k_st[d][:, tile, :], in1=E3[:, :dk], op=ALU.mult),
                 reads=[k_st[d], E3], writes=[KB])
        for d in range(2):
            tile, M1, M2, bk1, bk2, bk3 = ctxs[d]
            QA, KA, QC, KC = qa[d][it], ka[d][it], qc[d][it], kc[d][it]
            P.mm(bka[d], bka[d][:, 0:128], KA, KA[:dk, :], QA, QA[:dk, :], True, True)
            P.mm(bka[d], bka[d][:, 128:256], KC, KC[:dk, :], QC, QC[:dk, :], True, True)
        for d in range(2):
            tile, M1, M2, bk1, bk2, bk3 = ctxs[d]
            AM1, AM2 = am1[d][it], am2[d][it]
            P.op(P.dve, lambda e: e.tensor_tensor(out=AM1[:], in0=bka[d][:, 0:128], in1=M1, op=ALU.mult), reads=[bka[d], cm], writes=[AM1])
            P.op(P.dve, lambda e: e.tensor_tensor(out=AM2[:], in0=bka[d][:, 128:256], in1=M2, op=ALU.mult), reads=[bka[d], cm], writes=[AM2])
        for ci in range(2):
            for d in range(2):
                tile, M1, M2, bk1, bk2, bk3 = ctxs[d]
                c = ci if d == 0 else 1 - ci
                cs = slice(c * 64, (c + 1) * 64)
                QB, KB, AM1, AM2 = qb[d][it], kb[d][it], am1[d][it], am2[d][it]
                Sb = Sbf[d][sidx[d] % 2]
                P.mm(bk3, bk3[:, cs], v_st, v_st[:, tile, :], AM1, AM1[:, cs], True, False)
                P.mm(bk3, bk3[:, cs], v_st, v_st[:, tile, :], AM2, AM2[:, cs], False, False)
                P.mm(bk3, bk3[:, cs], Sb, Sb[:dk, :], QB, QB[:dk, cs], False, True)
                P.mm(bk3, bk3[:dk, 128 + ci * 128:256 + ci * 128], KB, KB[cs, :dk], v_st, v_st[cs, tile, :], True, True)
            for d in range(2):
                tile, M1, M2, bk1, bk2, bk3 = ctxs[d]
                c = ci if d == 0 else 1 - ci
                E12 = e12[d][it]
                ecol = (c * 64 + 63) if d == 0 else (c * 64)
                P.op(P.dve, lambda e: e.scalar_tensor_tensor(out=S32[d][:dk, :], in0=S32[d][:dk, :], scalar=E12[:dk, ecol:ecol + 1],
                                                             in1=bk3[:dk, 128 + ci * 128:256 + ci * 128], op0=ALU.mult, op1=ALU.add),
                     reads=[S32[d], E12, bk3], writes=[S32[d]])
            for d in range(2):
                sidx[d] += 1
                Sn = Sbf[d][sidx[d] % 2]
                P.op(P.act, lambda e: e.activation(out=Sn[:dk, :], in_=S32[d][:dk, :], func=AF.Identity), reads=[S32[d]], writes=[Sn])
        for d in range(2):
            tile, M1, M2, bk1, bk2, bk3 = ctxs[d]
            ts = slice(tile * 128, (tile + 1) * 128)
            P.op(P.dve, lambda e: e.tensor_tensor(out=O_st[:, ts], in0=bk3[:, 0:128], in1=O_st[:, ts], op=ALU.add),
                 reads=[bk3, O_st], writes=[O_st])
    P.pop()


def _readout(P, nextps, O_st, gate_st, gn_ap, gn_buf, eps_sb, ones_bf, yTv_rows):
    P.push()
    sq = [P.sbuf(f"rsq{i}", [128, 512], BF16) for i in range(2)]
    rs = [P.sbuf(f"rrs{i}", [128, 512], F32) for i in range(2)]
    yo = [P.sbuf(f"ryo{i}", [128, 512], F32) for i in range(2)]
    for bi, s in enumerate(range(0, NTOK, 512)):
        n = min(512, NTOK - s)
        SQ, RS, YO = sq[bi % 2], rs[bi % 2], yo[bi % 2]
        P.op(P.act, lambda e: e.activation(out=SQ[:, :n], in_=O_st[:, s:s + n], func=AF.Square), reads=[O_st], writes=[SQ])
        ps = nextps()
        P.mm(ps, ps[:, :n], ones_bf, ones_bf[:], SQ, SQ[:, :n], True, True)
        P.op(P.act, lambda e: e.activation(out=RS[:, :n], in_=ps[:, :n], func=AF.Ln, bias=eps_sb[:, 0:1], scale=1.0 / 128),
             reads=[ps, eps_sb], writes=[RS])
        P.op(P.act, lambda e: e.activation(out=RS[:, :n], in_=RS[:, :n], func=AF.Exp, scale=-0.5), reads=[RS], writes=[RS])
        P.op(P.dve, lambda e: e.scalar_tensor_tensor(out=YO[:, :n], in0=O_st[:, s:s + n], scalar=gn_ap, in1=RS[:, :n],
                                                     op0=ALU.mult, op1=ALU.mult), reads=[O_st, gn_buf, RS], writes=[YO])
        P.op(P.dve, lambda e: e.tensor_tensor(out=YO[:, :n], in0=YO[:, :n], in1=gate_st[:, s:s + n], op=ALU.mult),
             reads=[YO, gate_st], writes=[YO])
        P.dma(P.sp, yTv_rows[:, s:s + n], YO[:, :n], rd=YO)
    P.pop()


def na_tile_plan():
    plan = {}
    keys = {}
    for n in range(8):
        for m in range(32):
            rows_q = np.arange(8 * n, 8 * n + 8)
            r0 = np.clip(rows_q - 4, 0, 56)
            ok = False
            for kr in (2 * m, 2 * m + 1):
                if np.any((r0 <= kr) & (kr < r0 + 8)):
                    ok = True
            if not ok:
                continue
            key = ("top", m) if n == 0 else (("bot", m) if n == 7 else ("int", m - 4 * n))
            if key not in keys:
                keys[key] = (len(keys), n, m)
            plan[(n, m)] = keys[key][0]
    reps = sorted(keys.values())
    return plan, [(n, m) for (_, n, m) in reps]


def build_A():
    P = Prog()
    nc = P.nc
    plan, reps = na_tile_plan()
    NBT = len(reps)
    xT = P.din("xT", [NBLK, 128, NCH * 256])
    hscr = nc.dram_tensor("hscr", [NBLK, 128, NCH * 256], BF16).ap()
    vecsA = P.din("vecsA", [128, 5 * NCH])
    pvec = P.din("pvec", [128, 4])
    w_na_tm = P.din("w_na_tm", [D, 256])
    w_na_fm = P.din("w_na_fm", [D, 512])
    w_gla_tm = P.din("w_gla_tm", [D, 384])
    w_gla_fm = P.din("w_gla_fm", [D, 160])
    w_hg_tm = P.din("w_hg_tm", [D, 512])
    w_hg_fm = P.din("w_hg_fm", [D, 128])
    wa2 = P.din("wa2", [128, 128])
    ba = P.din("ba", [128, 128])
    lbraw = P.din("lbraw", [128, 512])
    lflag = P.din("lflag", [128, 1])
    rope = P.din("rope", [128, 32 * 128])
    cmats = P.din("cmats", [128, 12 * 128])
    identd = P.din("ident", [128, 128])
    nabias = P.din("nabias", [128, 2 * NBT * 512])
    yT = P.dout("yT", [512, NTOK])
    xTv = xT

    vA = P.sbuf("vA", [128, 5 * NCH], F32)
    gsA = P.sbuf("gsA", [128, 2 * NCH], F32)
    pv = P.sbuf("pv", [128, 4], F32)
    pvs = P.sbuf("pvs", [128, 1], F32)
    eps_sb = P.sbuf("eps_sb", [128, 1], F32)
    ones_bf = P.sbuf("ones_bf", [128, 128], BF16)
    ones_f = P.sbuf("ones_f", [128, 128], F32)
    ident_bf = P.sbuf("ident_bf", [128, 128], BF16)
    cm = P.sbuf("cm", [128, 12 * 128], F32)
    pss = [P.psum(f"ps{i}", [128, 512], F32) for i in range(8)]
    psi = [0]

    psa = [0]
    psb = [0]

    def nextps(pool=None):
        if pool == "a":
            psa[0] += 1
            return pss[psa[0] % 4]
        if pool == "b":
            psb[0] += 1
            return pss[4 + psb[0] % 4]
        psi[0] += 1
        return pss[psi[0] % 8]

    P.dma(P.sp, vA[:], vecsA[:, :], wr=vA)
    P.dma(P.sp, pv[:], pvec[:, :], wr=pv)
    P.dma(P.sp, cm[:], cmats[:, :], wr=cm)
    P.dma(P.pool, ident_bf[:], identd[:, :], wr=ident_bf)
    P.op(P.dve, lambda e: e.memset(eps_sb[:], EPS), writes=[eps_sb])
    P.op(P.dve, lambda e: e.memset(ones_bf[:], 1.0), writes=[ones_bf])
    P.op(P.dve, lambda e: e.memset(ones_f[:], 1.0), writes=[ones_f])
    for kind in range(2):
        P.op(P.dve, lambda e: e.scalar_tensor_tensor(out=gsA[:, kind * NCH:(kind + 1) * NCH],
                                                     in0=vA[:, (kind * 2 + 1) * NCH:(kind * 2 + 2) * NCH], scalar=1.0,
                                                     in1=vA[:, 4 * NCH:5 * NCH], op0=ALU.add, op1=ALU.mult), reads=[vA], writes=[gsA])
    P.op(P.dve, lambda e: e.tensor_scalar(out=pvs[:], in0=pv[:, 2:3], scalar1=float(128 ** -0.5), scalar2=None, op0=ALU.mult),
         reads=[pv], writes=[pvs])

    def load_w(buf, src, ncols):
        v = src.rearrange("(k p) c -> p k c", p=128)
        for h in range(4):
            P.dma(P.pool, buf[:, h * 4:(h + 1) * 4, :], v[:, h * 4:(h + 1) * 4, :], wr=buf)

    def na_phase():
        P.push()
        w_tm = P.sbuf("wna_tm", [128, NCH, 256], BF16)
        w_fm = P.sbuf("wna_fm", [128, NCH, 512], BF16)
        load_w(w_tm, w_na_tm, 256)
        load_w(w_fm, w_na_fm, 512)
        qT = P.sbuf("na_qT", [128, 2, NTOK], BF16)
        kT = P.sbuf("na_kT", [128, 2, NTOK], BF16)
        v_na = P.sbuf("na_v", [128, NT, 256], BF16)
        bias_sb = P.sbuf("na_bias", [128, 2 * NBT, 512], BF16)
        nbv = nabias.rearrange("p (t q) -> p t q", q=512)
        for t0 in range(0, 2 * NBT, 8):
            t1 = min(2 * NBT, t0 + 8)
            P.dma(P.pool, bias_sb[:, t0:t1, :], nbv[:, t0:t1, :], wr=bias_sb)
        tq = [P.sbuf(f"na_tq{i}", [128, 256], F32) for i in range(2)]
        sqn = [P.sbuf(f"na_sq{i}", [128, 256], BF16) for i in range(2)]
        rsn = [P.sbuf(f"na_rs{i}", [128, 256], F32) for i in range(2)]
        cnt = [0]

        def na_fm_cb(gi, blk, ps):
            i = cnt[0] % 2
            cnt[0] += 1
            TQ, SQ, RS = tq[i], sqn[i], rsn[i]
            P.op(P.act, lambda e: e.activation(out=TQ[:], in_=ps[:, :256], func=AF.Identity), reads=[ps], writes=[TQ])
            P.op(P.act, lambda e: e.activation(out=SQ[:], in_=TQ[:], func=AF.Square), reads=[TQ], writes=[SQ])
            p2 = nextps()
            P.mm(p2, p2[:, :256], ones_bf, ones_bf[:], SQ, SQ[:], True, True)
            P.op(P.act, lambda e: e.activation(out=RS[:], in_=p2[:, :256], func=AF.Ln, bias=eps_sb[:, 0:1], scale=1.0 / 128),
                 reads=[p2, eps_sb], writes=[RS])
            P.op(P.act, lambda e: e.activation(out=RS[:], in_=RS[:], func=AF.Exp, scale=-0.5), reads=[RS], writes=[RS])
            dst = qT if gi < 2 else kT
            gsc = pvs[:, 0:1] if gi < 2 else pv[:, 3:4]
            P.op(P.dve, lambda e: e.scalar_tensor_tensor(out=dst[:, gi % 2, blk * 256:(blk + 1) * 256], in0=TQ[:], scalar=gsc, in1=RS[:],
                                                         op0=ALU.mult, op1=ALU.mult), reads=[TQ, pvs, pv, RS], writes=[dst])

        def na_tm_cb(gi, tile, ps):
            P.op(P.act, lambda e: e.activation(out=v_na[:, tile, :], in_=ps[:, :256], func=AF.Identity), reads=[ps], writes=[v_na])

        _stage1(P, xTv, vA, gsA, eps_sb, ones_bf, pss, nextps, w_tm, [(0, 256)], w_fm, [(0, 128), (128, 128), (256, 128), (384, 128)],
                na_tm_cb, na_fm_cb, h_store=hscr)
        if DBG.get("stop") == 1:
            P.pop()
            return
        pT = [P.sbuf(f"na_pT{i}", [128, 512], BF16) for i in range(3)]
        rsum = [P.sbuf(f"na_rsum{i}", [128, 512], F32) for i in range(2)]
        yo = [P.sbuf(f"na_yo{i}", [128, 512], F32) for i in range(2)]
        pi = 0
        qi = 0
        for hh in range(2):
            qblocks = [(0, 256, None)] + [(CTX + 512 * n, 512, n) for n in range(8)]
            for (q0, nq, n) in qblocks:
                ktiles = [(0, None), (1, None)]
                if n is not None:
                    ktiles += [(2 + m, hh * NBT + plan[(n, m)]) for m in range(32) if (n, m) in plan]
                po = nextps("a")
                psm = nextps("a")

                def emit_score(ki):
                    kt, bt = ktiles[ki]
                    pscore = nextps("b")
                    P.mm(pscore, pscore[:, :nq], kT, kT[:, hh, kt * 128:(kt + 1) * 128], qT, qT[:, hh, q0:q0 + nq], True, bt is None)
                    if bt is not None:
                        P.mm(pscore, pscore[:, :nq], ident_bf, ident_bf[:], bias_sb, bias_sb[:, bt, :nq], False, True)
                    return pscore

                pend = emit_score(0)
                for ki, (kt, bt) in enumerate(ktiles):
                    pscore = pend
                    if ki + 1 < len(ktiles):
                        pend = emit_score(ki + 1)
                    PT = pT[pi % 3]
                    pi += 1
                    P.op(P.act, lambda e: e.activation(out=PT[:, :nq], in_=pscore[:, :nq], func=AF.Exp), reads=[pscore], writes=[PT])
                    last = ki == len(ktiles) - 1
                    P.mm(po, po[:, :nq], v_na, v_na[:, kt, hh * 128:(hh + 1) * 128], PT, PT[:, :nq], ki == 0, last, inc=True)
                    P.mm(psm, psm[:, :nq], ones_bf, ones_bf[:], PT, PT[:, :nq], ki == 0, last, inc=True)
                RSM, YO = rsum[qi % 2], yo[qi % 2]
                qi += 1
                P.op(P.dve, lambda e: e.reciprocal(out=RSM[:, :nq], in_=psm[:, :nq]), reads=[psm], writes=[RSM])
                P.op(P.dve, lambda e: e.tensor_tensor(out=YO[:, :nq], in0=po[:, :nq], in1=RSM[:, :nq], op=ALU.mult), reads=[po, RSM], writes=[YO])
                P.dma(P.sp, yT[256 + hh * 128:256 + (hh + 1) * 128, q0:q0 + nq], YO[:, :nq], rd=YO)
        P.pop()


    if not DBG.get("skip_na"):
        na_phase()
    if DBG.get("stop") == 1:
        P.finish()
        return nc
    if DBG.get("stop") == 2:
        P.finish()
        return nc
    P.push()
    w_tm = P.sbuf("wgla_tm", [128, NCH, 384], BF16)
    w_fm = P.sbuf("wgla_fm", [128, NCH, 160], BF16)
    load_w(w_tm, w_gla_tm, 384)
    load_w(w_fm, w_gla_fm, 160)
    rope_sb = P.sbuf("rope_sb", [128, 32, 128], F32)
    ropev = rope.rearrange("p (t c) -> p t c", c=128)
    for h in range(8):
        P.dma(P.sp, rope_sb[:, h * 4:(h + 1) * 4, :], ropev[:, h * 4:(h + 1) * 4, :], wr=rope_sb)
    wa2_sb = P.sbuf("wa2_sb", [128, 128], F32)
    ba_sb = P.sbuf("ba_sb", [128, 128], F32)
    P.dma(P.sp, wa2_sb[:], wa2[:, :], wr=wa2_sb)
    P.dma(P.sp, ba_sb[:], ba[:, :], wr=ba_sb)
    q_g = P.sbuf("gla_q", [128, NT, 64], BF16)
    k_g = P.sbuf("gla_k", [128, NT, 64], BF16)
    v_g = P.sbuf("gla_v", [128, NT, 128], BF16)
    la_g = P.sbuf("gla_la", [128, NT, 128], F32)
    gate_g = P.sbuf("gla_gate", [128, NTOK], BF16)
    al_g = [P.sbuf(f"gla_al{i}", [128, 256], F32) for i in range(2)]
    tmpg = [P.sbuf(f"gla_tmp{i}", [128, 256], F32) for i in range(2)]
    rt = [P.sbuf(f"gla_rt{i}", [128, 256], F32) for i in range(2)]
    cg = [0, 0]

    def gla_fm_cb(gi, blk, ps):
        if gi == 0:
            if DBG.get("skip_gate"):
                return
            T = tmpg[cg[0] % 2]
            cg[0] += 1
            _silu_gate(P, ps, 256, gate_g, gate_g[:, blk * 256:(blk + 1) * 256], T)
        elif not DBG.get("skip_la"):
            AL = al_g[blk % 2]
            P.op(P.act, lambda e: e.activation(out=AL[:, :], in_=ps[:, :256], func=AF.Identity), reads=[ps], writes=[AL])
            for tt in range(2):
                tile = blk * 2 + tt
                p2 = nextps()
                P.mm(p2, p2[:, :128], AL, AL[:, tt * 128:(tt + 1) * 128], wa2_sb, wa2_sb[:, :], True, True)
                T = tmpg[cg[0] % 2]
                cg[0] += 1
                P.op(P.dve, lambda e: e.tensor_tensor(out=T[:, :128], in0=p2[:, :128], in1=ba_sb[:, :], op=ALU.add), reads=[p2, ba_sb], writes=[T])
                P.op(P.act, lambda e: e.activation(out=T[:, :128], in_=T[:, :128], func=AF.Exp, scale=-1.0), reads=[T], writes=[T])
                P.op(P.act, lambda e: e.activation(out=T[:, :128], in_=T[:, :128], func=AF.Ln, bias=ones_f[:, 0:1], scale=1.0),
                     reads=[T, ones_f], writes=[T])
                P.op(P.dve, lambda e: e.tensor_scalar(out=la_g[:, tile, :], in0=T[:, :128], scalar1=-1.0 / 16.0, scalar2=None, op0=ALU.mult),
                     reads=[T], writes=[la_g])

    def gla_tm_cb(gi, tile, ps):
        if DBG.get("skip_tm"):
            return
        if tile < 2 or DBG.get("skip_rope"):
            P.op(P.act, lambda e: e.activation(out=q_g[:, tile, :], in_=ps[:, 0:64], func=AF.Identity, scale=0.125), reads=[ps], writes=[q_g])
            P.op(P.act, lambda e: e.activation(out=k_g[:, tile, :], in_=ps[:, 128:192], func=AF.Identity), reads=[ps], writes=[k_g])
        else:
            lt = tile - 2
            R = rt[cg[1] % 2]
            cg[1] += 1
            P.op(P.act, lambda e: e.activation(out=R[:, 0:128], in_=ps[:, 0:128], func=AF.Identity, scale=0.125), reads=[ps], writes=[R])
            P.op(P.act, lambda e: e.activation(out=R[:, 128:256], in_=ps[:, 128:256], func=AF.Identity), reads=[ps], writes=[R])
            for hh2 in range(2):
                P.op(P.dve, lambda e: e.tensor_tensor(out=R[:, hh2 * 128:(hh2 + 1) * 128], in0=R[:, hh2 * 128:(hh2 + 1) * 128],
                                                      in1=rope_sb[:, lt, :], op=ALU.mult), reads=[R, rope_sb], writes=[R])
            P.op(P.dve, lambda e: e.tensor_tensor(out=q_g[:, tile, :], in0=R[:, 0:64], in1=R[:, 64:128], op=ALU.add), reads=[R], writes=[q_g])
            P.op(P.dve, lambda e: e.tensor_tensor(out=k_g[:, tile, :], in0=R[:, 128:192], in1=R[:, 192:256], op=ALU.add), reads=[R], writes=[k_g])
        P.op(P.act, lambda e: e.activation(out=v_g[:, tile, :], in_=ps[:, 256:384], func=AF.Identity), reads=[ps], writes=[v_g])

    _stage1(P, xTv, vA, gsA, eps_sb, ones_bf, pss, nextps, w_tm, [(0, 384)], w_fm,
            [(32, 128)] if DBG.get("skip_alow") else [(32, 128), (0, 128)], gla_tm_cb, gla_fm_cb, h_load=hscr)
    if DBG.get("stop") == 3:
        P.pop()
        P.finish()
        return nc
    O_g = P.sbuf("gla_O", [128, NTOK], F32)
    _scan(P, pss, nextps, cm, ident_bf, 64, q_g, [k_g, k_g], v_g,
          [lambda tile: (la_g, la_g[:, tile, 0:64]), lambda tile: (la_g, la_g[:, tile, 64:128])], O_g)
    if DBG.get("stop") == 4:
        P.pop()
        P.finish()
        return nc
    _readout(P, nextps, O_g, gate_g, pv[:, 0:1], pv, eps_sb, ones_bf, yT[0:128, :])
    P.pop()
    if DBG.get("stop") == 5:
        P.finish()
        return nc

    P.push()
    w_tm = P.sbuf("whg_tm", [128, NCH, 512], BF16)
    w_fm = P.sbuf("whg_fm", [128, NCH, 128], BF16)
    load_w(w_tm, w_hg_tm, 512)
    load_w(w_fm, w_hg_fm, 128)
    lb_r = P.sbuf("lb_r", [128, 512], F32)
    lf_sb = P.sbuf("lf_sb", [128, 1], F32)
    lb_t = P.sbuf("lb_t", [128, 256], F32)
    oml_t = P.sbuf("oml_t", [128, 256], F32)
    P.dma(P.sp, lb_r[:], lbraw[:, :], wr=lb_r)
    P.dma(P.sp, lf_sb[:], lflag[:, :], wr=lf_sb)
    P.op(P.act, lambda e: e.activation(out=lb_r[:], in_=lb_r[:], func=AF.Exp), reads=[lb_r], writes=[lb_r])
    P.op(P.dve, lambda e: e.tensor_tensor(out=lb_t[:], in0=lb_r[:, 0:256], in1=lb_r[:, 256:512], op=ALU.add), reads=[lb_r], writes=[lb_t])
    P.op(P.dve, lambda e: e.reciprocal(out=lb_t[:], in_=lb_t[:]), reads=[lb_t], writes=[lb_t])
    P.op(P.dve, lambda e: e.tensor_tensor(out=lb_t[:], in0=lb_t[:], in1=lb_r[:, 256:512], op=ALU.mult), reads=[lb_t, lb_r], writes=[lb_t])
    P.op(P.dve, lambda e: e.tensor_scalar(out=lb_t[:], in0=lb_t[:], scalar1=lf_sb[:, 0:1], scalar2=None, op0=ALU.mult),
         reads=[lb_t, lf_sb], writes=[lb_t])
    P.op(P.dve, lambda e: e.tensor_scalar(out=oml_t[:], in0=lb_t[:], scalar1=-1.0, scalar2=1.0, op0=ALU.mult, op1=ALU.add),
         reads=[lb_t], writes=[oml_t])
    q_h = P.sbuf("hg_q", [128, NT, 128], BF16)
    k_h = [P.sbuf(f"hg_k{d}", [128, NT, 128], BF16) for d in range(2)]
    v_h = P.sbuf("hg_v", [128, NT, 128], BF16)
    la_h = P.sbuf("hg_la", [128, NT, 256], F32)
    gate_h = P.sbuf("hg_gate", [128, NTOK], BF16)
    tmph = [P.sbuf(f"hg_tmp{i}", [128, 256], F32) for i in range(2)]
    eh = [P.sbuf(f"hg_e{i}", [128, 384], F32) for i in range(2)]
    ch = [0, 0]

    def hg_fm_cb(gi, blk, ps):
        T = tmph[ch[0] % 2]
        ch[0] += 1
        _silu_gate(P, ps, 256, gate_h, gate_h[:, blk * 256:(blk + 1) * 256], T)

    def hg_tm_cb(gi, tile, ps):
        E = eh[ch[1] % 2]
        ch[1] += 1
        P.op(P.act, lambda e: e.activation(out=v_h[:, tile, :], in_=ps[:, 384:512], func=AF.Identity), reads=[ps], writes=[v_h])
        P.op(P.act, lambda e: e.activation(out=E[:, 0:384], in_=ps[:, 0:384], func=AF.Exp, scale=-1.0), reads=[ps], writes=[E])
        P.op(P.dve, lambda e: e.tensor_scalar(out=E[:, 0:384], in0=E[:, 0:384], scalar1=1.0, scalar2=None, op0=ALU.add), reads=[E], writes=[E])
        P.op(P.dve, lambda e: e.reciprocal(out=E[:, 0:384], in_=E[:, 0:384]), reads=[E], writes=[E])
        P.op(P.dve, lambda e: e.tensor_tensor(out=q_h[:, tile, :], in0=ps[:, 0:128], in1=E[:, 0:128], op=ALU.mult), reads=[ps, E], writes=[q_h])
        P.op(P.dve, lambda e: e.tensor_tensor(out=E[:, 128:384], in0=E[:, 128:384], in1=oml_t[:], op=ALU.mult), reads=[E, oml_t], writes=[E])
        P.op(P.dve, lambda e: e.tensor_tensor(out=E[:, 128:384], in0=E[:, 128:384], in1=lb_t[:], op=ALU.add), reads=[E, lb_t], writes=[E])
        P.op(P.act, lambda e: e.activation(out=la_h[:, tile, :], in_=E[:, 128:384], func=AF.Ln), reads=[E], writes=[la_h])
        for d in range(2):
            P.op(P.dve, lambda e: e.tensor_scalar(out=k_h[d][:, tile, :], in0=E[:, 128 + d * 128:256 + d * 128], scalar1=-1.0, scalar2=1.0,
                                                  op0=ALU.mult, op1=ALU.add), reads=[E], writes=[k_h[d]])

    _stage1(P, xTv, vA, gsA, eps_sb, ones_bf, pss, nextps, w_tm, [(0, 512)], w_fm, [(0, 128)], hg_tm_cb, hg_fm_cb, h_load=hscr)
    O_h = P.sbuf("hg_O", [128, NTOK], F32)
    _scan(P, pss, nextps, cm, ident_bf, 128, q_h, k_h, v_h,
          [lambda tile: (la_h, la_h[:, tile, 0:128]), lambda tile: (la_h, la_h[:, tile, 128:256])], O_h)
    _readout(P, nextps, O_h, gate_h, pv[:, 1:2], pv, eps_sb, ones_bf, yT[128:256, :])
    P.pop()
    P.finish()
    return nc


GLA_Q0, GLA_K0, GLA_V0, GLA_R0, GLA_A0 = 0, 256, 512, 1024, 1536
HG_Q0, HG_F0, HG_I0, HG_G0 = 1568, 2080, 3104, 3616
NA_Q0, NA_K0, NA_V0 = 4128, 5152, 6176


def _pp(v):
    return np.ascontiguousarray(np.asarray(v, np.float32).reshape(NCH, 128).T)


def _consts():
    j = np.arange(128)[:, None]
    i = np.arange(128)[None, :]
    same = (j // 64) == (i // 64)
    base = (i // 64) * 64
    half = (i % 64) // 32
    jhalf = (j % 64) // 32
    mats = []
    f = lambda m: m.astype(np.float32)
    U = f(same & (j <= i))
    Ua = U - f(same & (j <= base + 32 * half + 15))
    Uc = U - f(same & (j <= base + 31))
    Ub = f(same & (j > i))
    M1 = f(same & (jhalf == half) & (j <= i))
    M2 = f(same & (jhalf == 0) & (half == 1))
    mats += [U, Ua, Uc, Ub, M1, M2]
    Ur = f(same & (j >= i))
    Uar = Ur - f(same & (j >= base + 32 * half + 16))
    Ucr = Ur - f(same & (j >= base + 32))
    Ubr = f(same & (j < i))
    M1r = f(same & (jhalf == half) & (j >= i))
    M2r = f(same & (jhalf == 1) & (half == 0))
    mats += [Ur, Uar, Ucr, Ubr, M1r, M2r]
    cm = np.concatenate(mats, axis=1).astype(np.float32)
    ident = np.eye(128, dtype=np.float32)
    pos = np.arange(SEQ)
    rowp, colp = pos // 64, pos % 64
    inv = (10000.0 ** (-np.arange(16, dtype=np.float32) / 16)).astype(np.float32)
    cos = np.zeros((SEQ, 64), np.float32)
    sin = np.zeros((SEQ, 64), np.float32)
    for half, pp in enumerate((rowp, colp)):
        ang = pp.astype(np.float32)[:, None] * inv[None, :]
        c_, s_ = np.cos(ang).astype(np.float32), np.sin(ang).astype(np.float32)
        b0 = half * 32
        cos[:, b0:b0 + 16] = c_
        cos[:, b0 + 16:b0 + 32] = c_
        sin[:, b0:b0 + 16] = -s_
        sin[:, b0 + 16:b0 + 32] = s_
    tab = np.concatenate([cos, sin], axis=1).reshape(32, 128, 128).transpose(1, 0, 2).reshape(128, 32 * 128)
    return cm, ident, np.ascontiguousarray(tab)


def _swap_idx():
    idx = np.arange(64)
    out = idx.copy()
    for b0 in (0, 32):
        out[b0:b0 + 16] = idx[b0 + 16:b0 + 32]
        out[b0 + 16:b0 + 32] = idx[b0:b0 + 16]
    return out


def _na_bias_tiles(rpb_h, reps):
    out = np.full((128, len(reps), 512), NEG, np.float32)
    kl = np.arange(128)
    ql = np.arange(512)
    for ti, (n, m) in enumerate(reps):
        kr = (2 * m + kl // 64)[:, None]
        kc = (kl % 64)[:, None]
        r = (8 * n + ql // 64)[None, :]
        c = (ql % 64)[None, :]
        r0 = np.clip(r - 4, 0, 56)
        c0 = np.clip(c - 8, 0, 48)
        valid = (kr >= r0) & (kr < r0 + 8) & (kc >= c0) & (kc < c0 + 16)
        dr = np.clip(kr - r + 7, 0, 14)
        dc = np.clip(kc - c + 15, 0, 30)
        vals = rpb_h[dr, dc]
        out[:, ti, :] = np.where(valid, vals, np.float32(NEG))
    return out


_CACHE = {}
DBG = {}


def _get(name, fn):
    if name not in _CACHE:
        _CACHE[name] = fn()
    return _CACHE[name]


def kernel(x, c, ctx, c_ctx, w_mod, b_mod, attn_norm, w_in, gla_w_a2, gla_b_a, gla_norm, hg_lower_bounds, hg_norm,
           na_q_norm, na_k_norm, na_rpb, w_out, mlp_norm, w_mlp1, w_mlp2):
    f = lambda a: np.asarray(a, dtype=np.float32)
    x, c, ctx, c_ctx, w_mod, b_mod, attn_norm, w_in = map(f, (x, c, ctx, c_ctx, w_mod, b_mod, attn_norm, w_in))
    gla_w_a2, gla_b_a, gla_norm, hg_lower_bounds, hg_norm = map(f, (gla_w_a2, gla_b_a, gla_norm, hg_lower_bounds, hg_norm))
    na_q_norm, na_k_norm, na_rpb, w_out, mlp_norm, w_mlp1, w_mlp2 = map(f, (na_q_norm, na_k_norm, na_rpb, w_out, mlp_norm, w_mlp1, w_mlp2))
    cores = list(range(8))
    ncM = _get("M", build_M)
    cvec = np.stack([c[0], c[1], c_ctx], axis=1)
    cT = np.ascontiguousarray(cvec.reshape(NCH, 128, 3).transpose(1, 0, 2).reshape(128, NCH * 3))
    wcat = np.concatenate([w_mod[0], w_mod[1]], axis=1)
    bcat = np.concatenate([b_mod[0], b_mod[1]], axis=0)
    ims = []
    for cid in cores:
        sl = slice(cid * 3072, (cid + 1) * 3072)
        ims.append({"cT": cT, "wm": np.ascontiguousarray(wcat[:, sl]), "bm": np.ascontiguousarray(bcat[sl].reshape(24, 128).T)})
    res = run_bass_kernel_spmd(ncM, ims, core_ids=cores)
    mod = np.concatenate([r["om"].reshape(128, 24, 3).transpose(1, 0, 2).reshape(3072, 3) for r in res.results], axis=0)
    mod = mod.reshape(2, 6, D, 3)
    DBG["mod"] = mod
    if DBG.get("onlyM"):
        return None

    cm, ident, ropetab = _consts()
    plan, reps = na_tile_plan()
    sw = _swap_idx()
    ncA = _get("A", build_A)
    blocksB = [(0, 512, 0), (512, 512, 0), (1024, 64, 1)]
    ncB = _get("B", lambda: build_B(blocksB))
    xs = [x[0], x[1]]
    cs = [ctx[0], ctx[1]]
    for l in range(2):
        ims = []
        for cid in cores:
            b, j = cid // 4, cid % 4
            W = w_in[l]
            g64 = lambda o: np.arange(o + j * 64, o + (j + 1) * 64)
            g128 = lambda o, h=None: np.arange(o + (j if h is None else h) * 128, o + ((j if h is None else h) + 1) * 128)
            gq, gk = g64(GLA_Q0), g64(GLA_K0)
            cols_gla_tm = np.concatenate([gq, gq[sw], gk, gk[sw], g128(GLA_V0)])
            cols_gla_fm = np.concatenate([np.arange(GLA_A0, GLA_A0 + 32), g128(GLA_R0)])
            cols_hg_tm = np.concatenate([g128(HG_Q0), g128(HG_F0), g128(HG_F0 + 512), g128(HG_I0)])
            cols_hg_fm = g128(HG_G0)
            h0, h1 = 2 * j, 2 * j + 1
            cols_na_tm = np.concatenate([g128(NA_V0, h0), g128(NA_V0, h1)])
            cols_na_fm = np.concatenate([g128(NA_Q0, h0), g128(NA_Q0, h1), g128(NA_K0, h0), g128(NA_K0, h1)])
            wa2 = np.zeros((128, 128), np.float32)
            wa2[0:16, 0:64] = gla_w_a2[l, 0][:, j * 64:(j + 1) * 64]
            wa2[16:32, 64:128] = gla_w_a2[l, 1][:, j * 64:(j + 1) * 64]
            ba = np.concatenate([gla_b_a[l, 0][j * 64:(j + 1) * 64], gla_b_a[l, 1][j * 64:(j + 1) * 64]])[None, :]
            hc = slice(j * 128, (j + 1) * 128)
            lbraw = np.concatenate([hg_lower_bounds[0, 0][hc], hg_lower_bounds[0, 1][hc],
                                    hg_lower_bounds[1, 0][hc], hg_lower_bounds[1, 1][hc]])[None, :]
            nab = np.concatenate([_na_bias_tiles(na_rpb[l, h0], reps), _na_bias_tiles(na_rpb[l, h1], reps)], axis=1)
            vecsA = np.concatenate([_pp(mod[l, 0, :, b]), _pp(mod[l, 1, :, b]), _pp(mod[l, 0, :, 2]), _pp(mod[l, 1, :, 2]),
                                    _pp(attn_norm[l])], axis=1)
            pvec = np.stack([gla_norm[l], hg_norm[l], na_q_norm[l], na_k_norm[l]], axis=1)
            ims.append({
                "xT": np.ascontiguousarray(np.concatenate([cs[b].T, xs[b].T], axis=1).reshape(NCH, 128, NBLK, 256).transpose(2, 1, 0, 3)).reshape(NBLK, 128, NCH * 256),
                "vecsA": np.ascontiguousarray(vecsA), "pvec": np.ascontiguousarray(pvec),
                "w_na_tm": np.ascontiguousarray(W[:, cols_na_tm]), "w_na_fm": np.ascontiguousarray(W[:, cols_na_fm]),
                "w_gla_tm": np.ascontiguousarray(W[:, cols_gla_tm]), "w_gla_fm": np.ascontiguousarray(W[:, cols_gla_fm]),
                "w_hg_tm": np.ascontiguousarray(W[:, cols_hg_tm]), "w_hg_fm": np.ascontiguousarray(W[:, cols_hg_fm]),
                "wa2": wa2, "ba": np.ascontiguousarray(np.repeat(ba, 128, axis=0)), "lbraw": np.ascontiguousarray(np.repeat(lbraw, 128, axis=0)),
                "lflag": np.full((128, 1), float(l), np.float32),
                "rope": ropetab, "cmats": cm, "ident": ident,
                "nabias": np.ascontiguousarray(nab.reshape(128, -1)),
            })
        if DBG.get("onlyA"):
            r0 = run_bass_kernel_spmd(ncA, ims[:1], core_ids=[0], trace=bool(DBG.get("trace")))
            DBG["yT"] = r0.results[0]["yT"]
            DBG["res"] = r0
            return None
        res = run_bass_kernel_spmd(ncA, ims, core_ids=cores)
        yfull = [np.zeros((D, NTOK), np.float32) for _ in range(2)]
        for cid in cores:
            b, j = cid // 4, cid % 4
            yt = res.results[cid]["yT"]
            yfull[b][j * 128:(j + 1) * 128] = yt[0:128]
            yfull[b][512 + j * 128:512 + (j + 1) * 128] = yt[128:256]
            yfull[b][1024 + 2 * j * 128:1024 + (2 * j + 2) * 128] = yt[256:512]
        _CACHE["dbg_y%d" % l] = yfull
        ims = []
        for cid in cores:
            b, q = cid // 4, cid % 4
            lat = slice(q * 1024, (q + 1) * 1024)
            cr = slice(q * 64, (q + 1) * 64)
            xT = np.concatenate([xs[b][lat].T, cs[b][cr].T], axis=1)
            yT = np.concatenate([yfull[b][:, CTX + q * 1024:CTX + (q + 1) * 1024], yfull[b][:, q * 64:(q + 1) * 64]], axis=1)
            vecs = np.concatenate([_pp(mod[l, 2, :, b]), _pp(mod[l, 3, :, b]), _pp(mod[l, 4, :, b]), _pp(mod[l, 5, :, b]),
                                   _pp(mod[l, 2, :, 2]), _pp(mod[l, 3, :, 2]), _pp(mod[l, 4, :, 2]), _pp(mod[l, 5, :, 2]),
                                   _pp(mlp_norm[l])], axis=1)
            ims.append({"xT": np.ascontiguousarray(xT), "yT": np.ascontiguousarray(yT), "vecs": np.ascontiguousarray(vecs),
                        "w_out": w_out[l], "w1": w_mlp1[l], "w2": w_mlp2[l]})
        res = run_bass_kernel_spmd(ncB, ims, core_ids=cores)
        nx = [np.zeros_like(xs[0]), np.zeros_like(xs[1])]
        ncx = [np.zeros_like(cs[0]), np.zeros_like(cs[1])]
        for cid in cores:
            b, q = cid // 4, cid % 4
            xo = res.results[cid]["xo"]
            nx[b][q * 1024:(q + 1) * 1024] = xo[:, :1024].T
            ncx[b][q * 64:(q + 1) * 64] = xo[:, 1024:].T
        xs, cs = nx, ncx
    return np.stack(xs, axis=0).astype(np.float32)
```
